# Optimizing a Trainium2 kernel written in Bass

```python
import math
import jax, jax.numpy as jnp
from jax import lax
import numpy as np

D_MODEL = 1024
BATCH = 32
SEQ = 256
DEPTH = 4
DEC_BATCH = 2
DEC_SEQ = 1024
PAST_LEN = 512

GRID_W = 64
MIX_WIDTH = D_MODEL
A_WIDTH = MIX_WIDTH // 2
A_HEADS = 4
A_V_DIM = A_WIDTH // A_HEADS
A_QK_DIM = A_V_DIM // 2
B_WIDTH = MIX_WIDTH // 4
C_WIDTH = MIX_WIDTH - A_WIDTH - B_WIDTH
C_GROUPS = 4
C_GROUP_DIM = C_WIDTH // C_GROUPS
CHUNK = 128
CONV_W = 3
D_FF = 4 * D_MODEL
Q_BLOCK = 128
ROPE_BASE = 10000.0
ROPE_AXIS_DIM = A_QK_DIM // 2
IN_WIDTH = 3 * A_WIDTH + 3 * B_WIDTH + 2 * C_WIDTH
EPS = 1e-6

kernel_name = 'hybrid_diffattn_conv_chunkmlp_prefix_dit'


def _rmsnorm(x, g):
    xf = x.astype(jnp.float32)
    y = xf * lax.rsqrt(jnp.mean(xf * xf, axis=-1, keepdims=True) + EPS)
    return (y * g.astype(jnp.float32)).astype(x.dtype)


def _rope_tables(n_tokens):
    rows = n_tokens // GRID_W
    row = jnp.repeat(jnp.arange(rows, dtype=jnp.float32), GRID_W)
    col = jnp.tile(jnp.arange(GRID_W, dtype=jnp.float32), rows)
    inv = ROPE_BASE ** (-jnp.arange(0, ROPE_AXIS_DIM, 2, dtype=jnp.float32) / ROPE_AXIS_DIM)
    ang_r = row[:, None] * inv
    ang_c = col[:, None] * inv
    return (jnp.cos(ang_r), jnp.sin(ang_r), jnp.cos(ang_c), jnp.sin(ang_c))


def _rot_half(x, cos, sin):
    x1, x2 = jnp.split(x, 2, axis=-1)
    return jnp.concatenate([x1 * cos - x2 * sin, x2 * cos + x1 * sin], axis=-1)


def _apply_axial_rope(x, tables):
    cr, sr, cc, sc = tables
    b, t, h, _ = x.shape
    xf = x.astype(jnp.float32).reshape(b, t, h, 2, A_QK_DIM)
    bc = lambda a: a[:, None, None, :]
    xr, xc = jnp.split(xf, 2, axis=-1)
    out = jnp.concatenate([_rot_half(xr, bc(cr), bc(sr)), _rot_half(xc, bc(cc), bc(sc))], axis=-1)
    return out.reshape(b, t, h, 2 * A_QK_DIM).astype(x.dtype)


def _project(h, w_in_l):
    p = h @ w_in_l
    sizes = (A_WIDTH,) * 3 + (B_WIDTH,) * 3 + (C_WIDTH,) * 2
    idx = [int(s) for s in np.cumsum(sizes)[:-1]]
    q, k, v, gb, gc, xin, u, vs = jnp.split(p, idx, axis=-1)
    b, t, _ = h.shape
    q = q.reshape(b, t, A_HEADS, 2 * A_QK_DIM)
    k = k.reshape(b, t, A_HEADS, 2 * A_QK_DIM)
    v = v.reshape(b, t, A_HEADS, A_V_DIM)
    return q, k, v, gb, gc, xin, u, vs


def _diff_attention(q, k, v, lam, subln_g, lam_init):
    b, tq = q.shape[0], q.shape[1]
    tk = k.shape[1]
    n_blk = tq // Q_BLOCK
    k2 = k.reshape(b, tk, A_HEADS, 2, A_QK_DIM)
    qb = q.reshape(b, n_blk, Q_BLOCK, A_HEADS, 2, A_QK_DIM).transpose(1, 0, 2, 3, 4, 5)
    scale = A_QK_DIM ** -0.5

    def one_block(qblk):
        s = jnp.einsum('bqhcd,bkhcd->bchqk', qblk, k2, preferred_element_type=jnp.float32) * scale
        p = jax.nn.softmax(s, axis=-1)
        a = p[:, 0] - lam * p[:, 1]
        return jnp.einsum('bhqk,bkhd->bqhd', a.astype(v.dtype), v)

    o = lax.map(one_block, qb)
    o = o.transpose(1, 0, 2, 3, 4).reshape(b, tq, A_HEADS, A_V_DIM)
    o = _rmsnorm(o, subln_g) * (1.0 - lam_init)
    return o.reshape(b, tq, A_WIDTH)


def _short_conv(x, w, bias):
    xp = jnp.pad(x, ((0, 0), (1, 1), (0, 0)))
    return xp[:, :-2] * w[0] + xp[:, 1:-1] * w[1] + xp[:, 2:] * w[2] + bias


def _chunk_mlp(u, vs, w_s, b_s):
    b, t, _ = vs.shape
    vc = vs.reshape(b, t // CHUNK, CHUNK, C_GROUPS, C_GROUP_DIM)
    mixed = jnp.einsum('gpq,bnqgd->bnpgd', w_s, vc) + b_s.T[None, None, :, :, None]
    return u * mixed.reshape(b, t, C_WIDTH)


def _lambda(lq1, lk1, lq2, lk2, lam_init):
    f = lambda a: a.astype(jnp.float32)
    return jnp.exp(jnp.sum(f(lq1) * f(lk1))) - jnp.exp(jnp.sum(f(lq2) * f(lk2))) + lam_init


def _mlp(h, w_up_l, w_down_l):
    a = jax.nn.relu(h @ w_up_l)
    return (a * a) @ w_down_l


def setup_inputs(seed: int = 0) -> dict:
    key = jax.random.key(seed)
    ks = jax.random.split(key, 24)
    n = lambda k, s, sc: jax.random.normal(k, s, jnp.float32) * sc
    return {
        'x_prompt': n(ks[0], (BATCH, SEQ, D_MODEL), 1.0),
        'x_sample': n(ks[1], (DEC_BATCH, DEC_SEQ, D_MODEL), 1.0),
        'cache_k': n(ks[2], (DEC_BATCH, DEPTH, PAST_LEN, A_HEADS, 2 * A_QK_DIM), 1.0),
        'cache_v': n(ks[3], (DEC_BATCH, DEPTH, PAST_LEN, A_HEADS, A_V_DIM), 1.0),
        'c': n(ks[4], (DEC_BATCH, D_MODEL), 1.0),
        'c_ctx': n(ks[5], (D_MODEL,), 1.0),
        'w_mod': n(ks[6], (DEPTH, D_MODEL, 6 * D_MODEL), 0.5 * D_MODEL ** -0.5),
        'b_mod': n(ks[7], (DEPTH, 6 * D_MODEL), 0.02),
        'norm_mix': 1.0 + n(ks[8], (DEPTH, D_MODEL), 0.02),
        'norm_mlp': 1.0 + n(ks[9], (DEPTH, D_MODEL), 0.02),
        'w_in': n(ks[10], (DEPTH, D_MODEL, IN_WIDTH), D_MODEL ** -0.5),
        'lam_q1': n(ks[11], (DEPTH, A_QK_DIM), 0.1),
        'lam_k1': n(ks[12], (DEPTH, A_QK_DIM), 0.1),
        'lam_q2': n(ks[13], (DEPTH, A_QK_DIM), 0.1),
        'lam_k2': n(ks[14], (DEPTH, A_QK_DIM), 0.1),
        'subln': 1.0 + n(ks[15], (DEPTH, A_V_DIM), 0.02),
        'conv_w': n(ks[16], (DEPTH, CONV_W, B_WIDTH), CONV_W ** -0.5),
        'conv_b': n(ks[17], (DEPTH, B_WIDTH), 0.02),
        'w_s': n(ks[18], (DEPTH, C_GROUPS, CHUNK, CHUNK), CHUNK ** -0.5),
        'b_s': 1.0 + n(ks[19], (DEPTH, C_GROUPS, CHUNK), 0.02),
        'w_out': n(ks[20], (DEPTH, MIX_WIDTH, D_MODEL), MIX_WIDTH ** -0.5),
        'w_up': n(ks[21], (DEPTH, D_MODEL, D_FF), D_MODEL ** -0.5),
        'w_down': n(ks[22], (DEPTH, D_FF, D_MODEL), D_FF ** -0.5),
        'norm_final': 1.0 + n(ks[23], (D_MODEL,), 0.02),
    }


def reference(x_prompt, x_sample, cache_k, cache_v, c, c_ctx, w_mod, b_mod, norm_mix, norm_mlp,
              w_in, lam_q1, lam_k1, lam_q2, lam_k2, subln, conv_w, conv_b, w_s, b_s,
              w_out, w_up, w_down, norm_final):
    xp = x_prompt
    xs = x_sample
    rope = _rope_tables(xs.shape[1])
    sc_ctx = jax.nn.silu(c_ctx)
    sc_lat = jax.nn.silu(c)
    new_k, new_v = [], []
    for d in range(DEPTH):
        lam_init = 0.8 - 0.6 * math.exp(-0.3 * d)
        lam = _lambda(lam_q1[d], lam_k1[d], lam_q2[d], lam_k2[d], lam_init)

        mod_p = sc_ctx @ w_mod[d] + b_mod[d]
        sa, ca, ga, sm, cm, gm = jnp.split(mod_p, 6, axis=-1)
        h = _rmsnorm(xp, norm_mix[d]) * (1.0 + ca) + sa
        q, k, v, gb, gc, xin, u, vs = _project(h, w_in[d])
        att = _diff_attention(q, k, v, lam, subln[d], lam_init)
        conv = gb * _short_conv(gc * xin, conv_w[d], conv_b[d])
        cmlp = _chunk_mlp(u, vs, w_s[d], b_s[d])
        xp = xp + ga * (jnp.concatenate([att, conv, cmlp], axis=-1) @ w_out[d])
        h = _rmsnorm(xp, norm_mlp[d]) * (1.0 + cm) + sm
        xp = xp + gm * _mlp(h, w_up[d], w_down[d])
        new_k.append(k)
        new_v.append(v)

        mod_s = (sc_lat @ w_mod[d] + b_mod[d])[:, None, :]
        sa, ca, ga, sm, cm, gm = jnp.split(mod_s, 6, axis=-1)
        h = _rmsnorm(xs, norm_mix[d]) * (1.0 + ca) + sa
        q, k, v, gb, gc, xin, u, vs = _project(h, w_in[d])
        q = _apply_axial_rope(q, rope)
        k = _apply_axial_rope(k, rope)
        k_all = jnp.concatenate([cache_k[:, d].astype(k.dtype), k], axis=1)
        v_all = jnp.concatenate([cache_v[:, d].astype(v.dtype), v], axis=1)
        att = _diff_attention(q, k_all, v_all, lam, subln[d], lam_init)
        conv = gb * _short_conv(gc * xin, conv_w[d], conv_b[d])
        cmlp = _chunk_mlp(u, vs, w_s[d], b_s[d])
        xs = xs + ga * (jnp.concatenate([att, conv, cmlp], axis=-1) @ w_out[d])
        h = _rmsnorm(xs, norm_mlp[d]) * (1.0 + cm) + sm
        xs = xs + gm * _mlp(h, w_up[d], w_down[d])

    y_prompt = _rmsnorm(xp, norm_final)
    y_sample = _rmsnorm(xs, norm_final)
    new_cache_k = jnp.stack(new_k, axis=1)
    new_cache_v = jnp.stack(new_v, axis=1)
    return (y_prompt, y_sample, new_cache_k, new_cache_v)
```

```python
import math
import numpy as np
import concourse.bass as bass
import concourse.mybir as mybir
from concourse.bass_utils import run_bass_kernel_spmd

F32 = mybir.dt.float32
BF16 = mybir.dt.bfloat16
AF = mybir.ActivationFunctionType
ALU = mybir.AluOpType
AX = mybir.AxisListType

D = 1024
DEPTH = 4
T = 1280
NPT = 1024
NST = 256
PAST = 512
NKS = PAST + 1024
EPS = 1e-6
TT = [(0, 512), (512, 512), (1024, 256)]
NSLOT = 4
SLOT_ELEMS = 4096
BROWS = 513


def blks(a, n):
    return list(range(a // 256, (a + n + 255) // 256))


class Op:
    __slots__ = ("q", "fn", "deps", "kind", "sig", "sem", "val", "inc", "idx")


class Sched:
    QS = ("pe", "act", "dve", "pool", "sp")

    def __init__(self):
        self.ops = {q: [] for q in self.QS}
        self.res = {}
        self.dma_keys = {}
        self.cc_count = 0

    def emit(self, q, fn, reads=(), writes=(), kind="c", sig=True, semkey=None):
        op = Op()
        op.q = q
        op.fn = fn
        op.kind = kind
        op.sig = sig
        op.sem = None
        op.val = None
        op.inc = 1
        deps = {}
        for r in reads:
            e = self.res.get(r)
            if e is not None and e[0] is not None:
                deps[id(e[0])] = (e[0], True)
        for w in writes:
            e = self.res.get(w)
            if e is not None:
                if e[0] is not None and id(e[0]) not in deps:
                    deps[id(e[0])] = (e[0], False)
                for rd in e[1]:
                    if id(rd) not in deps:
                        deps[id(rd)] = (rd, False)
        dl = []
        for dep, raw in deps.values():
            if dep is op:
                continue
            if dep.q == q and dep.kind == "c" and kind == "c":
                if q == "pe":
                    continue
            dl.append(dep)
        if kind == "d":
            ent = self.dma_keys.setdefault(semkey, [0, None])
            if ent[1] is not None:
                dl.append(ent[1])
            ent[0] += 1
            ent[1] = op
            op.sem = ("dma", semkey)
            op.val = ent[0] * 16
        elif kind == "cc":
            self.cc_count += 1
            op.sem = ("cc", 0)
            op.val = self.cc_count
        op.deps = dl
        for r in reads:
            e = self.res.setdefault(r, [None, []])
            e[1].append(op)
        for w in writes:
            self.res[w] = [op, []]
        op.idx = len(self.ops[q])
        self.ops[q].append(op)
        return op

    def finalize(self):
        for q in self.QS:
            ops = [o for o in self.ops[q] if o.kind == "c"]
            if ops:
                ops[-1].sig = True
            cnt = 0
            for o in ops:
                if o.sig:
                    cnt += 1
                    o.val = cnt
                o.sem = ("q", q)
            nxt = None
            for o in reversed(ops):
                if o.sig:
                    nxt = o.val
                else:
                    o.val = nxt

    def run(self, nc):
        self.finalize()
        names = [("q", q) for q in self.QS] + [("dma", k) for k in self.dma_keys] + [("cc", 0)]
        from contextlib import ExitStack
        with ExitStack() as st:
            sems = {}
            for i, n in enumerate(names):
                sems[n] = st.enter_context(nc.semaphore("s%d" % i))
            block = st.enter_context(nc.Block())
            handles = {"pe": block.tensor, "act": block.scalar, "dve": block.vector,
                       "pool": block.gpsimd, "sp": block.sync}
            for q in self.QS:
                ops = self.ops[q]

                def body(eng, ops=ops, q=q):
                    waited = {}
                    for o in ops:
                        for dpn in o.deps:
                            if waited.get(dpn.sem, 0) < dpn.val:
                                eng.wait_ge(sems[dpn.sem], dpn.val)
                                waited[dpn.sem] = dpn.val
                        ins = o.fn(eng)
                        if o.kind == "c":
                            if o.sig:
                                ins.then_inc(sems[o.sem], 1)
                        elif o.kind == "d":
                            ins.then_inc(sems[o.sem], 16)
                        else:
                            ins.then_inc(sems[o.sem], 1)
                    if q == "sp":
                        for k, ent in self.dma_keys.items():
                            if waited.get(("dma", k), 0) < ent[0] * 16:
                                eng.wait_ge(sems[("dma", k)], ent[0] * 16)
                        for qq in self.QS:
                            cops = [o for o in self.ops[qq] if o.kind == "c"]
                            if cops:
                                eng.wait_ge(sems[("q", qq)], cops[-1].val)

                handles[q](body)


def I(method, **kw):
    return lambda e: getattr(e, method)(**kw)


def build_nc(depth=DEPTH):
    nc = bass.Bass("TRN2", target_bir_lowering=False)
    S = Sched()

    def din(name, shape, dt=F32):
        return nc.dram_tensor(name, list(shape), dt, kind="ExternalInput").ap()

    xT_d = din("xT", [128, 8, T])
    ckT_d = din("ckT", [depth, 128, 4, PAST])
    cv_d = din("cv", [depth, PAST, 512])
    cvec_d = din("cvec", [128, 8, 2])
    w_mod_d = din("w_mod", [depth, D, 6 * D])
    w_in_d = din("w_in", [depth, D, 2816])
    w_out_d = din("w_out", [depth, D, D])
    w_up_d = din("w_up", [depth, D, 4 * D])
    w_down_d = din("w_down", [depth, 4 * D, D])
    bmod_d = din("bmod", [128, depth, 48])
    gmix_d = din("gmix", [128, depth, 8])
    gmlp_d = din("gmlp", [128, depth, 8])
    gfin_d = din("gfin", [128, 8])
    lamv_d = din("lamv", [128, depth, 4, 64])
    gsub_d = din("gsub", [128, depth, 128])
    cw_d = din("cw", [128, depth, 2, 4])
    wsT_d = din("wsT", [128, depth, 4, 128])
    bsB_d = din("bsB", [128, depth, 2, 128])
    cosT_d = din("cosT", [128, NST])
    sinS_d = din("sinS", [128, NST])
    perm_d = din("perm", [128, 128])
    ident_d = din("ident", [128, 128])
    sel_d = din("sel", [128, 2, 4])

    yT_d = nc.dram_tensor("yT", [128, 8, T], F32, kind="ExternalOutput").ap()
    nk_d = nc.dram_tensor("nk", [4, depth, 256, 512], F32, kind="ExternalOutput").ap()
    nv_d = nc.dram_tensor("nv", [4, depth, 256, 512], F32, kind="ExternalOutput").ap()
    bounce_t = [nc.dram_tensor("bounce%d" % d, [BROWS, 512], BF16, kind="Internal") for d in range(depth)]
    gath_t = [nc.dram_tensor("gath%d" % d, [4 * BROWS, 512], BF16, kind="Internal") for d in range(depth)]

    from contextlib import ExitStack
    st = ExitStack()

    def sb(name, shape, dt):
        return st.enter_context(nc.sbuf_tensor(name, list(shape), dt))

    X = sb("X", [128, 8, T], F32)
    HT = sb("HT", [128, 8, T], BF16)
    WS = sb("WS", [128, NSLOT, SLOT_ELEMS], BF16)
    R = sb("R", [128, 40960], BF16)
    PS = st.enter_context(nc.psum_tensor("PS", [128, 8, 512], F32))

    def rview(off, n, pat=None, dt=None, **kw):
        v = R[:, off:off + n]
        if dt is not None:
            v = v.bitcast(dt)
        if pat is not None:
            v = v.rearrange(pat, **kw)
        return v

    HID = rview(0, 40960, "p (c t) -> p c t", c=32)
    QT = rview(0, 5120, "p (c t) -> p c t", c=4)
    KT = rview(5120, 5120, "p (c t) -> p c t", c=4)
    VA = rview(10240, 5200, "p (k h e) -> p k h e", k=10, h=4)
    G = rview(15440, 5120, "p (c t) -> p c t", c=4)
    ZU = rview(20560, 10240, "p (c t) -> p c t", dt=F32, c=4)
    VS = rview(30800, 2560, "p (k e) -> p k e", k=10)
    KALL = rview(15440, 6144, "p (h k) -> p h k", h=4)
    VALL = rview(21584, 6240, "p (k h e) -> p k h e", k=12, h=4)
    CAT = HT
    LAMV = rview(33360, depth * 512, "p (d a e) -> p d a e", dt=F32, d=depth, a=4)
    LTMP = rview(36432, 512, "p (a e) -> p a e", dt=F32, a=4)

    MOD = [sb("MOD%d" % i, [128, 48, 2], F32) for i in range(2)]
    S1 = sb("S1", [128, 8, 2], F32)
    S2 = sb("S2", [128, 8, 2], F32)
    SC = sb("SC", [128, 8, 2], BF16)
    CV = sb("CVEC", [128, 8, 2], F32)
    BMOD = sb("BMOD", [128, depth, 48], F32)
    GMIX = sb("GMIX", [128, depth, 8], F32)
    GMLP = sb("GMLP", [128, depth, 8], F32)
    GFIN = sb("GFIN", [128, 8], F32)
    GSUB = sb("GSUB", [128, 128], F32)
    CW = sb("CW", [128, depth, 2, 4], F32)
    WST = sb("WST", [128, depth, 4, 128], BF16)
    BSB = sb("BSB", [128, depth, 2, 128], F32)
    COST = sb("COST", [128, NST], F32)
    SINS = sb("SINS", [128, NST], F32)
    PERM = sb("PERM", [128, 128], F32)
    IDB = sb("IDB", [128, 128], BF16)
    ONES = sb("ONES", [128, 128], BF16)
    SEL = sb("SEL", [128, 2, 4], F32)
    LAM = sb("LAM", [128, depth, 4], F32)
    SQ = [sb("SQ%d" % i, [128, 512], BF16) for i in range(3)]
    RSTDT = sb("RSTD", [128, 2, 512], F32)
    RSTD = [RSTDT[:, 0, :], RSTDT[:, 1, :]]
    CONVY = RSTDT[:].rearrange("p a n -> p (a n)")
    NTMP = [sb("NTMP%d" % i, [128, 512], F32) for i in range(3)]
    STGT = sb("STG", [128, 2, 512], F32)
    STG = [STGT[:, 0, :], STGT[:, 1, :]]
    CONVT = STGT[:].rearrange("p a n -> p (a n)")
    ZB = sb("ZB", [128, 2, 2], BF16)
    HB = sb("HB", [128, 4, 2, 2], BF16)
    HBF = sb("HBF", [128, 4, 2, 2], F32)
    HTMP = sb("HTMP", [128, 2, 2, 4], F32)
    HALO = sb("HALO", [128, 2, 2], F32)
    ATT_A = sb("ATTA", [128, 4, 128], F32)
    ATT_OB = [sb("ATTO%d" % i, [128, 4, 128], BF16) for i in range(2)]
    ATT_S = sb("ATTS", [128, 32], F32)
    FEN = sb("FEN", [128, 2], F32)
    EPSB = sb("EPSB", [128, 1], F32)

    rr = {}

    def rot(name, n):
        v = rr.get(name, 0)
        rr[name] = v + 1
        return v % n

    def mmbank():
        return rot("mm", 4)

    pieces = []
    issued = [0]

    def add_piece(src_list):
        pieces.append(src_list)
        return len(pieces) - 1

    def ensure_issued(upto):
        while issued[0] <= min(upto, len(pieces) - 1):
            j = issued[0]
            slot = j % NSLOT
            plist = pieces[j]
            for hi, (kc0, kcn, cols, src) in enumerate(plist):
                dst = WS[:, slot, kc0 * cols:(kc0 + kcn) * cols].rearrange("p (k n) -> p k n", k=kcn)
                wkeys = [("WS", slot, hi)] if len(plist) == 2 else [("WS", slot, 0), ("WS", slot, 1)]
                S.emit("pool", I("dma_start", out=dst, in_=src), reads=[], writes=wkeys,
                       kind="d", semkey=("ws", slot, hi))
            issued[0] += 1

    def wslot(pi, la=NSLOT - 1):
        ensure_issued(pi + la)
        return pi % NSLOT

    def wkeys(slot):
        return [("WS", slot, 0), ("WS", slot, 1)]

    def wview(slot, kcn, cols):
        return WS[:, slot, 0:kcn * cols].rearrange("p (k n) -> p k n", k=kcn)

    def wsrc(w, c0, cols):
        return w.rearrange("(k p) n -> p k n", p=128)[:, :, c0:c0 + cols]

    plan = []
    WIN_ORDER = [(1536, 512, "g"), (2048, 512, "xu"), (512, 512, "k"), (1024, 512, "v"),
                 (0, 512, "q"), (2560, 256, "vs")]

    def mod_piece(d, j):
        pi = add_piece([(0, 8, 512, wsrc(w_mod_d[d], j * 512, 512))])
        plan.append(("mod", d, j, pi))

    for j in range(4):
        mod_piece(0, j)
    pend0 = list(range(4, 12))
    for d in range(depth):
        for wi, (c0, cols, nm) in enumerate(WIN_ORDER):
            pi = add_piece([(0, 8, cols, wsrc(w_in_d[d], c0, cols))])
            plan.append(("in_" + nm, d, 0, pi))
            if d == 0:
                for _ in range(2 if wi < 2 else 1):
                    if pend0:
                        mod_piece(0, pend0.pop(0))
        plan.append(("mix", d, 0, -1))
        for j in range(2):
            pi = add_piece([(0, 8, 512, wsrc(w_out_d[d], j * 512, 512))])
            plan.append(("out", d, j, pi))
        for j in range(8):
            pi = add_piece([(0, 8, 512, wsrc(w_up_d[d], j * 512, 512))])
            plan.append(("up", d, j, pi))
            if d + 1 < depth:
                mod_piece(d + 1, j)
        for j in range(8):
            src = w_down_d[d].rearrange("(k p) n -> p k n", p=128)
            pi = add_piece([(0, 16, 128, src[:, 0:16, j * 128:(j + 1) * 128]),
                            (16, 16, 128, src[:, 16:32, j * 128:(j + 1) * 128])])
            plan.append(("down", d, j, pi))
            if d + 1 < depth and j < 4:
                mod_piece(d + 1, 8 + j)

    def load(dst, src, key, q="sp", extra_w=()):
        S.emit(q, I("dma_start", out=dst, in_=src), reads=[], writes=[key] + list(extra_w),
               kind="d", semkey=key)

    load(CV[:], cvec_d, "CV")
    load(BMOD[:], bmod_d, "BMOD")
    for fc in range(8):
        load(X[:, fc, :], xT_d[:, fc, :], ("Xld", fc), extra_w=[("X", fc, b) for b in range(5)])
    load(GMIX[:], gmix_d, "GMIX")
    load(GMLP[:], gmlp_d, "GMLP")
    load(GFIN[:], gfin_d, "GFIN")
    load(LAMV[:], lamv_d, "LAMV")
    load(CW[:], cw_d, "CW")
    load(WST[:], wsT_d, "WST", q="pool")
    load(BSB[:], bsB_d, "BSB")
    load(COST[:], cosT_d, "COST")
    load(SINS[:], sinS_d, "SINS")
    load(PERM[:], perm_d, "PERM")
    load(IDB[:], ident_d, "IDB", q="pool")
    load(SEL[:], sel_d, "SEL")

    S.emit("dve", I("memset", ap=ONES[:], constant=1.0 / 1024.0), writes=["ONES"])
    S.emit("dve", I("memset", ap=EPSB[:], constant=EPS), writes=["EPSB"])
    S.emit("act", I("activation", out=SC[:], in_=CV[:], func=AF.Silu), reads=["CV"], writes=["SC"])
    lam_inits = [0.8 - 0.6 * math.exp(-0.3 * d) for d in range(depth)]
    for d in range(depth):
        for i in range(2):
            S.emit("dve", I("tensor_tensor", out=LTMP[:, i, :], in0=LAMV[:, d, 2 * i, :],
                            in1=LAMV[:, d, 2 * i + 1, :], op=ALU.mult),
                   reads=["LAMV"], writes=[("LTMP", i)])
        S.emit("dve", I("tensor_reduce", out=LAM[:, d, 1:3], in_=LTMP[:, 0:2, :], axis=AX.X, op=ALU.add),
               reads=[("LTMP", 0), ("LTMP", 1)], writes=[("LAM", d, 1)])
        S.emit("act", I("activation", out=LAM[:, d, 1:3], in_=LAM[:, d, 1:3], func=AF.Exp),
               reads=[("LAM", d, 1)], writes=[("LAM", d, 1)])
        S.emit("dve", I("scalar_tensor_tensor", out=LAM[:, d, 0:1], in0=LAM[:, d, 2:3],
                        scalar=-lam_inits[d], in1=LAM[:, d, 1:2], op0=ALU.add, op1=ALU.subtract),
               reads=[("LAM", d, 1)], writes=[("LAM", d, 0)])

    def do_mod_piece(d, j, pi):
        slot = wslot(pi)
        wv = wview(slot, 8, 512)
        bank = mmbank()
        M = MOD[d % 2]
        for fc in range(4):
            for kc in range(8):
                S.emit("pe", I("matmul", out=PS[:, bank, fc * 2:fc * 2 + 2],
                               lhsT=wv[:, kc, fc * 128:(fc + 1) * 128],
                               rhs=SC[:, kc, :], start=(kc == 0), stop=(kc == 7)),
                       reads=wkeys(slot) + ["SC"], writes=[("PS", bank)], sig=(kc == 7 and fc == 3))
        for v in range(2):
            S.emit("dve", I("tensor_tensor", out=M[:, j * 4:(j + 1) * 4, v], in0=PS[:, bank, v:8:2],
                            in1=BMOD[:, d, j * 4:(j + 1) * 4], op=ALU.add),
                   reads=[("PS", bank), "BMOD"], writes=[("MOD", d % 2, j, v)])

    def mod_ap(d, which, fc, v):
        return MOD[d % 2][:, which * 8 + fc, v:v + 1]

    def mod_key(d, which):
        return [("MOD", d % 2, which * 2 + i, v) for i in range(2) for v in range(2)]

    def do_scale_prep(d, which, gain, gkey, dst, name):
        M = MOD[d % 2]
        for v in range(2):
            S.emit("dve", I("scalar_tensor_tensor", out=dst[:, :, v], in0=M[:, which * 8:which * 8 + 8, v],
                            scalar=1.0, in1=gain[:, d, :], op0=ALU.add, op1=ALU.mult),
                   reads=mod_key(d, which) + [gkey], writes=[(name, v)])

    def stat_accum(oc, ti):
        t0, n = TT[ti]
        bl = blks(t0, n)
        sq = rot("sq", 3)
        S.emit("act", I("activation", out=SQ[sq][:, 0:n], in_=X[:, oc, t0:t0 + n], func=AF.Square),
               reads=[("X", oc, b) for b in bl], writes=[("SQ", sq)])
        S.emit("pe", I("matmul", out=PS[:, 5 + ti, 0:n], lhsT=ONES[:], rhs=SQ[sq][:, 0:n],
                       start=(oc == 0), stop=(oc == 7)),
               reads=["ONES", ("SQ", sq)], writes=[("PS", 5 + ti)], sig=True)

    def do_norm(d, scale_t, scale_name, shift_which, tis=(0, 1, 2), have_stats=False):
        for ti in tis:
            t0, n = TT[ti]
            v = 0 if ti < 2 else 1
            bl = blks(t0, n)
            if have_stats:
                bank = 5 + ti
            else:
                bank = mmbank()
            for fc in range(8):
                if have_stats:
                    break
                sq = rot("sq", 3)
                S.emit("act", I("activation", out=SQ[sq][:, 0:n], in_=X[:, fc, t0:t0 + n], func=AF.Square),
                       reads=[("X", fc, b) for b in bl], writes=[("SQ", sq)])
                S.emit("pe", I("matmul", out=PS[:, bank, 0:n], lhsT=ONES[:], rhs=SQ[sq][:, 0:n],
                               start=(fc == 0), stop=(fc == 7)),
                       reads=["ONES", ("SQ", sq)], writes=[("PS", bank)], sig=True)
            rs = rot("rstd", 2)
            S.emit("act", I("activation", out=RSTD[rs][:, 0:n], in_=PS[:, bank, 0:n], func=AF.Ln, bias=EPSB[:, 0:1], scale=1.0),
                   reads=[("PS", bank), "EPSB"], writes=[("RSTD", rs)])
            S.emit("act", I("activation", out=RSTD[rs][:, 0:n], in_=RSTD[rs][:, 0:n], func=AF.Exp, scale=-0.5),
                   reads=[("RSTD", rs)], writes=[("RSTD", rs)])
            for fc in range(8):
                nt = rot("ntmp", 3)
                S.emit("dve", I("scalar_tensor_tensor", out=NTMP[nt][:, 0:n], in0=X[:, fc, t0:t0 + n],
                                scalar=scale_t[:, fc, v:v + 1], in1=RSTD[rs][:, 0:n], op0=ALU.mult, op1=ALU.mult),
                       reads=[("X", fc, b) for b in bl] + [(scale_name, v), ("RSTD", rs)], writes=[("NTMP", nt)])
                S.emit("act", I("activation", out=HT[:, fc, t0:t0 + n], in_=NTMP[nt][:, 0:n], func=AF.Identity,
                                bias=mod_ap(d, shift_which, fc, v), scale=1.0),
                       reads=[("NTMP", nt)] + mod_key(d, shift_which), writes=[("HT", fc, b) for b in bl])

    def ws_matmul_fm(slot, wv, fc, ti, bank):
        t0, n = TT[ti]
        bl = blks(t0, n)
        for kc in range(8):
            S.emit("pe", I("matmul", out=PS[:, bank, 0:n], lhsT=wv[:, kc, fc * 128:(fc + 1) * 128],
                           rhs=HT[:, kc, t0:t0 + n], start=(kc == 0), stop=(kc == 7)),
                   reads=wkeys(slot) + [("HT", kc, b) for b in bl], writes=[("PS", bank)], sig=(kc == 7))

    def ws_matmul_tm(slot, wv, tk, bank, cols):
        for kc in range(8):
            S.emit("pe", I("matmul", out=PS[:, bank, 0:cols], lhsT=HT[:, kc, tk * 128:(tk + 1) * 128],
                           rhs=wv[:, kc, 0:cols], start=(kc == 0), stop=(kc == 7)),
                   reads=wkeys(slot) + [("HT", kc, tk // 2)], writes=[("PS", bank)], sig=(kc == 7))

    def rope_evac(bank, dstT, fc, keyname):
        nt = rot("ntmp", 3)
        QSv = NTMP[nt][:, 0:NST]
        RTv = NTMP[nt][:, NST:2 * NST]
        S.emit("act", I("activation", out=QSv, in_=PS[:, bank, 0:NST], func=AF.Copy),
               reads=[("PS", bank)], writes=[("NTMP", nt)])
        b2 = mmbank()
        S.emit("pe", I("matmul", out=PS[:, b2, 0:NST], lhsT=PERM[:], rhs=QSv, start=True, stop=True),
               reads=["PERM", ("NTMP", nt)], writes=[("PS", b2)], sig=True)
        S.emit("dve", I("tensor_tensor", out=RTv, in0=PS[:, b2, 0:NST], in1=SINS[:], op=ALU.mult),
               reads=[("PS", b2), "SINS"], writes=[("NTMP", nt)])
        S.emit("dve", I("tensor_tensor", out=QSv, in0=QSv, in1=COST[:], op=ALU.mult),
               reads=[("NTMP", nt), "COST"], writes=[("NTMP", nt)])
        S.emit("dve", I("tensor_tensor", out=dstT[:, fc, NPT:T], in0=QSv, in1=RTv, op=ALU.add),
               reads=[("NTMP", nt)], writes=[(keyname, fc, 4)])

    def do_in_piece(kind, d, pi):
        cols = 256 if kind == "vs" else 512
        slot = wslot(pi)
        wv = wview(slot, 8, cols)
        if kind in ("q", "k"):
            dstT = QT if kind == "q" else KT
            kn = "QT" if kind == "q" else "KT"
            for fc in range(4):
                for ti in range(3):
                    if kind == "k" and ti < 2:
                        continue
                    t0, n = TT[ti]
                    bank = mmbank()
                    ws_matmul_fm(slot, wv, fc, ti, bank)
                    if ti < 2:
                        S.emit("act", I("activation", out=dstT[:, fc, t0:t0 + n], in_=PS[:, bank, 0:n], func=AF.Copy),
                               reads=[("PS", bank)], writes=[(kn, fc, b) for b in blks(t0, n)])
                    else:
                        rope_evac(bank, dstT, fc, kn)
            if kind == "k":
                for tk in range(8):
                    bank = mmbank()
                    ws_matmul_tm(slot, wv, tk, bank, 512)
                    sg = rot("stg", 2)
                    S.emit("dve", I("tensor_copy", out=STG[sg][:], in_=PS[:, bank, :]),
                           reads=[("PS", bank)], writes=[("STG", sg)])
                    S.emit("sp", I("dma_start", out=nk_d[tk // 2, d, (tk % 2) * 128:(tk % 2) * 128 + 128, :], in_=STG[sg][:]),
                           reads=[("STG", sg)], writes=[], kind="d", semkey=("stgo", sg))
                    oi = rot("atto", 2)
                    KB = ATT_OB[oi]
                    S.emit("act", I("activation", out=KB[:].rearrange("p h e -> p (h e)"), in_=STG[sg][:], func=AF.Copy),
                           reads=[("STG", sg)], writes=[("ATTO", oi, u) for u in range(4)])
                    tb = mmbank()
                    TPV = PS[:, tb, 0:256].bitcast(BF16)
                    for h in range(4):
                        S.emit("pe", I("transpose", out=TPV[:, h * 128:(h + 1) * 128], in_=KB[:, h, :], identity=IDB[:]),
                               reads=[("ATTO", oi, h), "IDB"], writes=[("PS", tb)], sig=(h == 3))
                    S.emit("act", I("activation", out=KT[:, :, tk * 128:(tk + 1) * 128],
                                    in_=TPV[:, 0:512].rearrange("p (h q) -> p h q", h=4), func=AF.Copy),
                           reads=[("PS", tb)], writes=[("KT", fc, tk // 2) for fc in range(4)])
        elif kind == "v":
            for tk in range(10):
                bank = mmbank()
                ws_matmul_tm(slot, wv, tk, bank, 512)
                sg = rot("stg", 2)
                S.emit("dve", I("tensor_copy", out=STG[sg][:], in_=PS[:, bank, :]),
                       reads=[("PS", bank)], writes=[("STG", sg)])
                S.emit("act", I("activation", out=VA[:, tk, :, 0:128],
                                in_=STG[sg][:].rearrange("p (h e) -> p h e", h=4), func=AF.Copy),
                       reads=[("STG", sg)], writes=[("VA", tk)])
                if tk < 8:
                    S.emit("sp", I("dma_start", out=nv_d[tk // 2, d, (tk % 2) * 128:(tk % 2) * 128 + 128, :], in_=STG[sg][:]),
                           reads=[("STG", sg)], writes=[], kind="d", semkey=("stgo", sg))
        elif kind == "g":
            for ti in range(3):
                for fc in range(4):
                    t0, n = TT[ti]
                    bank = mmbank()
                    ws_matmul_fm(slot, wv, fc, ti, bank)
                    S.emit("act", I("activation", out=G[:, fc, t0:t0 + n], in_=PS[:, bank, 0:n], func=AF.Copy),
                           reads=[("PS", bank)], writes=[("G", fc, b) for b in blks(t0, n)])
        elif kind == "xu":
            for fc in range(4):
                for ti in range(3):
                    t0, n = TT[ti]
                    bank = mmbank()
                    ws_matmul_fm(slot, wv, fc, ti, bank)
                    if fc < 2:
                        S.emit("dve", I("tensor_tensor", out=ZU[:, fc, t0:t0 + n], in0=PS[:, bank, 0:n],
                                        in1=G[:, 2 + fc, t0:t0 + n], op=ALU.mult),
                               reads=[("PS", bank)] + [("G", 2 + fc, b) for b in blks(t0, n)],
                               writes=[("ZU", fc, b) for b in blks(t0, n)])
                    else:
                        S.emit("act", I("activation", out=ZU[:, fc, t0:t0 + n], in_=PS[:, bank, 0:n], func=AF.Copy),
                               reads=[("PS", bank)], writes=[("ZU", fc, b) for b in blks(t0, n)])
        elif kind == "vs":
            for tk in range(10):
                bank = mmbank()
                ws_matmul_tm(slot, wv, tk, bank, 256)
                S.emit("act", I("activation", out=VS[:, tk, :], in_=PS[:, bank, 0:256], func=AF.Copy),
                       reads=[("PS", bank)], writes=[("VS", tk)])
                if tk % 2 == 1 and tk >= 3:
                    blk = tk // 2 - 1
                    do_cmlp(d, blk)
                    do_conv(d, blk)
            do_cmlp(d, 4)

    def flat_ap(t, off, dims):
        return bass.AP(t, off, [list(x) for x in dims])

    def do_exchange(d):
        bt = bounce_t[d]
        gt = gath_t[d]
        S.emit("dve", I("tensor_copy", out=ZB[:, :, 0], in_=ZU[:, 0:2, NPT]),
               reads=[("ZU", 0, 4), ("ZU", 1, 4)], writes=[("ZB", 0)])
        S.emit("dve", I("tensor_copy", out=ZB[:, :, 1], in_=ZU[:, 0:2, T - 1]),
               reads=[("ZU", 0, 4), ("ZU", 1, 4)], writes=[("ZB", 1)])
        S.emit("sp", I("dma_start", out=flat_ap(bt, 0, [[256, 128], [128 * 256, 4], [1, 256]]), in_=KT[:, :, NPT:T]),
               reads=[("KT", fc, 4) for fc in range(4)], writes=[("BOUNCE", d, "k")], kind="d", semkey="bk")
        for jj in range(2):
            S.emit("sp", I("dma_start", out=flat_ap(bt, 131072 + jj * 65536, [[512, 128], [128, 4], [1, 128]]),
                           in_=VA[:, 8 + jj, :, 0:128]),
                   reads=[("VA", 8 + jj)], writes=[("BOUNCE", d, "v", jj)], kind="d", semkey=("bv", jj))
        S.emit("sp", I("dma_start", out=flat_ap(bt, 262144, [[4, 128], [1, 4]]), in_=ZB[:].rearrange("p j e -> p (j e)")),
               reads=[("ZB", 0), ("ZB", 1)], writes=[("BOUNCE", d, "h")], kind="d", semkey="bh")
        S.emit("pool", I("collective_compute", kind="AllGather", op=ALU.bypass,
                         replica_groups=[[0, 1, 2, 3], [4, 5, 6, 7]], ins=[bt.ap()], outs=[gt.ap()]),
               reads=[("BOUNCE", d, "k"), ("BOUNCE", d, "v", 0), ("BOUNCE", d, "v", 1), ("BOUNCE", d, "h")], writes=[("GATH", d)], kind="cc")
        S.emit("sp", I("dma_start", out=HB[:].rearrange("p r j e -> p r (j e)"),
                       in_=flat_ap(gt, 262144, [[4, 128], [BROWS * 512, 4], [1, 4]])),
               reads=[("GATH", d)], writes=["HB"], kind="d", semkey="hb")

    def do_halo(d):
        S.emit("dve", I("tensor_copy", out=HBF[:], in_=HB[:]), reads=["HB"], writes=["HBF"])
        for w, esrc in ((0, 1), (1, 0)):
            for j in range(2):
                S.emit("dve", I("tensor_tensor", out=HTMP[:, w, j, :], in0=HBF[:, :, j, esrc], in1=SEL[:, w, :], op=ALU.mult),
                       reads=["HBF", "SEL"], writes=[("HTMP", w, j)])
        S.emit("dve", I("tensor_reduce", out=HALO[:], in_=HTMP[:], axis=AX.X, op=ALU.add),
               reads=[("HTMP", w, j) for w in range(2) for j in range(2)], writes=["HALO"])

    KALLK = ["KALLc"] + [("KALLg", r) for r in range(4)]
    VALLK = [("VALLc", kk) for kk in range(4)] + ["VALL1"] + [("VALLg", r, jj) for r in range(4) for jj in range(2)]

    def do_load_sample_keys(d):
        gt = gath_t[d]
        S.emit("pool", I("dma_start", out=KALL[:, :, 0:PAST], in_=ckT_d[d]),
               reads=[], writes=["KALLc"], kind="d", semkey="ck")
        for kk in range(4):
            S.emit("pool", I("dma_start", out=VALL[:, kk, :, 0:128],
                             in_=cv_d[d, kk * 128:(kk + 1) * 128, :].rearrange("p (h e) -> p h e", h=4)),
                   reads=[], writes=[("VALLc", kk)], kind="d", semkey=("cvv", kk))
        S.emit("dve", I("memset", ap=VALL[:, :, :, 128:129], constant=1.0), reads=[], writes=["VALL1"])
        for r in range(4):
            S.emit("sp", I("dma_start", out=KALL[:, :, PAST + r * 256:PAST + (r + 1) * 256],
                           in_=flat_ap(gt, r * BROWS * 512, [[256, 128], [128 * 256, 4], [1, 256]])),
                   reads=[("GATH", d)], writes=[("KALLg", r)], kind="d", semkey=("gk", r))
            for jj in range(2):
                S.emit("sp", I("dma_start", out=VALL[:, 4 + 2 * r + jj, :, 0:128],
                               in_=flat_ap(gt, r * BROWS * 512 + 131072 + jj * 65536, [[512, 128], [128, 4], [1, 128]])),
                       reads=[("GATH", d)], writes=[("VALLg", r, jj)], kind="d", semkey=("gv", r, jj))

    ETALL = rview(33360, 6144)

    def oblock(k):
        return 5 + k // 3, (k % 3) * 129

    def do_attention(d, qb, sample, filler=None):
        nkt = 12 if sample else 2
        q0 = qb * 256
        ebase = {}
        ekey = {}
        for h in range(4):
            if sample:
                ebase[h] = (0, 3072)
                ekey[h] = ([("ETS", i) for i in range(0, 3)], [("ETS", i) for i in range(3, 6)])
            else:
                sl = rot("ets", 6)
                ebase[h] = (sl * 1024, sl * 1024 + 512)
                ekey[h] = ([("ETS", sl)], [("ETS", sl)])

        def scores(h):
            for j in range(nkt // 2):
                spair = ((0, 1), (2, 3))[rot("sb", 2)]
                for kk in range(2):
                    kt = 2 * j + kk
                    for c in range(2):
                        if sample:
                            lhsT = KALL[c * 64:(c + 1) * 64, h, kt * 128:(kt + 1) * 128]
                            rk = list(KALLK)
                        else:
                            lhsT = KT[c * 64:(c + 1) * 64, h, q0 + kt * 128:q0 + (kt + 1) * 128]
                            rk = [("KT", h, qb)]
                        S.emit("pe", I("matmul", out=PS[:, spair[c], kk * 256:(kk + 1) * 256], lhsT=lhsT,
                                       rhs=QT[c * 64:(c + 1) * 64, h, q0:q0 + 256], start=True, stop=True),
                               reads=rk + [("QT", h, qb)], writes=[("PS", spair[c])], sig=(kk == 1))
                for c in range(2):
                    o0 = ebase[h][c] + j * 512
                    S.emit("act", I("activation", out=ETALL[:, o0:o0 + 512], in_=PS[:, spair[c], :],
                                    func=AF.Exp, scale=0.125),
                           reads=[("PS", spair[c])], writes=ekey[h][c])

        def batch(units):
            U = len(units)
            oi = rot("atto", 2)
            ATT_O = ATT_OB[oi]
            obanks = sorted(set(oblock(k)[0] for k in range(2 * U)))
            okeys = [("PS", bk) for bk in obanks]
            order = [(u, c) for c in range(2) for u in range(U)] if sample else [(u, c) for u in range(U) for c in range(2)]
            for (u, c) in order:
                h, qt = units[u]
                if True:
                    bk, col = oblock(u * 2 + c)
                    for kt in range(nkt):
                        if sample:
                            rhs = VALL[:, kt, h, 0:129]
                            rk = list(VALLK)
                        else:
                            rhs = VA[:, qb * 2 + kt, h, 0:129]
                            rk = [("VA", qb * 2 + kt), "VA1"]
                        e0 = ebase[h][c] + kt * 256 + qt * 128
                        last = (kt == nkt - 1)
                        S.emit("pe", I("matmul", out=PS[:, bk, col:col + 129], lhsT=ETALL[:, e0:e0 + 128], rhs=rhs,
                                       start=(kt == 0), stop=last),
                               reads=rk + ekey[h][c], writes=[("PS", bk)], sig=last)
            sm = ATT_S
            for bk in obanks:
                ks = [k for k in range(2 * U) if oblock(k)[0] == bk]
                n = len(ks)
                S.emit("dve", I("reciprocal", out=sm[:, ks[0]:ks[0] + n],
                                in_=PS[:, bk, 0:n * 129].rearrange("p (k e) -> p k e", e=129)[:, :, 128]),
                       reads=[("PS", bk)], writes=[("ATTS", "r", bk)])
            rkeys = [("ATTS", "r", bk) for bk in obanks]
            S.emit("dve", I("tensor_scalar", out=sm[:, 8:8 + U], in0=sm[:, 1:2 * U:2], scalar1=LAM[:, d, 0:1],
                            scalar2=None, op0=ALU.mult),
                   reads=rkeys + [("LAM", d, 0)], writes=[("ATTS", "n")])
            for u in range(U):
                bk, col = oblock(u * 2)
                S.emit("dve", I("tensor_scalar", out=ATT_A[:, u, :], in0=PS[:, bk, col:col + 128],
                                scalar1=sm[:, 2 * u:2 * u + 1], scalar2=None, op0=ALU.mult),
                       reads=[("PS", bk)] + rkeys, writes=[("ATTA", u)])
            for u in range(U):
                bk, col = oblock(u * 2 + 1)
                S.emit("dve", I("scalar_tensor_tensor", out=ATT_A[:, u, :], in0=PS[:, bk, col:col + 128],
                                scalar=sm[:, 8 + u:9 + u], in1=ATT_A[:, u, :], op0=ALU.mult, op1=ALU.add),
                       reads=[("PS", bk), ("ATTS", "n"), ("ATTA", u)], writes=[("ATTA", u)])
            akeys = [("ATTA", u) for u in range(U)]
            jn = rot("ntmp", 3)
            ATT_J = NTMP[jn]
            S.emit("dve", I("tensor_tensor", out=ATT_J[:, 0:U * 128], in0=ATT_A[:, 0:U, :].rearrange("p u e -> p (u e)"),
                            in1=ATT_A[:, 0:U, :].rearrange("p u e -> p (u e)"), op=ALU.mult),
                   reads=akeys, writes=[("NTMP", jn)])
            S.emit("dve", I("tensor_reduce", out=sm[:, 16:16 + U], in_=ATT_J[:, 0:U * 128].rearrange("p (u e) -> p u e", u=U),
                            axis=AX.X, op=ALU.add),
                   reads=[("NTMP", jn)], writes=[("ATTS", "q")])
            S.emit("act", I("activation", out=sm[:, 20:20 + U], in_=sm[:, 16:16 + U], func=AF.Ln, bias=EPSB[:, 0:1],
                            scale=1.0 / 128.0),
                   reads=[("ATTS", "q"), "EPSB"], writes=[("ATTS", "l")])
            S.emit("act", I("activation", out=sm[:, 24:24 + U], in_=sm[:, 20:20 + U], func=AF.Exp, scale=-0.5),
                   reads=[("ATTS", "l")], writes=[("ATTS", "s")])
            for u in range(U):
                S.emit("dve", I("scalar_tensor_tensor", out=ATT_O[:, u, :], in0=ATT_A[:, u, :], scalar=sm[:, 24 + u:25 + u],
                                in1=GSUB[:], op0=ALU.mult, op1=ALU.mult),
                       reads=[("ATTA", u), ("ATTS", "s"), "GSUB"], writes=[("ATTO", oi, u)])
            def fin():
                TPV = PS[:, 4, 0:256].bitcast(BF16)
                for u in range(U):
                    S.emit("pe", I("transpose", out=TPV[:, u * 128:(u + 1) * 128], in_=ATT_O[:, u, :], identity=IDB[:]),
                           reads=[("ATTO", oi, u), "IDB"], writes=[("PS", 4)], sig=(u == U - 1))
                if sample:
                    hh = units[0][0]
                    S.emit("act", I("activation", out=CAT[:, hh, q0:q0 + 256], in_=TPV[:, 0:256], func=AF.Copy),
                           reads=[("PS", 4)], writes=[("HT", hh, qb)])
                else:
                    qt0 = units[0][1]
                    t0 = q0 + qt0 * 128
                    S.emit("act", I("activation", out=CAT[:, 0:4, t0:t0 + 128],
                                    in_=TPV[:, 0:512].rearrange("p (h q) -> p h q", h=4), func=AF.Copy),
                           reads=[("PS", 4)], writes=[("HT", hh, qb) for hh in range(4)])
            return fin

        def push(fin):
            att_pending.append(fin)
            while len(att_pending) > 1:
                att_pending.pop(0)()

        if sample:
            for h in range(4):
                scores(h)
                push(batch([(h, 0), (h, 1)]))
                if filler is not None:
                    filler()
        else:
            for h in range(4):
                scores(h)
            for qt in range(2):
                push(batch([(h, qt) for h in range(4)]))
                if filler is not None:
                    filler()

    att_pending = []

    def att_flush():
        while att_pending:
            att_pending.pop(0)()

    def do_conv_prompt(d):
        for j in range(2):
            Z3 = ZU[:, j, 0:NPT].rearrange("p (s t) -> p s t", s=4)
            C3 = CAT[:, 4 + j, 0:NPT].rearrange("p (s t) -> p s t", s=4)
            zr = [("ZU", j, b) for b in range(4)]
            ck = [("HT", 4 + j, b) for b in range(4)]
            S.emit("dve", I("tensor_scalar", out=CONVY[:, 0:NPT], in0=ZU[:, j, 0:NPT], scalar1=CW[:, d, j, 1:2],
                            scalar2=CW[:, d, j, 3:4], op0=ALU.mult, op1=ALU.add),
                   reads=zr + ["CW"], writes=["CONVY", ("RSTD", 0), ("RSTD", 1)])
            Y3 = CONVY[:, 0:NPT].rearrange("p (s t) -> p s t", s=4)
            S.emit("dve", I("scalar_tensor_tensor", out=Y3[:, :, 1:256], in0=Z3[:, :, 0:255], scalar=CW[:, d, j, 0:1],
                            in1=Y3[:, :, 1:256], op0=ALU.mult, op1=ALU.add),
                   reads=zr + ["CW", "CONVY"], writes=["CONVY"])
            S.emit("dve", I("scalar_tensor_tensor", out=Y3[:, :, 0:255], in0=Z3[:, :, 1:256], scalar=CW[:, d, j, 2:3],
                            in1=Y3[:, :, 0:255], op0=ALU.mult, op1=ALU.add),
                   reads=zr + ["CW", "CONVY"], writes=["CONVY"])
            S.emit("dve", I("tensor_tensor", out=CAT[:, 4 + j, 0:NPT], in0=CONVY[:, 0:NPT], in1=G[:, j, 0:NPT], op=ALU.mult),
                   reads=["CONVY"] + [("G", j, b) for b in range(4)], writes=ck)

    def do_conv(d, blk):
        t0 = blk * 256
        sample = blk == 4
        for j in range(2):
            ci = rot("ntmp", 3)
            Y = NTMP[ci][:, 0:256]
            zr = [("ZU", j, blk)]
            S.emit("dve", I("tensor_scalar", out=Y[:], in0=ZU[:, j, t0:t0 + 256], scalar1=CW[:, d, j, 1:2],
                            scalar2=CW[:, d, j, 3:4], op0=ALU.mult, op1=ALU.add),
                   reads=zr + ["CW"], writes=[("NTMP", ci)])
            S.emit("dve", I("scalar_tensor_tensor", out=Y[:, 1:256], in0=ZU[:, j, t0:t0 + 255], scalar=CW[:, d, j, 0:1],
                            in1=Y[:, 1:256], op0=ALU.mult, op1=ALU.add),
                   reads=zr + ["CW", ("NTMP", ci)], writes=[("NTMP", ci)])
            S.emit("dve", I("scalar_tensor_tensor", out=Y[:, 0:255], in0=ZU[:, j, t0 + 1:t0 + 256], scalar=CW[:, d, j, 2:3],
                            in1=Y[:, 0:255], op0=ALU.mult, op1=ALU.add),
                   reads=zr + ["CW", ("NTMP", ci)], writes=[("NTMP", ci)])
            if sample:
                S.emit("dve", I("scalar_tensor_tensor", out=Y[:, 0:1], in0=HALO[:, 0, j:j + 1], scalar=CW[:, d, j, 0:1],
                                in1=Y[:, 0:1], op0=ALU.mult, op1=ALU.add),
                       reads=["HALO", "CW", ("NTMP", ci)], writes=[("NTMP", ci)])
                S.emit("dve", I("scalar_tensor_tensor", out=Y[:, 255:256], in0=HALO[:, 1, j:j + 1], scalar=CW[:, d, j, 2:3],
                                in1=Y[:, 255:256], op0=ALU.mult, op1=ALU.add),
                       reads=["HALO", "CW", ("NTMP", ci)], writes=[("NTMP", ci)])
            S.emit("dve", I("tensor_tensor", out=CAT[:, 4 + j, t0:t0 + 256], in0=Y[:], in1=G[:, j, t0:t0 + 256], op=ALU.mult),
                   reads=[("NTMP", ci), ("G", j, blk)], writes=[("HT", 4 + j, blk)])

    def do_cmlp(d, blk):
        for half in range(2):
            tk = blk * 2 + half
            t0 = tk * 128
            for pair in range(2):
                bank = mmbank()
                S.emit("pe", I("matmul", out=PS[:, bank, 0:256], lhsT=VS[:, tk, pair * 128:(pair + 1) * 128],
                               rhs=WST[:, d, pair * 2:pair * 2 + 2, :].rearrange("q g p -> q (g p)"),
                               start=True, stop=True),
                       reads=[("VS", tk), "WST"], writes=[("PS", bank)], sig=True)
                ci = rot("ntmp", 3)
                for gi in range(2):
                    S.emit("dve", I("tensor_tensor", out=NTMP[ci][gi * 64:gi * 64 + 64, 0:128],
                                    in0=PS[gi * 64:gi * 64 + 64, bank, gi * 128:(gi + 1) * 128],
                                    in1=BSB[gi * 64:gi * 64 + 64, d, pair, :], op=ALU.add),
                           reads=[("PS", bank), "BSB"], writes=[("NTMP", ci, gi)])
                S.emit("dve", I("tensor_tensor", out=CAT[:, 6 + pair, t0:t0 + 128], in0=NTMP[ci][:, 0:128],
                                in1=ZU[:, 2 + pair, t0:t0 + 128], op=ALU.mult),
                       reads=[("NTMP", ci, 0), ("NTMP", ci, 1), ("ZU", 2 + pair, blk)], writes=[("HT", 6 + pair, blk)])

    def resid_evac(bank, oc, ti, d, which, rng=None):
        t0, n = TT[ti] if rng is None else rng
        v = 0 if t0 < NPT else 1
        bl = blks(t0, n)
        S.emit("dve", I("scalar_tensor_tensor", out=X[:, oc, t0:t0 + n], in0=PS[:, bank, 0:n],
                        scalar=mod_ap(d, which, oc, v), in1=X[:, oc, t0:t0 + n], op0=ALU.mult, op1=ALU.add),
               reads=[("PS", bank)] + mod_key(d, which) + [("X", oc, b) for b in bl], writes=[("X", oc, b) for b in bl])

    def out_groups(d, pis, rng):
        slots = [wslot(pis[0], la=NSLOT - 1), wslot(pis[1], la=NSLOT - 2)]
        res = []
        for j in range(2):
            wv = wview(slots[j], 8, 512)
            for fc in range(4):
                def g(j=j, fc=fc, wv=wv):
                    oc = j * 4 + fc
                    t0, n = rng
                    bank = mmbank()
                    for kc in range(8):
                        S.emit("pe", I("matmul", out=PS[:, bank, 0:n], lhsT=wv[:, kc, fc * 128:(fc + 1) * 128],
                                       rhs=CAT[:, kc, t0:t0 + n], start=(kc == 0), stop=(kc == 7)),
                               reads=wkeys(slots[j]) + [("HT", kc, b) for b in blks(t0, n)], writes=[("PS", bank)],
                               sig=(kc == 7))
                    resid_evac(bank, oc, 0, d, 2, rng=rng)
                res.append(g)
        return res

    def do_out_piece(d, j, pi, tis=(0, 1, 2)):
        slot = wslot(pi, la=NSLOT - 1 - j)
        wv = wview(slot, 8, 512)
        for ti in tis:
            for fc in range(4):
                oc = j * 4 + fc
                t0, n = TT[ti]
                bank = mmbank()
                for kc in range(8):
                    S.emit("pe", I("matmul", out=PS[:, bank, 0:n], lhsT=wv[:, kc, fc * 128:(fc + 1) * 128],
                                   rhs=CAT[:, kc, t0:t0 + n], start=(kc == 0), stop=(kc == 7)),
                           reads=wkeys(slot) + [("HT", kc, b) for b in blks(t0, n)], writes=[("PS", bank)], sig=(kc == 7))
                resid_evac(bank, oc, ti, d, 2)

    def do_up_piece(d, j, pi):
        slot = wslot(pi)
        wv = wview(slot, 8, 512)
        for ti in range(3):
            for fc in range(4):
                hc = j * 4 + fc
                t0, n = TT[ti]
                bank = mmbank()
                ws_matmul_fm(slot, wv, fc, ti, bank)
                nt = rot("ntmp", 3)
                S.emit("act", I("activation", out=NTMP[nt][:, 0:n], in_=PS[:, bank, 0:n], func=AF.Relu),
                       reads=[("PS", bank)], writes=[("NTMP", nt)])
                S.emit("dve", I("tensor_tensor", out=HID[:, hc, t0:t0 + n], in0=NTMP[nt][:, 0:n], in1=NTMP[nt][:, 0:n],
                                op=ALU.mult),
                       reads=[("NTMP", nt)], writes=[("HID", hc, b) for b in blks(t0, n)])

    stat_pending = []

    def do_down_piece(d, j, pi):
        slot = wslot(pi)
        wv = wview(slot, 32, 128)
        for ti in range(3):
            t0, n = TT[ti]
            bank = mmbank()
            for kc in range(32):
                S.emit("pe", I("matmul", out=PS[:, bank, 0:n], lhsT=wv[:, kc, :], rhs=HID[:, kc, t0:t0 + n],
                               start=(kc == 0), stop=(kc == 31)),
                       reads=wkeys(slot) + [("HID", kc, b) for b in blks(t0, n)], writes=[("PS", bank)], sig=(kc == 31))
            resid_evac(bank, j, ti, d, 5)
            stat_pending.append((j, ti))
            while len(stat_pending) > (0 if (j == 7 and ti == 2) else 2):
                stat_accum(*stat_pending.pop(0))

    REGION_MIX = ([("QT", c, b) for c in range(4) for b in range(5)] + [("KT", c, b) for c in range(4) for b in range(5)]
                  + [("VA", k) for k in range(10)] + ["VA1"] + [("G", c, b) for c in range(4) for b in range(5)]
                  + [("ZU", c, b) for c in range(4) for b in range(5)] + [("VS", k) for k in range(10)]
                  + [("ETS", i) for i in range(6)] + KALLK + VALLK)
    REGION_HID = [("HID", c, b) for c in range(32) for b in range(5)]
    GZ_KEYS = [("G", c, b) for c in range(4) for b in range(5)] + [("ZU", c, b) for c in range(4) for b in range(5)]

    def fence(keys):
        S.emit("dve", I("memset", ap=FEN[:, 0:1], constant=0.0), reads=[], writes=list(keys))

    for (kind, d, j, pi) in plan:
        if kind == "mod":
            do_mod_piece(d, j, pi)
            continue
        if kind == "in_g":
            load(GSUB[:], gsub_d[:, d, :], "GSUB")
            S.emit("dve", I("tensor_scalar", out=GSUB[:], in0=GSUB[:], scalar1=(1.0 - lam_inits[d]),
                            scalar2=None, op0=ALU.mult),
                   reads=["GSUB"], writes=["GSUB"])
            do_scale_prep(d, 1, GMIX, "GMIX", S1, "S1")
            fence(REGION_MIX + REGION_HID)
            do_norm(d, S1, "S1", 0, have_stats=(d > 0))
            S.emit("dve", I("memset", ap=VA[:, :, :, 128:129], constant=1.0), reads=[], writes=["VA1"])
        if kind.startswith("in_"):
            do_in_piece(kind[3:], d, pi)
            if kind == "in_v":
                do_exchange(d)
            continue
        if kind == "mix":
            continue
        if kind == "out":
            if j == 0:
                out_pis = [pi]
                continue
            out_pis.append(pi)
            gb = [out_groups(d, out_pis, (b * 256, 256)) for b in range(5)]

            def fill_from(lst, k):
                def f():
                    for _ in range(k):
                        if lst:
                            lst.pop(0)()
                return f

            sched = [None, (0, 4), (0, 4), (1, 4), (1, 4), (2, 4), (2, 4), (3, 3), (3, 3), (3, 2)]
            fidx = [0]

            def filler():
                e = sched[fidx[0]] if fidx[0] < len(sched) else None
                fidx[0] += 1
                if e is not None:
                    fill_from(gb[e[0]], e[1])()

            do_attention(d, 0, False)
            do_attention(d, 1, False, filler=filler)
            do_halo(d)
            do_conv(d, 4)
            fence(GZ_KEYS + KALLK + VALLK)
            do_load_sample_keys(d)
            do_attention(d, 2, False, filler=filler)
            do_scale_prep(d, 4, GMLP, "GMLP", S2, "S2")
            do_attention(d, 3, False, filler=filler)
            att_flush()
            fill_from(gb[0], 8)()
            fill_from(gb[1], 8)()
            do_norm(d, S2, "S2", 3, tis=(0,))
            do_attention(d, 4, True, filler=filler)
            for b_ in range(4):
                fill_from(gb[b_], 8)()
            att_flush()
            fill_from(gb[4], 8)()
            do_norm(d, S2, "S2", 3, tis=(1,))
            do_norm(d, S2, "S2", 3, tis=(2,))
            fence(REGION_MIX + REGION_HID)
            continue
        if kind == "up":
            do_up_piece(d, j, pi)
            continue
        if kind == "down":
            do_down_piece(d, j, pi)
            continue

    for ti, (t0, n) in enumerate(TT):
        bank = 5 + ti
        bl = blks(t0, n)
        rs = rot("rstd", 2)
        S.emit("act", I("activation", out=RSTD[rs][:, 0:n], in_=PS[:, bank, 0:n], func=AF.Ln, bias=EPSB[:, 0:1], scale=1.0),
               reads=[("PS", bank), "EPSB"], writes=[("RSTD", rs)])
        S.emit("act", I("activation", out=RSTD[rs][:, 0:n], in_=RSTD[rs][:, 0:n], func=AF.Exp, scale=-0.5),
               reads=[("RSTD", rs)], writes=[("RSTD", rs)])
        for fc in range(8):
            S.emit("dve", I("scalar_tensor_tensor", out=X[:, fc, t0:t0 + n], in0=X[:, fc, t0:t0 + n],
                            scalar=GFIN[:, fc:fc + 1], in1=RSTD[rs][:, 0:n], op0=ALU.mult, op1=ALU.mult),
                   reads=[("X", fc, b) for b in bl] + ["GFIN", ("RSTD", rs)], writes=[("X", fc, b) for b in bl])
    for fc in range(8):
        S.emit("sp", I("dma_start", out=yT_d[:, fc, :], in_=X[:, fc, :]),
               reads=[("X", fc, b) for b in range(5)], writes=[], kind="d", semkey=("yo", fc))

    S.run(nc)
    st.close()
    return nc


def _rope_tables_np(pos0, n):
    t = np.arange(pos0, pos0 + n)
    row = (t // 64).astype(np.float64)
    col = (t % 64).astype(np.float64)
    inv = 10000.0 ** (-np.arange(0, 32, 2, dtype=np.float64) / 32.0)
    ar = row[:, None] * inv[None, :]
    ac = col[:, None] * inv[None, :]
    cosT = np.zeros((128, n), np.float32)
    sinS = np.zeros((128, n), np.float32)
    for p in range(128):
        e = p % 64
        ax = e // 32
        i = e % 32
        first = i < 16
        ang = (ar if ax == 0 else ac)[:, i % 16]
        cosT[p] = np.cos(ang)
        sinS[p] = (-np.sin(ang)) if first else np.sin(ang)
    return cosT, sinS


def _perm_np():
    P = np.zeros((128, 128), np.float32)
    for m in range(128):
        i = m % 32
        partner = m + 16 if i < 16 else m - 16
        P[partner, m] = 1.0
    return P


_NC_CACHE = {}


def _prepare(x_prompt, x_sample, cache_k, cache_v, c, c_ctx, w_mod, b_mod, norm_mix, norm_mlp,
             w_in, lam_q1, lam_k1, lam_q2, lam_k2, subln, conv_w, conv_b, w_s, b_s,
             w_out, w_up, w_down, norm_final):
    f = lambda a: np.ascontiguousarray(np.asarray(a, dtype=np.float32))
    x_prompt, x_sample, cache_k, cache_v, c, c_ctx = map(f, (x_prompt, x_sample, cache_k, cache_v, c, c_ctx))
    w_mod, b_mod, norm_mix, norm_mlp, w_in = map(f, (w_mod, b_mod, norm_mix, norm_mlp, w_in))
    lam_q1, lam_k1, lam_q2, lam_k2, subln = map(f, (lam_q1, lam_k1, lam_q2, lam_k2, subln))
    conv_w, conv_b, w_s, b_s, w_out, w_up, w_down, norm_final = map(
        f, (conv_w, conv_b, w_s, b_s, w_out, w_up, w_down, norm_final))
    depth = w_in.shape[0]
    bmod = np.ascontiguousarray(b_mod.reshape(depth, 48, 128).transpose(2, 0, 1))
    gmix = np.ascontiguousarray(norm_mix.reshape(depth, 8, 128).transpose(2, 0, 1))
    gmlp = np.ascontiguousarray(norm_mlp.reshape(depth, 8, 128).transpose(2, 0, 1))
    gfin = np.ascontiguousarray(norm_final.reshape(8, 128).T)
    lamv = np.ascontiguousarray(np.broadcast_to(np.stack([lam_q1, lam_k1, lam_q2, lam_k2], axis=1)[None], (128, depth, 4, 64)))
    gsub = np.ascontiguousarray(np.broadcast_to(subln[None], (128, depth, 128)))
    cw = np.zeros((128, depth, 2, 4), np.float32)
    cw[:, :, :, 0:3] = conv_w.reshape(depth, 3, 2, 128).transpose(3, 0, 2, 1)
    cw[:, :, :, 3] = conv_b.reshape(depth, 2, 128).transpose(2, 0, 1)
    wsT = np.ascontiguousarray(w_s.transpose(3, 0, 1, 2))
    bsB = np.zeros((128, depth, 2, 128), np.float32)
    for pair in range(2):
        for gi in range(2):
            bsB[gi * 64:(gi + 1) * 64, :, pair, :] = b_s[:, pair * 2 + gi, :][None]
    perm = _perm_np()
    ident = np.eye(128, dtype=np.float32)
    in_maps = []
    for core in range(8):
        s = core // 4
        r = core % 4
        xtok = np.concatenate([x_prompt[4 * core:4 * core + 4].reshape(1024, D),
                               x_sample[s, r * 256:(r + 1) * 256]], axis=0)
        xT = np.ascontiguousarray(xtok.reshape(T, 8, 128).transpose(2, 1, 0))
        ckT = np.ascontiguousarray(cache_k[s, :depth].transpose(0, 3, 2, 1))
        cv = np.ascontiguousarray(cache_v[s, :depth].reshape(depth, PAST, 512))
        cvec = np.ascontiguousarray(np.stack([c_ctx, c[s]], axis=-1).reshape(8, 128, 2).transpose(1, 0, 2))
        cosT, sinS = _rope_tables_np(r * 256, 256)
        sel = np.zeros((128, 2, 4), np.float32)
        if r > 0:
            sel[:, 0, r - 1] = 1.0
        if r < 3:
            sel[:, 1, r + 1] = 1.0
        in_maps.append({
            "xT": xT, "ckT": ckT, "cv": cv, "cvec": cvec,
            "w_mod": w_mod, "w_in": w_in, "w_out": w_out, "w_up": w_up, "w_down": w_down,
            "bmod": bmod, "gmix": gmix, "gmlp": gmlp, "gfin": gfin, "lamv": lamv, "gsub": gsub,
            "cw": cw, "wsT": wsT, "bsB": bsB, "cosT": cosT, "sinS": sinS, "perm": perm, "ident": ident,
            "sel": sel,
        })
    return depth, in_maps


def _assemble(outs, depth):
    y_prompt = np.zeros((32, 256, D), np.float32)
    y_sample = np.zeros((2, 1024, D), np.float32)
    new_k = np.zeros((32, depth, 256, 4, 128), np.float32)
    new_v = np.zeros((32, depth, 256, 4, 128), np.float32)
    for core in range(8):
        s = core // 4
        r = core % 4
        ytok = np.asarray(outs[core]["yT"]).transpose(2, 1, 0).reshape(T, D)
        y_prompt[4 * core:4 * core + 4] = ytok[:1024].reshape(4, 256, D)
        y_sample[s, r * 256:(r + 1) * 256] = ytok[1024:]
        new_k[4 * core:4 * core + 4] = np.asarray(outs[core]["nk"]).reshape(4, depth, 256, 4, 128)
        new_v[4 * core:4 * core + 4] = np.asarray(outs[core]["nv"]).reshape(4, depth, 256, 4, 128)
    return (y_prompt, y_sample, new_k, new_v)


def kernel(**inputs):
    depth, in_maps = _prepare(**inputs)
    if depth not in _NC_CACHE:
        _NC_CACHE[depth] = build_nc(depth)
    nc = _NC_CACHE[depth]
    res = run_bass_kernel_spmd(nc, in_maps, core_ids=list(range(8)))
    return _assemble(res.results, depth)
```

```python
import math
import numpy as np
import concourse.bass as bass
import concourse.mybir as mybir
from concourse.bass_utils import run_bass_kernel_spmd

F32 = mybir.dt.float32
BF16 = mybir.dt.bfloat16
AF = mybir.ActivationFunctionType
ALU = mybir.AluOpType
AX = mybir.AxisListType

D = 1024
DEPTH = 4
T = 1280
NPT = 1024
NST = 256
PAST = 512
NKS = PAST + 1024
EPS = 1e-6
TT = [(0, 512), (512, 512), (1024, 256)]
NSLOT = 4
SLOT_ELEMS = 4096
BROWS = 513


def blks(a, n):
    return list(range(a // 256, (a + n + 255) // 256))


class Op:
    __slots__ = ("q", "fn", "deps", "kind", "sig", "sem", "val", "inc", "idx")


class Sched:
    QS = ("pe", "act", "dve", "pool", "sp")

    def __init__(self):
        self.ops = {q: [] for q in self.QS}
        self.res = {}
        self.dma_keys = {}
        self.cc_count = 0

    def emit(self, q, fn, reads=(), writes=(), kind="c", sig=True, semkey=None):
        op = Op()
        op.q = q
        op.fn = fn
        op.kind = kind
        op.sig = sig
        op.sem = None
        op.val = None
        op.inc = 1
        deps = {}
        for r in reads:
            e = self.res.get(r)
            if e is not None and e[0] is not None:
                deps[id(e[0])] = (e[0], True)
        for w in writes:
            e = self.res.get(w)
            if e is not None:
                if e[0] is not None and id(e[0]) not in deps:
                    deps[id(e[0])] = (e[0], False)
                for rd in e[1]:
                    if id(rd) not in deps:
                        deps[id(rd)] = (rd, False)
        dl = []
        for dep, raw in deps.values():
            if dep is op:
                continue
            if dep.q == q and dep.kind == "c" and kind == "c":
                if q == "pe":
                    continue
            dl.append(dep)
        if kind == "d":
            ent = self.dma_keys.setdefault(semkey, [0, None])
            if ent[1] is not None:
                dl.append(ent[1])
            ent[0] += 1
            ent[1] = op
            op.sem = ("dma", semkey)
            op.val = ent[0] * 16
        elif kind == "cc":
            self.cc_count += 1
            op.sem = ("cc", 0)
            op.val = self.cc_count
        op.deps = dl
        for r in reads:
            e = self.res.setdefault(r, [None, []])
            e[1].append(op)
        for w in writes:
            self.res[w] = [op, []]
        op.idx = len(self.ops[q])
        self.ops[q].append(op)
        return op

    def finalize(self):
        for q in self.QS:
            ops = [o for o in self.ops[q] if o.kind == "c"]
            if ops:
                ops[-1].sig = True
            cnt = 0
            for o in ops:
                if o.sig:
                    cnt += 1
                    o.val = cnt
                o.sem = ("q", q)
            nxt = None
            for o in reversed(ops):
                if o.sig:
                    nxt = o.val
                else:
                    o.val = nxt

    def run(self, nc):
        self.finalize()
        names = [("q", q) for q in self.QS] + [("dma", k) for k in self.dma_keys] + [("cc", 0)]
        from contextlib import ExitStack
        with ExitStack() as st:
            sems = {}
            for i, n in enumerate(names):
                sems[n] = st.enter_context(nc.semaphore("s%d" % i))
            block = st.enter_context(nc.Block())
            handles = {"pe": block.tensor, "act": block.scalar, "dve": block.vector,
                       "pool": block.gpsimd, "sp": block.sync}
            for q in self.QS:
                ops = self.ops[q]

                def body(eng, ops=ops, q=q):
                    waited = {}
                    for o in ops:
                        for dpn in o.deps:
                            if waited.get(dpn.sem, 0) < dpn.val:
                                eng.wait_ge(sems[dpn.sem], dpn.val)
                                waited[dpn.sem] = dpn.val
                        ins = o.fn(eng)
                        if o.kind == "c":
                            if o.sig:
                                ins.then_inc(sems[o.sem], 1)
                        elif o.kind == "d":
                            ins.then_inc(sems[o.sem], 16)
                        else:
                            ins.then_inc(sems[o.sem], 1)
                    if q == "sp":
                        for k, ent in self.dma_keys.items():
                            if waited.get(("dma", k), 0) < ent[0] * 16:
                                eng.wait_ge(sems[("dma", k)], ent[0] * 16)
                        for qq in self.QS:
                            cops = [o for o in self.ops[qq] if o.kind == "c"]
                            if cops:
                                eng.wait_ge(sems[("q", qq)], cops[-1].val)

                handles[q](body)


def I(method, **kw):
    return lambda e: getattr(e, method)(**kw)


def build_nc(depth=DEPTH):
    nc = bass.Bass("TRN2", target_bir_lowering=False)
    S = Sched()

    def din(name, shape, dt=F32):
        return nc.dram_tensor(name, list(shape), dt, kind="ExternalInput").ap()

    xT_d = din("xT", [128, 8, T])
    ckT_d = din("ckT", [depth, 128, 4, PAST])
    cv_d = din("cv", [depth, PAST, 512])
    cvec_d = din("cvec", [128, 8, 2])
    w_mod_d = din("w_mod", [depth, D, 6 * D])
    w_in_d = din("w_in", [depth, D, 2816])
    w_out_d = din("w_out", [depth, D, D])
    w_up_d = din("w_up", [depth, D, 4 * D])
    w_down_d = din("w_down", [depth, 4 * D, D])
    bmod_d = din("bmod", [128, depth, 48])
    gmix_d = din("gmix", [128, depth, 8])
    gmlp_d = din("gmlp", [128, depth, 8])
    gfin_d = din("gfin", [128, 8])
    lamv_d = din("lamv", [128, depth, 4, 64])
    gsub_d = din("gsub", [128, depth, 128])
    cw_d = din("cw", [128, depth, 2, 4])
    wsT_d = din("wsT", [128, depth, 4, 128])
    bsB_d = din("bsB", [128, depth, 2, 128])
    cosT_d = din("cosT", [128, NST])
    sinS_d = din("sinS", [128, NST])
    perm_d = din("perm", [128, 128])
    ident_d = din("ident", [128, 128])
    sel_d = din("sel", [128, 2, 4])

    yT_d = nc.dram_tensor("yT", [128, 8, T], F32, kind="ExternalOutput").ap()
    nk_d = nc.dram_tensor("nk", [4, depth, 256, 512], F32, kind="ExternalOutput").ap()
    nv_d = nc.dram_tensor("nv", [4, depth, 256, 512], F32, kind="ExternalOutput").ap()
    bounce_t = [nc.dram_tensor("bounce%d" % d, [BROWS, 512], BF16, kind="Internal") for d in range(depth)]
    gath_t = [nc.dram_tensor("gath%d" % d, [4 * BROWS, 512], BF16, kind="Internal") for d in range(depth)]

    from contextlib import ExitStack
    st = ExitStack()

    def sb(name, shape, dt):
        return st.enter_context(nc.sbuf_tensor(name, list(shape), dt))

    X = sb("X", [128, 8, T], F32)
    HT = sb("HT", [128, 8, T], BF16)
    WS = sb("WS", [128, NSLOT, SLOT_ELEMS], BF16)
    R = sb("R", [128, 40960], BF16)
    PS = st.enter_context(nc.psum_tensor("PS", [128, 8, 512], F32))

    def rview(off, n, pat=None, dt=None, **kw):
        v = R[:, off:off + n]
        if dt is not None:
            v = v.bitcast(dt)
        if pat is not None:
            v = v.rearrange(pat, **kw)
        return v

    HID = rview(0, 40960, "p (c t) -> p c t", c=32)
    QT = rview(0, 5120, "p (c t) -> p c t", c=4)
    KT = rview(5120, 5120, "p (c t) -> p c t", c=4)
    VA = rview(10240, 5200, "p (k h e) -> p k h e", k=10, h=4)
    G = rview(15440, 5120, "p (c t) -> p c t", c=4)
    ZU = rview(20560, 10240, "p (c t) -> p c t", dt=F32, c=4)
    VS = rview(30800, 2560, "p (k e) -> p k e", k=10)
    KALL = rview(15440, 6144, "p (h k) -> p h k", h=4)
    VALL = rview(21584, 6240, "p (k h e) -> p k h e", k=12, h=4)
    CAT = HT
    LAMV = rview(33360, depth * 512, "p (d a e) -> p d a e", dt=F32, d=depth, a=4)
    LTMP = rview(36432, 512, "p (a e) -> p a e", dt=F32, a=4)

    MOD = [sb("MOD%d" % i, [128, 48, 2], F32) for i in range(2)]
    S1 = sb("S1", [128, 8, 2], F32)
    S2 = sb("S2", [128, 8, 2], F32)
    SC = sb("SC", [128, 8, 2], BF16)
    CV = sb("CVEC", [128, 8, 2], F32)
    BMOD = sb("BMOD", [128, depth, 48], F32)
    GMIX = sb("GMIX", [128, depth, 8], F32)
    GMLP = sb("GMLP", [128, depth, 8], F32)
    GFIN = sb("GFIN", [128, 8], F32)
    GSUB = sb("GSUB", [128, 128], F32)
    CW = sb("CW", [128, depth, 2, 4], F32)
    WST = sb("WST", [128, depth, 4, 128], BF16)
    BSB = sb("BSB", [128, depth, 2, 128], F32)
    COST = sb("COST", [128, NST], F32)
    SINS = sb("SINS", [128, NST], F32)
    PERM = sb("PERM", [128, 128], F32)
    IDB = sb("IDB", [128, 128], BF16)
    ONES = sb("ONES", [128, 128], BF16)
    SEL = sb("SEL", [128, 2, 4], F32)
    LAM = sb("LAM", [128, depth, 4], F32)
    SQ = [sb("SQ%d" % i, [128, 512], BF16) for i in range(3)]
    RSTDT = sb("RSTD", [128, 2, 512], F32)
    RSTD = [RSTDT[:, 0, :], RSTDT[:, 1, :]]
    CONVY = RSTDT[:].rearrange("p a n -> p (a n)")
    NTMP = [sb("NTMP%d" % i, [128, 512], F32) for i in range(3)]
    STGT = sb("STG", [128, 2, 512], F32)
    STG = [STGT[:, 0, :], STGT[:, 1, :]]
    CONVT = STGT[:].rearrange("p a n -> p (a n)")
    ZB = sb("ZB", [128, 2, 2], BF16)
    HB = sb("HB", [128, 4, 2, 2], BF16)
    HBF = sb("HBF", [128, 4, 2, 2], F32)
    HTMP = sb("HTMP", [128, 2, 2, 4], F32)
    HALO = sb("HALO", [128, 2, 2], F32)
    ATT_A = sb("ATTA", [128, 4, 128], F32)
    ATT_OB = [sb("ATTO%d" % i, [128, 4, 128], BF16) for i in range(2)]
    ATT_S = sb("ATTS", [128, 32], F32)
    FEN = sb("FEN", [128, 2], F32)
    EPSB = sb("EPSB", [128, 1], F32)

    rr = {}

    def rot(name, n):
        v = rr.get(name, 0)
        rr[name] = v + 1
        return v % n

    def mmbank():
        return rot("mm", 4)

    pieces = []
    issued = [0]

    def add_piece(src_list):
        pieces.append(src_list)
        return len(pieces) - 1

    def ensure_issued(upto):
        while issued[0] <= min(upto, len(pieces) - 1):
            j = issued[0]
            slot = j % NSLOT
            plist = pieces[j]
            for hi, (kc0, kcn, cols, src) in enumerate(plist):
                dst = WS[:, slot, kc0 * cols:(kc0 + kcn) * cols].rearrange("p (k n) -> p k n", k=kcn)
                wkeys = [("WS", slot, hi)] if len(plist) == 2 else [("WS", slot, 0), ("WS", slot, 1)]
                S.emit("pool", I("dma_start", out=dst, in_=src), reads=[], writes=wkeys,
                       kind="d", semkey=("ws", slot, hi))
            issued[0] += 1

    def wslot(pi, la=NSLOT - 1):
        ensure_issued(pi + la)
        return pi % NSLOT

    def wkeys(slot):
        return [("WS", slot, 0), ("WS", slot, 1)]

    def wview(slot, kcn, cols):
        return WS[:, slot, 0:kcn * cols].rearrange("p (k n) -> p k n", k=kcn)

    def wsrc(w, c0, cols):
        return w.rearrange("(k p) n -> p k n", p=128)[:, :, c0:c0 + cols]

    plan = []
    WIN_ORDER = [(1536, 512, "g"), (2048, 512, "xu"), (512, 512, "k"), (1024, 512, "v"),
                 (0, 512, "q"), (2560, 256, "vs")]

    def mod_piece(d, j):
        pi = add_piece([(0, 8, 512, wsrc(w_mod_d[d], j * 512, 512))])
        plan.append(("mod", d, j, pi))

    for j in range(4):
        mod_piece(0, j)
    pend0 = list(range(4, 12))
    for d in range(depth):
        for wi, (c0, cols, nm) in enumerate(WIN_ORDER):
            pi = add_piece([(0, 8, cols, wsrc(w_in_d[d], c0, cols))])
            plan.append(("in_" + nm, d, 0, pi))
            if d == 0:
                for _ in range(2 if wi < 2 else 1):
                    if pend0:
                        mod_piece(0, pend0.pop(0))
        plan.append(("mix", d, 0, -1))
        for j in range(2):
            pi = add_piece([(0, 8, 512, wsrc(w_out_d[d], j * 512, 512))])
            plan.append(("out", d, j, pi))
        for j in range(8):
            pi = add_piece([(0, 8, 512, wsrc(w_up_d[d], j * 512, 512))])
            plan.append(("up", d, j, pi))
            if d + 1 < depth:
                mod_piece(d + 1, j)
        for j in range(8):
            src = w_down_d[d].rearrange("(k p) n -> p k n", p=128)
            pi = add_piece([(0, 16, 128, src[:, 0:16, j * 128:(j + 1) * 128]),
                            (16, 16, 128, src[:, 16:32, j * 128:(j + 1) * 128])])
            plan.append(("down", d, j, pi))
            if d + 1 < depth and j < 4:
                mod_piece(d + 1, 8 + j)

    def load(dst, src, key, q="sp", extra_w=()):
        S.emit(q, I("dma_start", out=dst, in_=src), reads=[], writes=[key] + list(extra_w),
               kind="d", semkey=key)

    load(CV[:], cvec_d, "CV")
    load(BMOD[:], bmod_d, "BMOD")
    for fc in range(8):
        load(X[:, fc, :], xT_d[:, fc, :], ("Xld", fc), extra_w=[("X", fc, b) for b in range(5)])
    load(GMIX[:], gmix_d, "GMIX")
    load(GMLP[:], gmlp_d, "GMLP")
    load(GFIN[:], gfin_d, "GFIN")
    load(LAMV[:], lamv_d, "LAMV")
    load(CW[:], cw_d, "CW")
    load(WST[:], wsT_d, "WST", q="pool")
    load(BSB[:], bsB_d, "BSB")
    load(COST[:], cosT_d, "COST")
    load(SINS[:], sinS_d, "SINS")
    load(PERM[:], perm_d, "PERM")
    load(IDB[:], ident_d, "IDB", q="pool")
    load(SEL[:], sel_d, "SEL")

    S.emit("dve", I("memset", ap=ONES[:], constant=1.0 / 1024.0), writes=["ONES"])
    S.emit("dve", I("memset", ap=EPSB[:], constant=EPS), writes=["EPSB"])
    S.emit("act", I("activation", out=SC[:], in_=CV[:], func=AF.Silu), reads=["CV"], writes=["SC"])
    lam_inits = [0.8 - 0.6 * math.exp(-0.3 * d) for d in range(depth)]
    for d in range(depth):
        for i in range(2):
            S.emit("dve", I("tensor_tensor", out=LTMP[:, i, :], in0=LAMV[:, d, 2 * i, :],
                            in1=LAMV[:, d, 2 * i + 1, :], op=ALU.mult),
                   reads=["LAMV"], writes=[("LTMP", i)])
        S.emit("dve", I("tensor_reduce", out=LAM[:, d, 1:3], in_=LTMP[:, 0:2, :], axis=AX.X, op=ALU.add),
               reads=[("LTMP", 0), ("LTMP", 1)], writes=[("LAM", d, 1)])
        S.emit("act", I("activation", out=LAM[:, d, 1:3], in_=LAM[:, d, 1:3], func=AF.Exp),
               reads=[("LAM", d, 1)], writes=[("LAM", d, 1)])
        S.emit("dve", I("scalar_tensor_tensor", out=LAM[:, d, 0:1], in0=LAM[:, d, 2:3],
                        scalar=-lam_inits[d], in1=LAM[:, d, 1:2], op0=ALU.add, op1=ALU.subtract),
               reads=[("LAM", d, 1)], writes=[("LAM", d, 0)])

    def do_mod_piece(d, j, pi):
        slot = wslot(pi)
        wv = wview(slot, 8, 512)
        bank = mmbank()
        M = MOD[d % 2]
        for fc in range(4):
            for kc in range(8):
                S.emit("pe", I("matmul", out=PS[:, bank, fc * 2:fc * 2 + 2],
                               lhsT=wv[:, kc, fc * 128:(fc + 1) * 128],
                               rhs=SC[:, kc, :], start=(kc == 0), stop=(kc == 7)),
                       reads=wkeys(slot) + ["SC"], writes=[("PS", bank)], sig=(kc == 7 and fc == 3))
        for v in range(2):
            S.emit("dve", I("tensor_tensor", out=M[:, j * 4:(j + 1) * 4, v], in0=PS[:, bank, v:8:2],
                            in1=BMOD[:, d, j * 4:(j + 1) * 4], op=ALU.add),
                   reads=[("PS", bank), "BMOD"], writes=[("MOD", d % 2, j, v)])

    def mod_ap(d, which, fc, v):
        return MOD[d % 2][:, which * 8 + fc, v:v + 1]

    def mod_key(d, which):
        return [("MOD", d % 2, which * 2 + i, v) for i in range(2) for v in range(2)]

    def do_scale_prep(d, which, gain, gkey, dst, name):
        M = MOD[d % 2]
        for v in range(2):
            S.emit("dve", I("scalar_tensor_tensor", out=dst[:, :, v], in0=M[:, which * 8:which * 8 + 8, v],
                            scalar=1.0, in1=gain[:, d, :], op0=ALU.add, op1=ALU.mult),
                   reads=mod_key(d, which) + [gkey], writes=[(name, v)])

    def stat_accum(oc, ti):
        t0, n = TT[ti]
        bl = blks(t0, n)
        sq = rot("sq", 3)
        S.emit("act", I("activation", out=SQ[sq][:, 0:n], in_=X[:, oc, t0:t0 + n], func=AF.Square),
               reads=[("X", oc, b) for b in bl], writes=[("SQ", sq)])
        S.emit("pe", I("matmul", out=PS[:, 5 + ti, 0:n], lhsT=ONES[:], rhs=SQ[sq][:, 0:n],
                       start=(oc == 0), stop=(oc == 7)),
               reads=["ONES", ("SQ", sq)], writes=[("PS", 5 + ti)], sig=True)

    def do_norm(d, scale_t, scale_name, shift_which, tis=(0, 1, 2), have_stats=False):
        for ti in tis:
            t0, n = TT[ti]
            v = 0 if ti < 2 else 1
            bl = blks(t0, n)
            if have_stats:
                bank = 5 + ti
            else:
                bank = mmbank()
            for fc in range(8):
                if have_stats:
                    break
                sq = rot("sq", 3)
                S.emit("act", I("activation", out=SQ[sq][:, 0:n], in_=X[:, fc, t0:t0 + n], func=AF.Square),
                       reads=[("X", fc, b) for b in bl], writes=[("SQ", sq)])
                S.emit("pe", I("matmul", out=PS[:, bank, 0:n], lhsT=ONES[:], rhs=SQ[sq][:, 0:n],
                               start=(fc == 0), stop=(fc == 7)),
                       reads=["ONES", ("SQ", sq)], writes=[("PS", bank)], sig=True)
            rs = rot("rstd", 2)
            S.emit("act", I("activation", out=RSTD[rs][:, 0:n], in_=PS[:, bank, 0:n], func=AF.Ln, bias=EPSB[:, 0:1], scale=1.0),
                   reads=[("PS", bank), "EPSB"], writes=[("RSTD", rs)])
            S.emit("act", I("activation", out=RSTD[rs][:, 0:n], in_=RSTD[rs][:, 0:n], func=AF.Exp, scale=-0.5),
                   reads=[("RSTD", rs)], writes=[("RSTD", rs)])
            for fc in range(8):
                nt = rot("ntmp", 3)
                S.emit("dve", I("scalar_tensor_tensor", out=NTMP[nt][:, 0:n], in0=X[:, fc, t0:t0 + n],
                                scalar=scale_t[:, fc, v:v + 1], in1=RSTD[rs][:, 0:n], op0=ALU.mult, op1=ALU.mult),
                       reads=[("X", fc, b) for b in bl] + [(scale_name, v), ("RSTD", rs)], writes=[("NTMP", nt)])
                S.emit("act", I("activation", out=HT[:, fc, t0:t0 + n], in_=NTMP[nt][:, 0:n], func=AF.Identity,
                                bias=mod_ap(d, shift_which, fc, v), scale=1.0),
                       reads=[("NTMP", nt)] + mod_key(d, shift_which), writes=[("HT", fc, b) for b in bl])

    def ws_matmul_fm(slot, wv, fc, ti, bank):
        t0, n = TT[ti]
        bl = blks(t0, n)
        for kc in range(8):
            S.emit("pe", I("matmul", out=PS[:, bank, 0:n], lhsT=wv[:, kc, fc * 128:(fc + 1) * 128],
                           rhs=HT[:, kc, t0:t0 + n], start=(kc == 0), stop=(kc == 7)),
                   reads=wkeys(slot) + [("HT", kc, b) for b in bl], writes=[("PS", bank)], sig=(kc == 7))

    def ws_matmul_tm(slot, wv, tk, bank, cols):
        for kc in range(8):
            S.emit("pe", I("matmul", out=PS[:, bank, 0:cols], lhsT=HT[:, kc, tk * 128:(tk + 1) * 128],
                           rhs=wv[:, kc, 0:cols], start=(kc == 0), stop=(kc == 7)),
                   reads=wkeys(slot) + [("HT", kc, tk // 2)], writes=[("PS", bank)], sig=(kc == 7))

    def rope_evac(bank, dstT, fc, keyname):
        nt = rot("ntmp", 3)
        QSv = NTMP[nt][:, 0:NST]
        RTv = NTMP[nt][:, NST:2 * NST]
        S.emit("act", I("activation", out=QSv, in_=PS[:, bank, 0:NST], func=AF.Copy),
               reads=[("PS", bank)], writes=[("NTMP", nt)])
        b2 = mmbank()
        S.emit("pe", I("matmul", out=PS[:, b2, 0:NST], lhsT=PERM[:], rhs=QSv, start=True, stop=True),
               reads=["PERM", ("NTMP", nt)], writes=[("PS", b2)], sig=True)
        S.emit("dve", I("tensor_tensor", out=RTv, in0=PS[:, b2, 0:NST], in1=SINS[:], op=ALU.mult),
               reads=[("PS", b2), "SINS"], writes=[("NTMP", nt)])
        S.emit("dve", I("tensor_tensor", out=QSv, in0=QSv, in1=COST[:], op=ALU.mult),
               reads=[("NTMP", nt), "COST"], writes=[("NTMP", nt)])
        S.emit("dve", I("tensor_tensor", out=dstT[:, fc, NPT:T], in0=QSv, in1=RTv, op=ALU.add),
               reads=[("NTMP", nt)], writes=[(keyname, fc, 4)])

    ktr_pending = []

    def do_in_piece(kind, d, pi):
        cols = 256 if kind == "vs" else 512
        slot = wslot(pi)
        wv = wview(slot, 8, cols)
        if kind in ("q", "k"):
            dstT = QT if kind == "q" else KT
            kn = "QT" if kind == "q" else "KT"
            for fc in range(4):
                for ti in range(3):
                    if kind == "k" and ti < 2:
                        continue
                    t0, n = TT[ti]
                    bank = mmbank()
                    ws_matmul_fm(slot, wv, fc, ti, bank)
                    if ti < 2:
                        S.emit("act", I("activation", out=dstT[:, fc, t0:t0 + n], in_=PS[:, bank, 0:n], func=AF.Copy),
                               reads=[("PS", bank)], writes=[(kn, fc, b) for b in blks(t0, n)])
                    else:
                        rope_evac(bank, dstT, fc, kn)
            if kind == "k":
                for tk in range(8):
                    bank = mmbank()
                    ws_matmul_tm(slot, wv, tk, bank, 512)
                    sg = rot("stg", 2)
                    S.emit("dve", I("tensor_copy", out=STG[sg][:], in_=PS[:, bank, :]),
                           reads=[("PS", bank)], writes=[("STG", sg)])
                    S.emit("sp", I("dma_start", out=nk_d[tk // 2, d, (tk % 2) * 128:(tk % 2) * 128 + 128, :], in_=STG[sg][:]),
                           reads=[("STG", sg)], writes=[], kind="d", semkey=("stgo", sg))
                    oi = rot("atto", 2)
                    KB = ATT_OB[oi]
                    S.emit("act", I("activation", out=KB[:].rearrange("p h e -> p (h e)"), in_=STG[sg][:], func=AF.Copy),
                           reads=[("STG", sg)], writes=[("ATTO", oi, u) for u in range(4)])

                    def ktr(tk=tk, oi=oi, KB=KB):
                        tb = mmbank()
                        TPV = PS[:, tb, 0:256].bitcast(BF16)
                        for h in range(4):
                            S.emit("pe", I("transpose", out=TPV[:, h * 128:(h + 1) * 128], in_=KB[:, h, :], identity=IDB[:]),
                                   reads=[("ATTO", oi, h), "IDB"], writes=[("PS", tb)], sig=(h == 3))
                        S.emit("act", I("activation", out=KT[:, :, tk * 128:(tk + 1) * 128],
                                        in_=TPV[:, 0:512].rearrange("p (h q) -> p h q", h=4), func=AF.Copy),
                               reads=[("PS", tb)], writes=[("KT", fc, tk // 2) for fc in range(4)])
                    ktr_pending.append(ktr)
                    while len(ktr_pending) > 1:
                        ktr_pending.pop(0)()
                while ktr_pending:
                    ktr_pending.pop(0)()
        elif kind == "v":
            for tk in range(10):
                bank = mmbank()
                ws_matmul_tm(slot, wv, tk, bank, 512)
                sg = rot("stg", 2)
                S.emit("dve", I("tensor_copy", out=STG[sg][:], in_=PS[:, bank, :]),
                       reads=[("PS", bank)], writes=[("STG", sg)])
                S.emit("act", I("activation", out=VA[:, tk, :, 0:128],
                                in_=STG[sg][:].rearrange("p (h e) -> p h e", h=4), func=AF.Copy),
                       reads=[("STG", sg)], writes=[("VA", tk)])
                if tk < 8:
                    S.emit("sp", I("dma_start", out=nv_d[tk // 2, d, (tk % 2) * 128:(tk % 2) * 128 + 128, :], in_=STG[sg][:]),
                           reads=[("STG", sg)], writes=[], kind="d", semkey=("stgo", sg))
        elif kind == "g":
            for ti in range(3):
                for fc in range(4):
                    t0, n = TT[ti]
                    bank = mmbank()
                    ws_matmul_fm(slot, wv, fc, ti, bank)
                    S.emit("act", I("activation", out=G[:, fc, t0:t0 + n], in_=PS[:, bank, 0:n], func=AF.Copy),
                           reads=[("PS", bank)], writes=[("G", fc, b) for b in blks(t0, n)])
        elif kind == "xu":
            for fc in range(4):
                for ti in range(3):
                    t0, n = TT[ti]
                    bank = mmbank()
                    ws_matmul_fm(slot, wv, fc, ti, bank)
                    if fc < 2:
                        S.emit("dve", I("tensor_tensor", out=ZU[:, fc, t0:t0 + n], in0=PS[:, bank, 0:n],
                                        in1=G[:, 2 + fc, t0:t0 + n], op=ALU.mult),
                               reads=[("PS", bank)] + [("G", 2 + fc, b) for b in blks(t0, n)],
                               writes=[("ZU", fc, b) for b in blks(t0, n)])
                    else:
                        S.emit("act", I("activation", out=ZU[:, fc, t0:t0 + n], in_=PS[:, bank, 0:n], func=AF.Copy),
                               reads=[("PS", bank)], writes=[("ZU", fc, b) for b in blks(t0, n)])
        elif kind == "vs":
            for tk in range(10):
                bank = mmbank()
                ws_matmul_tm(slot, wv, tk, bank, 256)
                S.emit("act", I("activation", out=VS[:, tk, :], in_=PS[:, bank, 0:256], func=AF.Copy),
                       reads=[("PS", bank)], writes=[("VS", tk)])
                if tk % 2 == 1 and tk >= 3:
                    blk = tk // 2 - 1
                    do_cmlp(d, blk)
                    do_conv(d, blk)
            do_cmlp(d, 4)

    def flat_ap(t, off, dims):
        return bass.AP(t, off, [list(x) for x in dims])

    def do_exchange(d):
        bt = bounce_t[d]
        gt = gath_t[d]
        S.emit("dve", I("tensor_copy", out=ZB[:, :, 0], in_=ZU[:, 0:2, NPT]),
               reads=[("ZU", 0, 4), ("ZU", 1, 4)], writes=[("ZB", 0)])
        S.emit("dve", I("tensor_copy", out=ZB[:, :, 1], in_=ZU[:, 0:2, T - 1]),
               reads=[("ZU", 0, 4), ("ZU", 1, 4)], writes=[("ZB", 1)])
        S.emit("sp", I("dma_start", out=flat_ap(bt, 0, [[256, 128], [128 * 256, 4], [1, 256]]), in_=KT[:, :, NPT:T]),
               reads=[("KT", fc, 4) for fc in range(4)], writes=[("BOUNCE", d, "k")], kind="d", semkey="bk")
        for jj in range(2):
            S.emit("sp", I("dma_start", out=flat_ap(bt, 131072 + jj * 65536, [[512, 128], [128, 4], [1, 128]]),
                           in_=VA[:, 8 + jj, :, 0:128]),
                   reads=[("VA", 8 + jj)], writes=[("BOUNCE", d, "v", jj)], kind="d", semkey=("bv", jj))
        S.emit("sp", I("dma_start", out=flat_ap(bt, 262144, [[4, 128], [1, 4]]), in_=ZB[:].rearrange("p j e -> p (j e)")),
               reads=[("ZB", 0), ("ZB", 1)], writes=[("BOUNCE", d, "h")], kind="d", semkey="bh")
        S.emit("pool", I("collective_compute", kind="AllGather", op=ALU.bypass,
                         replica_groups=[[0, 1, 2, 3], [4, 5, 6, 7]], ins=[bt.ap()], outs=[gt.ap()]),
               reads=[("BOUNCE", d, "k"), ("BOUNCE", d, "v", 0), ("BOUNCE", d, "v", 1), ("BOUNCE", d, "h")], writes=[("GATH", d)], kind="cc")
        S.emit("sp", I("dma_start", out=HB[:].rearrange("p r j e -> p r (j e)"),
                       in_=flat_ap(gt, 262144, [[4, 128], [BROWS * 512, 4], [1, 4]])),
               reads=[("GATH", d)], writes=["HB"], kind="d", semkey="hb")

    def do_halo(d):
        S.emit("dve", I("tensor_copy", out=HBF[:], in_=HB[:]), reads=["HB"], writes=["HBF"])
        for w, esrc in ((0, 1), (1, 0)):
            for j in range(2):
                S.emit("dve", I("tensor_tensor", out=HTMP[:, w, j, :], in0=HBF[:, :, j, esrc], in1=SEL[:, w, :], op=ALU.mult),
                       reads=["HBF", "SEL"], writes=[("HTMP", w, j)])
        S.emit("dve", I("tensor_reduce", out=HALO[:], in_=HTMP[:], axis=AX.X, op=ALU.add),
               reads=[("HTMP", w, j) for w in range(2) for j in range(2)], writes=["HALO"])

    KALLK = ["KALLc"] + [("KALLg", r) for r in range(4)]
    VALLK = [("VALLc", kk) for kk in range(4)] + ["VALL1"] + [("VALLg", r, jj) for r in range(4) for jj in range(2)]

    def do_load_sample_keys(d):
        gt = gath_t[d]
        S.emit("pool", I("dma_start", out=KALL[:, :, 0:PAST], in_=ckT_d[d]),
               reads=[], writes=["KALLc"], kind="d", semkey="ck")
        for kk in range(4):
            S.emit("pool", I("dma_start", out=VALL[:, kk, :, 0:128],
                             in_=cv_d[d, kk * 128:(kk + 1) * 128, :].rearrange("p (h e) -> p h e", h=4)),
                   reads=[], writes=[("VALLc", kk)], kind="d", semkey=("cvv", kk))
        S.emit("dve", I("memset", ap=VALL[:, :, :, 128:129], constant=1.0), reads=[], writes=["VALL1"])
        for r in range(4):
            S.emit("sp", I("dma_start", out=KALL[:, :, PAST + r * 256:PAST + (r + 1) * 256],
                           in_=flat_ap(gt, r * BROWS * 512, [[256, 128], [128 * 256, 4], [1, 256]])),
                   reads=[("GATH", d)], writes=[("KALLg", r)], kind="d", semkey=("gk", r))
            for jj in range(2):
                S.emit("sp", I("dma_start", out=VALL[:, 4 + 2 * r + jj, :, 0:128],
                               in_=flat_ap(gt, r * BROWS * 512 + 131072 + jj * 65536, [[512, 128], [128, 4], [1, 128]])),
                       reads=[("GATH", d)], writes=[("VALLg", r, jj)], kind="d", semkey=("gv", r, jj))

    ETALL = rview(33360, 6144)

    def oblock(k):
        return 5 + k // 3, (k % 3) * 129

    def do_attention(d, qb, sample, filler=None):
        nkt = 12 if sample else 2
        q0 = qb * 256
        ebase = {}
        ekey = {}
        for h in range(4):
            if sample:
                ebase[h] = (0, 3072)
                ekey[h] = ([("ETS", i) for i in range(0, 3)], [("ETS", i) for i in range(3, 6)])
            else:
                sl = rot("ets", 6)
                ebase[h] = (sl * 1024, sl * 1024 + 512)
                ekey[h] = ([("ETS", sl)], [("ETS", sl)])

        def scores(h):
            for j in range(nkt // 2):
                spair = ((0, 1), (2, 3))[rot("sb", 2)]
                for kk in range(2):
                    kt = 2 * j + kk
                    for c in range(2):
                        if sample:
                            lhsT = KALL[c * 64:(c + 1) * 64, h, kt * 128:(kt + 1) * 128]
                            rk = list(KALLK)
                        else:
                            lhsT = KT[c * 64:(c + 1) * 64, h, q0 + kt * 128:q0 + (kt + 1) * 128]
                            rk = [("KT", h, qb)]
                        S.emit("pe", I("matmul", out=PS[:, spair[c], kk * 256:(kk + 1) * 256], lhsT=lhsT,
                                       rhs=QT[c * 64:(c + 1) * 64, h, q0:q0 + 256], start=True, stop=True),
                               reads=rk + [("QT", h, qb)], writes=[("PS", spair[c])], sig=(kk == 1))
                for c in range(2):
                    o0 = ebase[h][c] + j * 512
                    S.emit("act", I("activation", out=ETALL[:, o0:o0 + 512], in_=PS[:, spair[c], :],
                                    func=AF.Exp, scale=0.125),
                           reads=[("PS", spair[c])], writes=ekey[h][c])

        def batch(units):
            U = len(units)
            oi = rot("atto", 2)
            ATT_O = ATT_OB[oi]
            obanks = sorted(set(oblock(k)[0] for k in range(2 * U)))
            okeys = [("PS", bk) for bk in obanks]
            order = [(u, c) for c in range(2) for u in range(U)] if sample else [(u, c) for u in range(U) for c in range(2)]
            for (u, c) in order:
                h, qt = units[u]
                if True:
                    bk, col = oblock(u * 2 + c)
                    for kt in range(nkt):
                        if sample:
                            rhs = VALL[:, kt, h, 0:129]
                            rk = list(VALLK)
                        else:
                            rhs = VA[:, qb * 2 + kt, h, 0:129]
                            rk = [("VA", qb * 2 + kt), "VA1"]
                        e0 = ebase[h][c] + kt * 256 + qt * 128
                        last = (kt == nkt - 1)
                        S.emit("pe", I("matmul", out=PS[:, bk, col:col + 129], lhsT=ETALL[:, e0:e0 + 128], rhs=rhs,
                                       start=(kt == 0), stop=last),
                               reads=rk + ekey[h][c], writes=[("PS", bk)], sig=last)
            sm = ATT_S
            for bk in obanks:
                ks = [k for k in range(2 * U) if oblock(k)[0] == bk]
                n = len(ks)
                S.emit("dve", I("reciprocal", out=sm[:, ks[0]:ks[0] + n],
                                in_=PS[:, bk, 0:n * 129].rearrange("p (k e) -> p k e", e=129)[:, :, 128]),
                       reads=[("PS", bk)], writes=[("ATTS", "r", bk)])
            rkeys = [("ATTS", "r", bk) for bk in obanks]
            S.emit("dve", I("tensor_scalar", out=sm[:, 8:8 + U], in0=sm[:, 1:2 * U:2], scalar1=LAM[:, d, 0:1],
                            scalar2=None, op0=ALU.mult),
                   reads=rkeys + [("LAM", d, 0)], writes=[("ATTS", "n")])
            for u in range(U):
                bk, col = oblock(u * 2)
                S.emit("dve", I("tensor_scalar", out=ATT_A[:, u, :], in0=PS[:, bk, col:col + 128],
                                scalar1=sm[:, 2 * u:2 * u + 1], scalar2=None, op0=ALU.mult),
                       reads=[("PS", bk)] + rkeys, writes=[("ATTA", u)])
            for u in range(U):
                bk, col = oblock(u * 2 + 1)
                S.emit("dve", I("scalar_tensor_tensor", out=ATT_A[:, u, :], in0=PS[:, bk, col:col + 128],
                                scalar=sm[:, 8 + u:9 + u], in1=ATT_A[:, u, :], op0=ALU.mult, op1=ALU.add),
                       reads=[("PS", bk), ("ATTS", "n"), ("ATTA", u)], writes=[("ATTA", u)])
            akeys = [("ATTA", u) for u in range(U)]
            jn = rot("ntmp", 3)
            ATT_J = NTMP[jn]
            S.emit("dve", I("tensor_tensor", out=ATT_J[:, 0:U * 128], in0=ATT_A[:, 0:U, :].rearrange("p u e -> p (u e)"),
                            in1=ATT_A[:, 0:U, :].rearrange("p u e -> p (u e)"), op=ALU.mult),
                   reads=akeys, writes=[("NTMP", jn)])
            S.emit("dve", I("tensor_reduce", out=sm[:, 16:16 + U], in_=ATT_J[:, 0:U * 128].rearrange("p (u e) -> p u e", u=U),
                            axis=AX.X, op=ALU.add),
                   reads=[("NTMP", jn)], writes=[("ATTS", "q")])
            S.emit("act", I("activation", out=sm[:, 20:20 + U], in_=sm[:, 16:16 + U], func=AF.Ln, bias=EPSB[:, 0:1],
                            scale=1.0 / 128.0),
                   reads=[("ATTS", "q"), "EPSB"], writes=[("ATTS", "l")])
            S.emit("act", I("activation", out=sm[:, 24:24 + U], in_=sm[:, 20:20 + U], func=AF.Exp, scale=-0.5),
                   reads=[("ATTS", "l")], writes=[("ATTS", "s")])
            for u in range(U):
                S.emit("dve", I("scalar_tensor_tensor", out=ATT_O[:, u, :], in0=ATT_A[:, u, :], scalar=sm[:, 24 + u:25 + u],
                                in1=GSUB[:], op0=ALU.mult, op1=ALU.mult),
                       reads=[("ATTA", u), ("ATTS", "s"), "GSUB"], writes=[("ATTO", oi, u)])
            def fin():
                TPV = PS[:, 4, 0:256].bitcast(BF16)
                for u in range(U):
                    S.emit("pe", I("transpose", out=TPV[:, u * 128:(u + 1) * 128], in_=ATT_O[:, u, :], identity=IDB[:]),
                           reads=[("ATTO", oi, u), "IDB"], writes=[("PS", 4)], sig=(u == U - 1))
                if sample:
                    hh = units[0][0]
                    S.emit("act", I("activation", out=CAT[:, hh, q0:q0 + 256], in_=TPV[:, 0:256], func=AF.Copy),
                           reads=[("PS", 4)], writes=[("HT", hh, qb)])
                else:
                    qt0 = units[0][1]
                    t0 = q0 + qt0 * 128
                    S.emit("act", I("activation", out=CAT[:, 0:4, t0:t0 + 128],
                                    in_=TPV[:, 0:512].rearrange("p (h q) -> p h q", h=4), func=AF.Copy),
                           reads=[("PS", 4)], writes=[("HT", hh, qb) for hh in range(4)])
            return fin

        def push(fin):
            att_pending.append(fin)
            while len(att_pending) > 1:
                att_pending.pop(0)()

        if sample:
            for h in range(4):
                scores(h)
                push(batch([(h, 0), (h, 1)]))
                if filler is not None:
                    filler()
        else:
            for h in range(4):
                scores(h)
            for qt in range(2):
                push(batch([(h, qt) for h in range(4)]))
                if filler is not None:
                    filler()

    att_pending = []

    def att_flush():
        while att_pending:
            att_pending.pop(0)()

    def do_conv_prompt(d):
        for j in range(2):
            Z3 = ZU[:, j, 0:NPT].rearrange("p (s t) -> p s t", s=4)
            C3 = CAT[:, 4 + j, 0:NPT].rearrange("p (s t) -> p s t", s=4)
            zr = [("ZU", j, b) for b in range(4)]
            ck = [("HT", 4 + j, b) for b in range(4)]
            S.emit("dve", I("tensor_scalar", out=CONVY[:, 0:NPT], in0=ZU[:, j, 0:NPT], scalar1=CW[:, d, j, 1:2],
                            scalar2=CW[:, d, j, 3:4], op0=ALU.mult, op1=ALU.add),
                   reads=zr + ["CW"], writes=["CONVY", ("RSTD", 0), ("RSTD", 1)])
            Y3 = CONVY[:, 0:NPT].rearrange("p (s t) -> p s t", s=4)
            S.emit("dve", I("scalar_tensor_tensor", out=Y3[:, :, 1:256], in0=Z3[:, :, 0:255], scalar=CW[:, d, j, 0:1],
                            in1=Y3[:, :, 1:256], op0=ALU.mult, op1=ALU.add),
                   reads=zr + ["CW", "CONVY"], writes=["CONVY"])
            S.emit("dve", I("scalar_tensor_tensor", out=Y3[:, :, 0:255], in0=Z3[:, :, 1:256], scalar=CW[:, d, j, 2:3],
                            in1=Y3[:, :, 0:255], op0=ALU.mult, op1=ALU.add),
                   reads=zr + ["CW", "CONVY"], writes=["CONVY"])
            S.emit("dve", I("tensor_tensor", out=CAT[:, 4 + j, 0:NPT], in0=CONVY[:, 0:NPT], in1=G[:, j, 0:NPT], op=ALU.mult),
                   reads=["CONVY"] + [("G", j, b) for b in range(4)], writes=ck)

    def do_conv(d, blk):
        t0 = blk * 256
        sample = blk == 4
        for j in range(2):
            ci = rot("ntmp", 3)
            Y = NTMP[ci][:, 0:256]
            zr = [("ZU", j, blk)]
            S.emit("dve", I("tensor_scalar", out=Y[:], in0=ZU[:, j, t0:t0 + 256], scalar1=CW[:, d, j, 1:2],
                            scalar2=CW[:, d, j, 3:4], op0=ALU.mult, op1=ALU.add),
                   reads=zr + ["CW"], writes=[("NTMP", ci)])
            S.emit("dve", I("scalar_tensor_tensor", out=Y[:, 1:256], in0=ZU[:, j, t0:t0 + 255], scalar=CW[:, d, j, 0:1],
                            in1=Y[:, 1:256], op0=ALU.mult, op1=ALU.add),
                   reads=zr + ["CW", ("NTMP", ci)], writes=[("NTMP", ci)])
            S.emit("dve", I("scalar_tensor_tensor", out=Y[:, 0:255], in0=ZU[:, j, t0 + 1:t0 + 256], scalar=CW[:, d, j, 2:3],
                            in1=Y[:, 0:255], op0=ALU.mult, op1=ALU.add),
                   reads=zr + ["CW", ("NTMP", ci)], writes=[("NTMP", ci)])
            if sample:
                S.emit("dve", I("scalar_tensor_tensor", out=Y[:, 0:1], in0=HALO[:, 0, j:j + 1], scalar=CW[:, d, j, 0:1],
                                in1=Y[:, 0:1], op0=ALU.mult, op1=ALU.add),
                       reads=["HALO", "CW", ("NTMP", ci)], writes=[("NTMP", ci)])
                S.emit("dve", I("scalar_tensor_tensor", out=Y[:, 255:256], in0=HALO[:, 1, j:j + 1], scalar=CW[:, d, j, 2:3],
                                in1=Y[:, 255:256], op0=ALU.mult, op1=ALU.add),
                       reads=["HALO", "CW", ("NTMP", ci)], writes=[("NTMP", ci)])
            S.emit("dve", I("tensor_tensor", out=CAT[:, 4 + j, t0:t0 + 256], in0=Y[:], in1=G[:, j, t0:t0 + 256], op=ALU.mult),
                   reads=[("NTMP", ci), ("G", j, blk)], writes=[("HT", 4 + j, blk)])

    def do_cmlp(d, blk):
        for half in range(2):
            tk = blk * 2 + half
            t0 = tk * 128
            for pair in range(2):
                bank = mmbank()
                S.emit("pe", I("matmul", out=PS[:, bank, 0:256], lhsT=VS[:, tk, pair * 128:(pair + 1) * 128],
                               rhs=WST[:, d, pair * 2:pair * 2 + 2, :].rearrange("q g p -> q (g p)"),
                               start=True, stop=True),
                       reads=[("VS", tk), "WST"], writes=[("PS", bank)], sig=True)
                ci = rot("ntmp", 3)
                for gi in range(2):
                    S.emit("dve", I("tensor_tensor", out=NTMP[ci][gi * 64:gi * 64 + 64, 0:128],
                                    in0=PS[gi * 64:gi * 64 + 64, bank, gi * 128:(gi + 1) * 128],
                                    in1=BSB[gi * 64:gi * 64 + 64, d, pair, :], op=ALU.add),
                           reads=[("PS", bank), "BSB"], writes=[("NTMP", ci, gi)])
                S.emit("dve", I("tensor_tensor", out=CAT[:, 6 + pair, t0:t0 + 128], in0=NTMP[ci][:, 0:128],
                                in1=ZU[:, 2 + pair, t0:t0 + 128], op=ALU.mult),
                       reads=[("NTMP", ci, 0), ("NTMP", ci, 1), ("ZU", 2 + pair, blk)], writes=[("HT", 6 + pair, blk)])

    def resid_evac(bank, oc, ti, d, which, rng=None):
        t0, n = TT[ti] if rng is None else rng
        v = 0 if t0 < NPT else 1
        bl = blks(t0, n)
        S.emit("dve", I("scalar_tensor_tensor", out=X[:, oc, t0:t0 + n], in0=PS[:, bank, 0:n],
                        scalar=mod_ap(d, which, oc, v), in1=X[:, oc, t0:t0 + n], op0=ALU.mult, op1=ALU.add),
               reads=[("PS", bank)] + mod_key(d, which) + [("X", oc, b) for b in bl], writes=[("X", oc, b) for b in bl])

    def out_groups(d, pis, rng):
        slots = [wslot(pis[0], la=NSLOT - 1), wslot(pis[1], la=NSLOT - 2)]
        res = []
        for j in range(2):
            wv = wview(slots[j], 8, 512)
            for fc in range(4):
                def g(j=j, fc=fc, wv=wv):
                    oc = j * 4 + fc
                    t0, n = rng
                    bank = mmbank()
                    for kc in range(8):
                        S.emit("pe", I("matmul", out=PS[:, bank, 0:n], lhsT=wv[:, kc, fc * 128:(fc + 1) * 128],
                                       rhs=CAT[:, kc, t0:t0 + n], start=(kc == 0), stop=(kc == 7)),
                               reads=wkeys(slots[j]) + [("HT", kc, b) for b in blks(t0, n)], writes=[("PS", bank)],
                               sig=(kc == 7))
                    resid_evac(bank, oc, 0, d, 2, rng=rng)
                res.append(g)
        return res

    def do_out_piece(d, j, pi, tis=(0, 1, 2)):
        slot = wslot(pi, la=NSLOT - 1 - j)
        wv = wview(slot, 8, 512)
        for ti in tis:
            for fc in range(4):
                oc = j * 4 + fc
                t0, n = TT[ti]
                bank = mmbank()
                for kc in range(8):
                    S.emit("pe", I("matmul", out=PS[:, bank, 0:n], lhsT=wv[:, kc, fc * 128:(fc + 1) * 128],
                                   rhs=CAT[:, kc, t0:t0 + n], start=(kc == 0), stop=(kc == 7)),
                           reads=wkeys(slot) + [("HT", kc, b) for b in blks(t0, n)], writes=[("PS", bank)], sig=(kc == 7))
                resid_evac(bank, oc, ti, d, 2)

    def do_up_piece(d, j, pi):
        slot = wslot(pi)
        wv = wview(slot, 8, 512)
        for ti in range(3):
            for fc in range(4):
                hc = j * 4 + fc
                t0, n = TT[ti]
                bank = mmbank()
                ws_matmul_fm(slot, wv, fc, ti, bank)
                nt = rot("ntmp", 3)
                S.emit("act", I("activation", out=NTMP[nt][:, 0:n], in_=PS[:, bank, 0:n], func=AF.Relu),
                       reads=[("PS", bank)], writes=[("NTMP", nt)])
                S.emit("dve", I("tensor_tensor", out=HID[:, hc, t0:t0 + n], in0=NTMP[nt][:, 0:n], in1=NTMP[nt][:, 0:n],
                                op=ALU.mult),
                       reads=[("NTMP", nt)], writes=[("HID", hc, b) for b in blks(t0, n)])

    stat_pending = []

    def do_down_piece(d, j, pi):
        slot = wslot(pi)
        wv = wview(slot, 32, 128)
        for ti in range(3):
            t0, n = TT[ti]
            bank = mmbank()
            for kc in range(32):
                S.emit("pe", I("matmul", out=PS[:, bank, 0:n], lhsT=wv[:, kc, :], rhs=HID[:, kc, t0:t0 + n],
                               start=(kc == 0), stop=(kc == 31)),
                       reads=wkeys(slot) + [("HID", kc, b) for b in blks(t0, n)], writes=[("PS", bank)], sig=(kc == 31))
            resid_evac(bank, j, ti, d, 5)
            stat_pending.append((j, ti))
            while len(stat_pending) > (0 if (j == 7 and ti == 2) else 2):
                stat_accum(*stat_pending.pop(0))

    REGION_MIX = ([("QT", c, b) for c in range(4) for b in range(5)] + [("KT", c, b) for c in range(4) for b in range(5)]
                  + [("VA", k) for k in range(10)] + ["VA1"] + [("G", c, b) for c in range(4) for b in range(5)]
                  + [("ZU", c, b) for c in range(4) for b in range(5)] + [("VS", k) for k in range(10)]
                  + [("ETS", i) for i in range(6)] + KALLK + VALLK)
    REGION_HID = [("HID", c, b) for c in range(32) for b in range(5)]
    GZ_KEYS = [("G", c, b) for c in range(4) for b in range(5)] + [("ZU", c, b) for c in range(4) for b in range(5)]

    def fence(keys):
        S.emit("dve", I("memset", ap=FEN[:, 0:1], constant=0.0), reads=[], writes=list(keys))

    for (kind, d, j, pi) in plan:
        if kind == "mod":
            do_mod_piece(d, j, pi)
            continue
        if kind == "in_g":
            load(GSUB[:], gsub_d[:, d, :], "GSUB")
            S.emit("dve", I("tensor_scalar", out=GSUB[:], in0=GSUB[:], scalar1=(1.0 - lam_inits[d]),
                            scalar2=None, op0=ALU.mult),
                   reads=["GSUB"], writes=["GSUB"])
            do_scale_prep(d, 1, GMIX, "GMIX", S1, "S1")
            fence(REGION_MIX + REGION_HID)
            do_norm(d, S1, "S1", 0, have_stats=(d > 0))
            S.emit("dve", I("memset", ap=VA[:, :, :, 128:129], constant=1.0), reads=[], writes=["VA1"])
        if kind.startswith("in_"):
            do_in_piece(kind[3:], d, pi)
            if kind == "in_v":
                do_exchange(d)
            continue
        if kind == "mix":
            continue
        if kind == "out":
            if j == 0:
                out_pis = [pi]
                continue
            out_pis.append(pi)
            gb = [out_groups(d, out_pis, (b * 256, 256)) for b in range(5)]

            def fill_from(lst, k):
                def f():
                    for _ in range(k):
                        if lst:
                            lst.pop(0)()
                return f

            sched = [None, (0, 4), (0, 4), (1, 4), (1, 4), (2, 4), (2, 4), (3, 3), (3, 3), (3, 2)]
            fidx = [0]

            def filler():
                e = sched[fidx[0]] if fidx[0] < len(sched) else None
                fidx[0] += 1
                if e is not None:
                    fill_from(gb[e[0]], e[1])()

            do_attention(d, 0, False)
            do_attention(d, 1, False, filler=filler)
            do_halo(d)
            do_conv(d, 4)
            fence(GZ_KEYS + KALLK + VALLK)
            do_load_sample_keys(d)
            do_attention(d, 2, False, filler=filler)
            do_scale_prep(d, 4, GMLP, "GMLP", S2, "S2")
            do_attention(d, 3, False, filler=filler)
            att_flush()
            fill_from(gb[0], 8)()
            fill_from(gb[1], 8)()
            do_norm(d, S2, "S2", 3, tis=(0,))
            do_attention(d, 4, True, filler=filler)
            for b_ in range(4):
                fill_from(gb[b_], 8)()
            att_flush()
            fill_from(gb[4], 8)()
            do_norm(d, S2, "S2", 3, tis=(1,))
            do_norm(d, S2, "S2", 3, tis=(2,))
            fence(REGION_MIX + REGION_HID)
            continue
        if kind == "up":
            do_up_piece(d, j, pi)
            continue
        if kind == "down":
            do_down_piece(d, j, pi)
            continue

    for ti, (t0, n) in enumerate(TT):
        bank = 5 + ti
        bl = blks(t0, n)
        rs = rot("rstd", 2)
        S.emit("act", I("activation", out=RSTD[rs][:, 0:n], in_=PS[:, bank, 0:n], func=AF.Ln, bias=EPSB[:, 0:1], scale=1.0),
               reads=[("PS", bank), "EPSB"], writes=[("RSTD", rs)])
        S.emit("act", I("activation", out=RSTD[rs][:, 0:n], in_=RSTD[rs][:, 0:n], func=AF.Exp, scale=-0.5),
               reads=[("RSTD", rs)], writes=[("RSTD", rs)])
        for fc in range(8):
            S.emit("dve", I("scalar_tensor_tensor", out=X[:, fc, t0:t0 + n], in0=X[:, fc, t0:t0 + n],
                            scalar=GFIN[:, fc:fc + 1], in1=RSTD[rs][:, 0:n], op0=ALU.mult, op1=ALU.mult),
                   reads=[("X", fc, b) for b in bl] + ["GFIN", ("RSTD", rs)], writes=[("X", fc, b) for b in bl])
    for fc in range(8):
        S.emit("sp", I("dma_start", out=yT_d[:, fc, :], in_=X[:, fc, :]),
               reads=[("X", fc, b) for b in range(5)], writes=[], kind="d", semkey=("yo", fc))

    S.run(nc)
    st.close()
    return nc


def _rope_tables_np(pos0, n):
    t = np.arange(pos0, pos0 + n)
    row = (t // 64).astype(np.float64)
    col = (t % 64).astype(np.float64)
    inv = 10000.0 ** (-np.arange(0, 32, 2, dtype=np.float64) / 32.0)
    ar = row[:, None] * inv[None, :]
    ac = col[:, None] * inv[None, :]
    cosT = np.zeros((128, n), np.float32)
    sinS = np.zeros((128, n), np.float32)
    for p in range(128):
        e = p % 64
        ax = e // 32
        i = e % 32
        first = i < 16
        ang = (ar if ax == 0 else ac)[:, i % 16]
        cosT[p] = np.cos(ang)
        sinS[p] = (-np.sin(ang)) if first else np.sin(ang)
    return cosT, sinS


def _perm_np():
    P = np.zeros((128, 128), np.float32)
    for m in range(128):
        i = m % 32
        partner = m + 16 if i < 16 else m - 16
        P[partner, m] = 1.0
    return P


_NC_CACHE = {}


def _prepare(x_prompt, x_sample, cache_k, cache_v, c, c_ctx, w_mod, b_mod, norm_mix, norm_mlp,
             w_in, lam_q1, lam_k1, lam_q2, lam_k2, subln, conv_w, conv_b, w_s, b_s,
             w_out, w_up, w_down, norm_final):
    f = lambda a: np.ascontiguousarray(np.asarray(a, dtype=np.float32))
    x_prompt, x_sample, cache_k, cache_v, c, c_ctx = map(f, (x_prompt, x_sample, cache_k, cache_v, c, c_ctx))
    w_mod, b_mod, norm_mix, norm_mlp, w_in = map(f, (w_mod, b_mod, norm_mix, norm_mlp, w_in))
    lam_q1, lam_k1, lam_q2, lam_k2, subln = map(f, (lam_q1, lam_k1, lam_q2, lam_k2, subln))
    conv_w, conv_b, w_s, b_s, w_out, w_up, w_down, norm_final = map(
        f, (conv_w, conv_b, w_s, b_s, w_out, w_up, w_down, norm_final))
    depth = w_in.shape[0]
    bmod = np.ascontiguousarray(b_mod.reshape(depth, 48, 128).transpose(2, 0, 1))
    gmix = np.ascontiguousarray(norm_mix.reshape(depth, 8, 128).transpose(2, 0, 1))
    gmlp = np.ascontiguousarray(norm_mlp.reshape(depth, 8, 128).transpose(2, 0, 1))
    gfin = np.ascontiguousarray(norm_final.reshape(8, 128).T)
    lamv = np.ascontiguousarray(np.broadcast_to(np.stack([lam_q1, lam_k1, lam_q2, lam_k2], axis=1)[None], (128, depth, 4, 64)))
    gsub = np.ascontiguousarray(np.broadcast_to(subln[None], (128, depth, 128)))
    cw = np.zeros((128, depth, 2, 4), np.float32)
    cw[:, :, :, 0:3] = conv_w.reshape(depth, 3, 2, 128).transpose(3, 0, 2, 1)
    cw[:, :, :, 3] = conv_b.reshape(depth, 2, 128).transpose(2, 0, 1)
    wsT = np.ascontiguousarray(w_s.transpose(3, 0, 1, 2))
    bsB = np.zeros((128, depth, 2, 128), np.float32)
    for pair in range(2):
        for gi in range(2):
            bsB[gi * 64:(gi + 1) * 64, :, pair, :] = b_s[:, pair * 2 + gi, :][None]
    perm = _perm_np()
    ident = np.eye(128, dtype=np.float32)
    in_maps = []
    for core in range(8):
        s = core // 4
        r = core % 4
        xtok = np.concatenate([x_prompt[4 * core:4 * core + 4].reshape(1024, D),
                               x_sample[s, r * 256:(r + 1) * 256]], axis=0)
        xT = np.ascontiguousarray(xtok.reshape(T, 8, 128).transpose(2, 1, 0))
        ckT = np.ascontiguousarray(cache_k[s, :depth].transpose(0, 3, 2, 1))
        cv = np.ascontiguousarray(cache_v[s, :depth].reshape(depth, PAST, 512))
        cvec = np.ascontiguousarray(np.stack([c_ctx, c[s]], axis=-1).reshape(8, 128, 2).transpose(1, 0, 2))
        cosT, sinS = _rope_tables_np(r * 256, 256)
        sel = np.zeros((128, 2, 4), np.float32)
        if r > 0:
            sel[:, 0, r - 1] = 1.0
        if r < 3:
            sel[:, 1, r + 1] = 1.0
        in_maps.append({
            "xT": xT, "ckT": ckT, "cv": cv, "cvec": cvec,
            "w_mod": w_mod, "w_in": w_in, "w_out": w_out, "w_up": w_up, "w_down": w_down,
            "bmod": bmod, "gmix": gmix, "gmlp": gmlp, "gfin": gfin, "lamv": lamv, "gsub": gsub,
            "cw": cw, "wsT": wsT, "bsB": bsB, "cosT": cosT, "sinS": sinS, "perm": perm, "ident": ident,
            "sel": sel,
        })
    return depth, in_maps


def _assemble(outs, depth):
    y_prompt = np.zeros((32, 256, D), np.float32)
    y_sample = np.zeros((2, 1024, D), np.float32)
    new_k = np.zeros((32, depth, 256, 4, 128), np.float32)
    new_v = np.zeros((32, depth, 256, 4, 128), np.float32)
    for core in range(8):
        s = core // 4
        r = core % 4
        ytok = np.asarray(outs[core]["yT"]).transpose(2, 1, 0).reshape(T, D)
        y_prompt[4 * core:4 * core + 4] = ytok[:1024].reshape(4, 256, D)
        y_sample[s, r * 256:(r + 1) * 256] = ytok[1024:]
        new_k[4 * core:4 * core + 4] = np.asarray(outs[core]["nk"]).reshape(4, depth, 256, 4, 128)
        new_v[4 * core:4 * core + 4] = np.asarray(outs[core]["nv"]).reshape(4, depth, 256, 4, 128)
    return (y_prompt, y_sample, new_k, new_v)


def kernel(**inputs):
    depth, in_maps = _prepare(**inputs)
    if depth not in _NC_CACHE:
        _NC_CACHE[depth] = build_nc(depth)
    nc = _NC_CACHE[depth]
    res = run_bass_kernel_spmd(nc, in_maps, core_ids=list(range(8)))
    return _assemble(res.results, depth)
```

```python
import math
import numpy as np
import concourse.bass as bass
import concourse.mybir as mybir
from concourse.bass_utils import run_bass_kernel_spmd

F32 = mybir.dt.float32
BF16 = mybir.dt.bfloat16
AF = mybir.ActivationFunctionType
ALU = mybir.AluOpType
AX = mybir.AxisListType

D = 1024
DEPTH = 4
T = 1280
NPT = 1024
NST = 256
PAST = 512
NKS = PAST + 1024
EPS = 1e-6
TT = [(0, 512), (512, 512), (1024, 256)]
NSLOT = 4
SLOT_ELEMS = 4096
BROWS = 513


def blks(a, n):
    return list(range(a // 256, (a + n + 255) // 256))


class Op:
    __slots__ = ("q", "fn", "deps", "kind", "sig", "sem", "val", "inc", "idx")


class Sched:
    QS = ("pe", "act", "dve", "pool", "sp")

    def __init__(self):
        self.ops = {q: [] for q in self.QS}
        self.res = {}
        self.dma_keys = {}
        self.cc_count = 0

    def emit(self, q, fn, reads=(), writes=(), kind="c", sig=True, semkey=None):
        op = Op()
        op.q = q
        op.fn = fn
        op.kind = kind
        op.sig = sig
        op.sem = None
        op.val = None
        op.inc = 1
        deps = {}
        for r in reads:
            e = self.res.get(r)
            if e is not None and e[0] is not None:
                deps[id(e[0])] = (e[0], True)
        for w in writes:
            e = self.res.get(w)
            if e is not None:
                if e[0] is not None and id(e[0]) not in deps:
                    deps[id(e[0])] = (e[0], False)
                for rd in e[1]:
                    if id(rd) not in deps:
                        deps[id(rd)] = (rd, False)
        dl = []
        for dep, raw in deps.values():
            if dep is op:
                continue
            if dep.q == q and dep.kind == "c" and kind == "c":
                if q == "pe":
                    continue
            dl.append(dep)
        if kind == "d":
            ent = self.dma_keys.setdefault(semkey, [0, None])
            if ent[1] is not None:
                dl.append(ent[1])
            ent[0] += 1
            ent[1] = op
            op.sem = ("dma", semkey)
            op.val = ent[0] * 16
        elif kind == "cc":
            self.cc_count += 1
            op.sem = ("cc", 0)
            op.val = self.cc_count
        op.deps = dl
        for r in reads:
            e = self.res.setdefault(r, [None, []])
            e[1].append(op)
        for w in writes:
            self.res[w] = [op, []]
        op.idx = len(self.ops[q])
        self.ops[q].append(op)
        return op

    def finalize(self):
        for q in self.QS:
            ops = [o for o in self.ops[q] if o.kind == "c"]
            if ops:
                ops[-1].sig = True
            cnt = 0
            for o in ops:
                if o.sig:
                    cnt += 1
                    o.val = cnt
                o.sem = ("q", q)
            nxt = None
            for o in reversed(ops):
                if o.sig:
                    nxt = o.val
                else:
                    o.val = nxt

    def run(self, nc):
        self.finalize()
        names = [("q", q) for q in self.QS] + [("dma", k) for k in self.dma_keys] + [("cc", 0)]
        from contextlib import ExitStack
        with ExitStack() as st:
            sems = {}
            for i, n in enumerate(names):
                sems[n] = st.enter_context(nc.semaphore("s%d" % i))
            block = st.enter_context(nc.Block())
            handles = {"pe": block.tensor, "act": block.scalar, "dve": block.vector,
                       "pool": block.gpsimd, "sp": block.sync}
            for q in self.QS:
                ops = self.ops[q]

                def body(eng, ops=ops, q=q):
                    waited = {}
                    for o in ops:
                        for dpn in o.deps:
                            if waited.get(dpn.sem, 0) < dpn.val:
                                eng.wait_ge(sems[dpn.sem], dpn.val)
                                waited[dpn.sem] = dpn.val
                        ins = o.fn(eng)
                        if o.kind == "c":
                            if o.sig:
                                ins.then_inc(sems[o.sem], 1)
                        elif o.kind == "d":
                            ins.then_inc(sems[o.sem], 16)
                        else:
                            ins.then_inc(sems[o.sem], 1)
                    if q == "sp":
                        for k, ent in self.dma_keys.items():
                            if waited.get(("dma", k), 0) < ent[0] * 16:
                                eng.wait_ge(sems[("dma", k)], ent[0] * 16)
                        for qq in self.QS:
                            cops = [o for o in self.ops[qq] if o.kind == "c"]
                            if cops:
                                eng.wait_ge(sems[("q", qq)], cops[-1].val)

                handles[q](body)


def I(method, **kw):
    return lambda e: getattr(e, method)(**kw)


def build_nc(depth=DEPTH):
    nc = bass.Bass("TRN2", target_bir_lowering=False)
    S = Sched()

    def din(name, shape, dt=F32):
        return nc.dram_tensor(name, list(shape), dt, kind="ExternalInput").ap()

    xT_d = din("xT", [128, 8, T])
    ckT_d = din("ckT", [depth, 128, 4, PAST])
    cv_d = din("cv", [depth, PAST, 512])
    cvec_d = din("cvec", [128, 8, 2])
    w_mod_d = din("w_mod", [depth, D, 6 * D])
    w_in_d = din("w_in", [depth, D, 2816])
    w_out_d = din("w_out", [depth, D, D])
    w_up_d = din("w_up", [depth, D, 4 * D])
    w_down_d = din("w_down", [depth, 4 * D, D])
    bmod_d = din("bmod", [128, depth, 48])
    gmix_d = din("gmix", [128, depth, 8])
    gmlp_d = din("gmlp", [128, depth, 8])
    gfin_d = din("gfin", [128, 8])
    lamv_d = din("lamv", [128, depth, 4, 64])
    gsub_d = din("gsub", [128, depth, 128])
    cw_d = din("cw", [128, depth, 2, 4])
    wsT_d = din("wsT", [128, depth, 4, 128])
    bsB_d = din("bsB", [128, depth, 2, 128])
    cosT_d = din("cosT", [128, NST])
    sinS_d = din("sinS", [128, NST])
    perm_d = din("perm", [128, 128])
    ident_d = din("ident", [128, 128])
    sel_d = din("sel", [128, 2, 4])

    yT_d = nc.dram_tensor("yT", [128, 8, T], F32, kind="ExternalOutput").ap()
    nk_d = nc.dram_tensor("nk", [4, depth, 256, 512], F32, kind="ExternalOutput").ap()
    nv_d = nc.dram_tensor("nv", [4, depth, 256, 512], F32, kind="ExternalOutput").ap()
    bounce_t = [nc.dram_tensor("bounce%d" % d, [BROWS, 512], BF16, kind="Internal") for d in range(depth)]
    gath_t = [nc.dram_tensor("gath%d" % d, [4 * BROWS, 512], BF16, kind="Internal") for d in range(depth)]

    from contextlib import ExitStack
    st = ExitStack()

    def sb(name, shape, dt):
        return st.enter_context(nc.sbuf_tensor(name, list(shape), dt))

    X = sb("X", [128, 8, T], F32)
    HT = sb("HT", [128, 8, T], BF16)
    WS = sb("WS", [128, NSLOT, SLOT_ELEMS], BF16)
    R = sb("R", [128, 40960], BF16)
    PS = st.enter_context(nc.psum_tensor("PS", [128, 8, 512], F32))

    def rview(off, n, pat=None, dt=None, **kw):
        v = R[:, off:off + n]
        if dt is not None:
            v = v.bitcast(dt)
        if pat is not None:
            v = v.rearrange(pat, **kw)
        return v

    HID = rview(0, 40960, "p (c t) -> p c t", c=32)
    QT = rview(0, 5120, "p (c t) -> p c t", c=4)
    KT = rview(5120, 5120, "p (c t) -> p c t", c=4)
    VA = rview(10240, 5200, "p (k h e) -> p k h e", k=10, h=4)
    G = rview(15440, 5120, "p (c t) -> p c t", c=4)
    ZU = rview(20560, 10240, "p (c t) -> p c t", dt=F32, c=4)
    VS = rview(30800, 2560, "p (k e) -> p k e", k=10)
    KALL = rview(15440, 6144, "p (h k) -> p h k", h=4)
    VALL = rview(21584, 6240, "p (k h e) -> p k h e", k=12, h=4)
    CAT = HT
    LAMV = rview(33360, depth * 512, "p (d a e) -> p d a e", dt=F32, d=depth, a=4)
    LTMP = rview(36432, 512, "p (a e) -> p a e", dt=F32, a=4)

    MOD = [sb("MOD%d" % i, [128, 48, 2], F32) for i in range(2)]
    S1 = sb("S1", [128, 8, 2], F32)
    S2 = sb("S2", [128, 8, 2], F32)
    SC = sb("SC", [128, 8, 2], BF16)
    CV = sb("CVEC", [128, 8, 2], F32)
    BMOD = sb("BMOD", [128, depth, 48], F32)
    GMIX = sb("GMIX", [128, depth, 8], F32)
    GMLP = sb("GMLP", [128, depth, 8], F32)
    GFIN = sb("GFIN", [128, 8], F32)
    GSUB = sb("GSUB", [128, 128], F32)
    CW = sb("CW", [128, depth, 2, 4], F32)
    WST = sb("WST", [128, depth, 4, 128], BF16)
    BSB = sb("BSB", [128, depth, 2, 128], F32)
    COST = sb("COST", [128, NST], F32)
    SINS = sb("SINS", [128, NST], F32)
    PERM = sb("PERM", [128, 128], F32)
    IDB = sb("IDB", [128, 128], BF16)
    ONES = sb("ONES", [128, 128], BF16)
    SEL = sb("SEL", [128, 2, 4], F32)
    LAM = sb("LAM", [128, depth, 4], F32)
    SQ = [sb("SQ%d" % i, [128, 512], BF16) for i in range(3)]
    RSTDT = sb("RSTD", [128, 2, 512], F32)
    RSTD = [RSTDT[:, 0, :], RSTDT[:, 1, :]]
    CONVY = RSTDT[:].rearrange("p a n -> p (a n)")
    NTMP = [sb("NTMP%d" % i, [128, 512], F32) for i in range(3)]
    STGT = sb("STG", [128, 2, 512], F32)
    STG = [STGT[:, 0, :], STGT[:, 1, :]]
    CONVT = STGT[:].rearrange("p a n -> p (a n)")
    ZB = sb("ZB", [128, 2, 2], BF16)
    HB = sb("HB", [128, 4, 2, 2], BF16)
    HBF = sb("HBF", [128, 4, 2, 2], F32)
    HTMP = sb("HTMP", [128, 2, 2, 4], F32)
    HALO = sb("HALO", [128, 2, 2], F32)
    ATT_A = sb("ATTA", [128, 4, 128], F32)
    ATT_OB = [sb("ATTO%d" % i, [128, 4, 128], BF16) for i in range(2)]
    ATT_S = sb("ATTS", [128, 32], F32)
    FEN = sb("FEN", [128, 2], F32)
    EPSB = sb("EPSB", [128, 1], F32)

    rr = {}

    def rot(name, n):
        v = rr.get(name, 0)
        rr[name] = v + 1
        return v % n

    mm_nb = [4]

    def mmbank():
        v = rr.get("mm", 0)
        rr["mm"] = v + 1
        return v % mm_nb[0]

    pieces = []
    issued = [0]

    def add_piece(src_list):
        pieces.append(src_list)
        return len(pieces) - 1

    def ensure_issued(upto):
        while issued[0] <= min(upto, len(pieces) - 1):
            j = issued[0]
            slot = j % NSLOT
            plist = pieces[j]
            for hi, (kc0, kcn, cols, src) in enumerate(plist):
                dst = WS[:, slot, kc0 * cols:(kc0 + kcn) * cols].rearrange("p (k n) -> p k n", k=kcn)
                wkeys = [("WS", slot, hi)] if len(plist) == 2 else [("WS", slot, 0), ("WS", slot, 1)]
                S.emit("pool", I("dma_start", out=dst, in_=src), reads=[], writes=wkeys,
                       kind="d", semkey=("ws", slot, hi))
            issued[0] += 1

    def wslot(pi, la=NSLOT - 1):
        ensure_issued(pi + la)
        return pi % NSLOT

    def wkeys(slot):
        return [("WS", slot, 0), ("WS", slot, 1)]

    def wview(slot, kcn, cols):
        return WS[:, slot, 0:kcn * cols].rearrange("p (k n) -> p k n", k=kcn)

    def wsrc(w, c0, cols):
        return w.rearrange("(k p) n -> p k n", p=128)[:, :, c0:c0 + cols]

    plan = []
    WIN_ORDER = [(1536, 512, "g"), (2048, 512, "xu"), (512, 512, "k"), (1024, 512, "v"),
                 (0, 512, "q"), (2560, 256, "vs")]

    def mod_piece(d, j):
        pi = add_piece([(0, 8, 512, wsrc(w_mod_d[d], j * 512, 512))])
        plan.append(("mod", d, j, pi))

    for j in range(4):
        mod_piece(0, j)
    pend0 = list(range(4, 12))
    for d in range(depth):
        for wi, (c0, cols, nm) in enumerate(WIN_ORDER):
            pi = add_piece([(0, 8, cols, wsrc(w_in_d[d], c0, cols))])
            plan.append(("in_" + nm, d, 0, pi))
            if d == 0:
                for _ in range(2 if wi < 2 else 1):
                    if pend0:
                        mod_piece(0, pend0.pop(0))
        plan.append(("mix", d, 0, -1))
        for j in range(2):
            pi = add_piece([(0, 8, 512, wsrc(w_out_d[d], j * 512, 512))])
            plan.append(("out", d, j, pi))
        for j in range(8):
            pi = add_piece([(0, 8, 512, wsrc(w_up_d[d], j * 512, 512))])
            plan.append(("up", d, j, pi))
            if d + 1 < depth:
                mod_piece(d + 1, j)
        for j in range(8):
            src = w_down_d[d].rearrange("(k p) n -> p k n", p=128)
            pi = add_piece([(0, 16, 128, src[:, 0:16, j * 128:(j + 1) * 128]),
                            (16, 16, 128, src[:, 16:32, j * 128:(j + 1) * 128])])
            plan.append(("down", d, j, pi))
            if d + 1 < depth and j < 4:
                mod_piece(d + 1, 8 + j)

    def load(dst, src, key, q="sp", extra_w=()):
        S.emit(q, I("dma_start", out=dst, in_=src), reads=[], writes=[key] + list(extra_w),
               kind="d", semkey=key)

    load(CV[:], cvec_d, "CV")
    load(BMOD[:], bmod_d, "BMOD")
    for fc in range(8):
        load(X[:, fc, :], xT_d[:, fc, :], ("Xld", fc), extra_w=[("X", fc, b) for b in range(5)])
    load(GMIX[:], gmix_d, "GMIX")
    load(GMLP[:], gmlp_d, "GMLP")
    load(GFIN[:], gfin_d, "GFIN")
    load(LAMV[:], lamv_d, "LAMV")
    load(CW[:], cw_d, "CW")
    load(WST[:], wsT_d, "WST", q="pool")
    load(BSB[:], bsB_d, "BSB")
    load(COST[:], cosT_d, "COST")
    load(SINS[:], sinS_d, "SINS")
    load(PERM[:], perm_d, "PERM")
    load(IDB[:], ident_d, "IDB", q="pool")
    load(SEL[:], sel_d, "SEL")

    S.emit("dve", I("memset", ap=ONES[:], constant=1.0 / 1024.0), writes=["ONES"])
    S.emit("dve", I("memset", ap=EPSB[:], constant=EPS), writes=["EPSB"])
    S.emit("act", I("activation", out=SC[:], in_=CV[:], func=AF.Silu), reads=["CV"], writes=["SC"])
    lam_inits = [0.8 - 0.6 * math.exp(-0.3 * d) for d in range(depth)]
    for d in range(depth):
        for i in range(2):
            S.emit("dve", I("tensor_tensor", out=LTMP[:, i, :], in0=LAMV[:, d, 2 * i, :],
                            in1=LAMV[:, d, 2 * i + 1, :], op=ALU.mult),
                   reads=["LAMV"], writes=[("LTMP", i)])
        S.emit("dve", I("tensor_reduce", out=LAM[:, d, 1:3], in_=LTMP[:, 0:2, :], axis=AX.X, op=ALU.add),
               reads=[("LTMP", 0), ("LTMP", 1)], writes=[("LAM", d, 1)])
        S.emit("act", I("activation", out=LAM[:, d, 1:3], in_=LAM[:, d, 1:3], func=AF.Exp),
               reads=[("LAM", d, 1)], writes=[("LAM", d, 1)])
        S.emit("dve", I("scalar_tensor_tensor", out=LAM[:, d, 0:1], in0=LAM[:, d, 2:3],
                        scalar=-lam_inits[d], in1=LAM[:, d, 1:2], op0=ALU.add, op1=ALU.subtract),
               reads=[("LAM", d, 1)], writes=[("LAM", d, 0)])

    def do_mod_piece(d, j, pi):
        slot = wslot(pi)
        wv = wview(slot, 8, 512)
        bank = mmbank()
        M = MOD[d % 2]
        for fc in range(4):
            for kc in range(8):
                S.emit("pe", I("matmul", out=PS[:, bank, fc * 2:fc * 2 + 2],
                               lhsT=wv[:, kc, fc * 128:(fc + 1) * 128],
                               rhs=SC[:, kc, :], start=(kc == 0), stop=(kc == 7)),
                       reads=wkeys(slot) + ["SC"], writes=[("PS", bank)], sig=(kc == 7 and fc == 3))
        for v in range(2):
            S.emit("dve", I("tensor_tensor", out=M[:, j * 4:(j + 1) * 4, v], in0=PS[:, bank, v:8:2],
                            in1=BMOD[:, d, j * 4:(j + 1) * 4], op=ALU.add),
                   reads=[("PS", bank), "BMOD"], writes=[("MOD", d % 2, j, v)])

    def mod_ap(d, which, fc, v):
        return MOD[d % 2][:, which * 8 + fc, v:v + 1]

    def mod_key(d, which):
        return [("MOD", d % 2, which * 2 + i, v) for i in range(2) for v in range(2)]

    def do_scale_prep(d, which, gain, gkey, dst, name):
        M = MOD[d % 2]
        for v in range(2):
            S.emit("dve", I("scalar_tensor_tensor", out=dst[:, :, v], in0=M[:, which * 8:which * 8 + 8, v],
                            scalar=1.0, in1=gain[:, d, :], op0=ALU.add, op1=ALU.mult),
                   reads=mod_key(d, which) + [gkey], writes=[(name, v)])

    def stat_accum(oc, ti):
        t0, n = TT[ti]
        bl = blks(t0, n)
        sq = rot("sq", 3)
        S.emit("act", I("activation", out=SQ[sq][:, 0:n], in_=X[:, oc, t0:t0 + n], func=AF.Square),
               reads=[("X", oc, b) for b in bl], writes=[("SQ", sq)])
        S.emit("pe", I("matmul", out=PS[:, 5 + ti, 0:n], lhsT=ONES[:], rhs=SQ[sq][:, 0:n],
                       start=(oc == 0), stop=(oc == 7)),
               reads=["ONES", ("SQ", sq)], writes=[("PS", 5 + ti)], sig=True)

    def do_norm(d, scale_t, scale_name, shift_which, tis=(0, 1, 2), have_stats=False):
        for ti in tis:
            t0, n = TT[ti]
            v = 0 if ti < 2 else 1
            bl = blks(t0, n)
            if have_stats:
                bank = 5 + ti
            else:
                bank = mmbank()
            for fc in range(8):
                if have_stats:
                    break
                sq = rot("sq", 3)
                S.emit("act", I("activation", out=SQ[sq][:, 0:n], in_=X[:, fc, t0:t0 + n], func=AF.Square),
                       reads=[("X", fc, b) for b in bl], writes=[("SQ", sq)])
                S.emit("pe", I("matmul", out=PS[:, bank, 0:n], lhsT=ONES[:], rhs=SQ[sq][:, 0:n],
                               start=(fc == 0), stop=(fc == 7)),
                       reads=["ONES", ("SQ", sq)], writes=[("PS", bank)], sig=True)
            rs = rot("rstd", 2)
            S.emit("act", I("activation", out=RSTD[rs][:, 0:n], in_=PS[:, bank, 0:n], func=AF.Ln, bias=EPSB[:, 0:1], scale=1.0),
                   reads=[("PS", bank), "EPSB"], writes=[("RSTD", rs)])
            S.emit("act", I("activation", out=RSTD[rs][:, 0:n], in_=RSTD[rs][:, 0:n], func=AF.Exp, scale=-0.5),
                   reads=[("RSTD", rs)], writes=[("RSTD", rs)])
            for fc in range(8):
                nt = rot("ntmp", 3)
                S.emit("dve", I("scalar_tensor_tensor", out=NTMP[nt][:, 0:n], in0=X[:, fc, t0:t0 + n],
                                scalar=scale_t[:, fc, v:v + 1], in1=RSTD[rs][:, 0:n], op0=ALU.mult, op1=ALU.mult),
                       reads=[("X", fc, b) for b in bl] + [(scale_name, v), ("RSTD", rs)], writes=[("NTMP", nt)])
                S.emit("act", I("activation", out=HT[:, fc, t0:t0 + n], in_=NTMP[nt][:, 0:n], func=AF.Identity,
                                bias=mod_ap(d, shift_which, fc, v), scale=1.0),
                       reads=[("NTMP", nt)] + mod_key(d, shift_which), writes=[("HT", fc, b) for b in bl])

    def ws_matmul_fm(slot, wv, fc, ti, bank):
        t0, n = TT[ti]
        bl = blks(t0, n)
        for kc in range(8):
            S.emit("pe", I("matmul", out=PS[:, bank, 0:n], lhsT=wv[:, kc, fc * 128:(fc + 1) * 128],
                           rhs=HT[:, kc, t0:t0 + n], start=(kc == 0), stop=(kc == 7)),
                   reads=wkeys(slot) + [("HT", kc, b) for b in bl], writes=[("PS", bank)], sig=(kc == 7))

    def ws_matmul_tm(slot, wv, tk, bank, cols):
        for kc in range(8):
            S.emit("pe", I("matmul", out=PS[:, bank, 0:cols], lhsT=HT[:, kc, tk * 128:(tk + 1) * 128],
                           rhs=wv[:, kc, 0:cols], start=(kc == 0), stop=(kc == 7)),
                   reads=wkeys(slot) + [("HT", kc, tk // 2)], writes=[("PS", bank)], sig=(kc == 7))

    def rope_evac(bank, dstT, fc, keyname):
        nt = rot("ntmp", 3)
        QSv = NTMP[nt][:, 0:NST]
        RTv = NTMP[nt][:, NST:2 * NST]
        S.emit("act", I("activation", out=QSv, in_=PS[:, bank, 0:NST], func=AF.Copy),
               reads=[("PS", bank)], writes=[("NTMP", nt)])
        b2 = mmbank()
        S.emit("pe", I("matmul", out=PS[:, b2, 0:NST], lhsT=PERM[:], rhs=QSv, start=True, stop=True),
               reads=["PERM", ("NTMP", nt)], writes=[("PS", b2)], sig=True)
        S.emit("dve", I("tensor_tensor", out=RTv, in0=PS[:, b2, 0:NST], in1=SINS[:], op=ALU.mult),
               reads=[("PS", b2), "SINS"], writes=[("NTMP", nt)])
        S.emit("dve", I("tensor_tensor", out=QSv, in0=QSv, in1=COST[:], op=ALU.mult),
               reads=[("NTMP", nt), "COST"], writes=[("NTMP", nt)])
        S.emit("dve", I("tensor_tensor", out=dstT[:, fc, NPT:T], in0=QSv, in1=RTv, op=ALU.add),
               reads=[("NTMP", nt)], writes=[(keyname, fc, 4)])

    ktr_pending = []

    def do_in_piece(kind, d, pi):
        cols = 256 if kind == "vs" else 512
        slot = wslot(pi)
        wv = wview(slot, 8, cols)
        if kind in ("q", "k"):
            dstT = QT if kind == "q" else KT
            kn = "QT" if kind == "q" else "KT"
            for fc in range(4):
                for ti in range(3):
                    if kind == "k" and ti < 2:
                        continue
                    t0, n = TT[ti]
                    bank = mmbank()
                    ws_matmul_fm(slot, wv, fc, ti, bank)
                    if ti < 2:
                        S.emit("act", I("activation", out=dstT[:, fc, t0:t0 + n], in_=PS[:, bank, 0:n], func=AF.Copy),
                               reads=[("PS", bank)], writes=[(kn, fc, b) for b in blks(t0, n)])
                    else:
                        rope_evac(bank, dstT, fc, kn)
            if kind == "k":
                for tk in range(8):
                    bank = mmbank()
                    ws_matmul_tm(slot, wv, tk, bank, 512)
                    sg = rot("stg", 2)
                    S.emit("dve", I("tensor_copy", out=STG[sg][:], in_=PS[:, bank, :]),
                           reads=[("PS", bank)], writes=[("STG", sg)])
                    S.emit("sp", I("dma_start", out=nk_d[tk // 2, d, (tk % 2) * 128:(tk % 2) * 128 + 128, :], in_=STG[sg][:]),
                           reads=[("STG", sg)], writes=[], kind="d", semkey=("stgo", sg))
                    oi = rot("atto", 2)
                    KB = ATT_OB[oi]
                    S.emit("act", I("activation", out=KB[:].rearrange("p h e -> p (h e)"), in_=STG[sg][:], func=AF.Copy),
                           reads=[("STG", sg)], writes=[("ATTO", oi, u) for u in range(4)])

                    def ktr(tk=tk, oi=oi, KB=KB):
                        tb = mmbank()
                        TPV = PS[:, tb, 0:256].bitcast(BF16)
                        for h in range(4):
                            S.emit("pe", I("transpose", out=TPV[:, h * 128:(h + 1) * 128], in_=KB[:, h, :], identity=IDB[:]),
                                   reads=[("ATTO", oi, h), "IDB"], writes=[("PS", tb)], sig=(h == 3))
                        S.emit("act", I("activation", out=KT[:, :, tk * 128:(tk + 1) * 128],
                                        in_=TPV[:, 0:512].rearrange("p (h q) -> p h q", h=4), func=AF.Copy),
                               reads=[("PS", tb)], writes=[("KT", fc, tk // 2) for fc in range(4)])
                    ktr_pending.append(ktr)
                    while len(ktr_pending) > 1:
                        ktr_pending.pop(0)()
                while ktr_pending:
                    ktr_pending.pop(0)()
        elif kind == "v":
            for tk in range(10):
                bank = mmbank()
                ws_matmul_tm(slot, wv, tk, bank, 512)
                sg = rot("stg", 2)
                S.emit("dve", I("tensor_copy", out=STG[sg][:], in_=PS[:, bank, :]),
                       reads=[("PS", bank)], writes=[("STG", sg)])
                S.emit("act", I("activation", out=VA[:, tk, :, 0:128],
                                in_=STG[sg][:].rearrange("p (h e) -> p h e", h=4), func=AF.Copy),
                       reads=[("STG", sg)], writes=[("VA", tk)])
                if tk < 8:
                    S.emit("sp", I("dma_start", out=nv_d[tk // 2, d, (tk % 2) * 128:(tk % 2) * 128 + 128, :], in_=STG[sg][:]),
                           reads=[("STG", sg)], writes=[], kind="d", semkey=("stgo", sg))
        elif kind == "g":
            for ti in range(3):
                for fc in range(4):
                    t0, n = TT[ti]
                    bank = mmbank()
                    ws_matmul_fm(slot, wv, fc, ti, bank)
                    S.emit("act", I("activation", out=G[:, fc, t0:t0 + n], in_=PS[:, bank, 0:n], func=AF.Copy),
                           reads=[("PS", bank)], writes=[("G", fc, b) for b in blks(t0, n)])
        elif kind == "xu":
            for fc in range(4):
                for ti in range(3):
                    t0, n = TT[ti]
                    bank = mmbank()
                    ws_matmul_fm(slot, wv, fc, ti, bank)
                    if fc < 2:
                        S.emit("dve", I("tensor_tensor", out=ZU[:, fc, t0:t0 + n], in0=PS[:, bank, 0:n],
                                        in1=G[:, 2 + fc, t0:t0 + n], op=ALU.mult),
                               reads=[("PS", bank)] + [("G", 2 + fc, b) for b in blks(t0, n)],
                               writes=[("ZU", fc, b) for b in blks(t0, n)])
                    else:
                        S.emit("act", I("activation", out=ZU[:, fc, t0:t0 + n], in_=PS[:, bank, 0:n], func=AF.Copy),
                               reads=[("PS", bank)], writes=[("ZU", fc, b) for b in blks(t0, n)])
        elif kind == "vs":
            for tk in range(10):
                bank = mmbank()
                ws_matmul_tm(slot, wv, tk, bank, 256)
                S.emit("act", I("activation", out=VS[:, tk, :], in_=PS[:, bank, 0:256], func=AF.Copy),
                       reads=[("PS", bank)], writes=[("VS", tk)])
                if tk % 2 == 1 and tk >= 3:
                    blk = tk // 2 - 1
                    do_cmlp(d, blk)
                    do_conv(d, blk)
            do_cmlp(d, 4)

    def flat_ap(t, off, dims):
        return bass.AP(t, off, [list(x) for x in dims])

    def do_exchange(d):
        bt = bounce_t[d]
        gt = gath_t[d]
        S.emit("dve", I("tensor_copy", out=ZB[:, :, 0], in_=ZU[:, 0:2, NPT]),
               reads=[("ZU", 0, 4), ("ZU", 1, 4)], writes=[("ZB", 0)])
        S.emit("dve", I("tensor_copy", out=ZB[:, :, 1], in_=ZU[:, 0:2, T - 1]),
               reads=[("ZU", 0, 4), ("ZU", 1, 4)], writes=[("ZB", 1)])
        S.emit("sp", I("dma_start", out=flat_ap(bt, 0, [[256, 128], [128 * 256, 4], [1, 256]]), in_=KT[:, :, NPT:T]),
               reads=[("KT", fc, 4) for fc in range(4)], writes=[("BOUNCE", d, "k")], kind="d", semkey="bk")
        for jj in range(2):
            S.emit("sp", I("dma_start", out=flat_ap(bt, 131072 + jj * 65536, [[512, 128], [128, 4], [1, 128]]),
                           in_=VA[:, 8 + jj, :, 0:128]),
                   reads=[("VA", 8 + jj)], writes=[("BOUNCE", d, "v", jj)], kind="d", semkey=("bv", jj))
        S.emit("sp", I("dma_start", out=flat_ap(bt, 262144, [[4, 128], [1, 4]]), in_=ZB[:].rearrange("p j e -> p (j e)")),
               reads=[("ZB", 0), ("ZB", 1)], writes=[("BOUNCE", d, "h")], kind="d", semkey="bh")
        S.emit("pool", I("collective_compute", kind="AllGather", op=ALU.bypass,
                         replica_groups=[[0, 1, 2, 3], [4, 5, 6, 7]], ins=[bt.ap()], outs=[gt.ap()]),
               reads=[("BOUNCE", d, "k"), ("BOUNCE", d, "v", 0), ("BOUNCE", d, "v", 1), ("BOUNCE", d, "h")], writes=[("GATH", d)], kind="cc")
        S.emit("sp", I("dma_start", out=HB[:].rearrange("p r j e -> p r (j e)"),
                       in_=flat_ap(gt, 262144, [[4, 128], [BROWS * 512, 4], [1, 4]])),
               reads=[("GATH", d)], writes=["HB"], kind="d", semkey="hb")

    def do_halo(d):
        S.emit("dve", I("tensor_copy", out=HBF[:], in_=HB[:]), reads=["HB"], writes=["HBF"])
        for w, esrc in ((0, 1), (1, 0)):
            for j in range(2):
                S.emit("dve", I("tensor_tensor", out=HTMP[:, w, j, :], in0=HBF[:, :, j, esrc], in1=SEL[:, w, :], op=ALU.mult),
                       reads=["HBF", "SEL"], writes=[("HTMP", w, j)])
        S.emit("dve", I("tensor_reduce", out=HALO[:], in_=HTMP[:], axis=AX.X, op=ALU.add),
               reads=[("HTMP", w, j) for w in range(2) for j in range(2)], writes=["HALO"])

    KALLK = ["KALLc"] + [("KALLg", r) for r in range(4)]
    VALLK = [("VALLc", kk) for kk in range(4)] + ["VALL1"] + [("VALLg", r, jj) for r in range(4) for jj in range(2)]

    def do_load_sample_keys(d):
        gt = gath_t[d]
        S.emit("pool", I("dma_start", out=KALL[:, :, 0:PAST], in_=ckT_d[d]),
               reads=[], writes=["KALLc"], kind="d", semkey="ck")
        for kk in range(4):
            S.emit("pool", I("dma_start", out=VALL[:, kk, :, 0:128],
                             in_=cv_d[d, kk * 128:(kk + 1) * 128, :].rearrange("p (h e) -> p h e", h=4)),
                   reads=[], writes=[("VALLc", kk)], kind="d", semkey=("cvv", kk))
        S.emit("dve", I("memset", ap=VALL[:, :, :, 128:129], constant=1.0), reads=[], writes=["VALL1"])
        for r in range(4):
            S.emit("sp", I("dma_start", out=KALL[:, :, PAST + r * 256:PAST + (r + 1) * 256],
                           in_=flat_ap(gt, r * BROWS * 512, [[256, 128], [128 * 256, 4], [1, 256]])),
                   reads=[("GATH", d)], writes=[("KALLg", r)], kind="d", semkey=("gk", r))
            for jj in range(2):
                S.emit("sp", I("dma_start", out=VALL[:, 4 + 2 * r + jj, :, 0:128],
                               in_=flat_ap(gt, r * BROWS * 512 + 131072 + jj * 65536, [[512, 128], [128, 4], [1, 128]])),
                       reads=[("GATH", d)], writes=[("VALLg", r, jj)], kind="d", semkey=("gv", r, jj))

    ETALL = rview(33360, 6144)

    def oblock(k):
        return 5 + k // 3, (k % 3) * 129

    def do_attention(d, qb, sample, filler=None):
        nkt = 12 if sample else 2
        q0 = qb * 256
        ebase = {}
        ekey = {}
        for h in range(4):
            if sample:
                ebase[h] = (0, 3072)
                ekey[h] = ([("ETS", i) for i in range(0, 3)], [("ETS", i) for i in range(3, 6)])
            else:
                sl = rot("ets", 6)
                ebase[h] = (sl * 1024, sl * 1024 + 512)
                ekey[h] = ([("ETS", sl)], [("ETS", sl)])

        def scores(h):
            for j in range(nkt // 2):
                spair = ((0, 1), (2, 3))[rot("sb", 2)]
                for kk in range(2):
                    kt = 2 * j + kk
                    for c in range(2):
                        if sample:
                            lhsT = KALL[c * 64:(c + 1) * 64, h, kt * 128:(kt + 1) * 128]
                            rk = list(KALLK)
                        else:
                            lhsT = KT[c * 64:(c + 1) * 64, h, q0 + kt * 128:q0 + (kt + 1) * 128]
                            rk = [("KT", h, qb)]
                        S.emit("pe", I("matmul", out=PS[:, spair[c], kk * 256:(kk + 1) * 256], lhsT=lhsT,
                                       rhs=QT[c * 64:(c + 1) * 64, h, q0:q0 + 256], start=True, stop=True),
                               reads=rk + [("QT", h, qb)], writes=[("PS", spair[c])], sig=(kk == 1))
                for c in range(2):
                    o0 = ebase[h][c] + j * 512
                    S.emit("act", I("activation", out=ETALL[:, o0:o0 + 512], in_=PS[:, spair[c], :],
                                    func=AF.Exp, scale=0.125),
                           reads=[("PS", spair[c])], writes=ekey[h][c])

        def batch(units):
            U = len(units)
            oi = rot("atto", 2)
            ATT_O = ATT_OB[oi]
            obanks = sorted(set(oblock(k)[0] for k in range(2 * U)))
            okeys = [("PS", bk) for bk in obanks]
            order = [(u, c) for c in range(2) for u in range(U)] if sample else [(u, c) for u in range(U) for c in range(2)]
            for (u, c) in order:
                h, qt = units[u]
                if True:
                    bk, col = oblock(u * 2 + c)
                    for kt in range(nkt):
                        if sample:
                            rhs = VALL[:, kt, h, 0:129]
                            rk = list(VALLK)
                        else:
                            rhs = VA[:, qb * 2 + kt, h, 0:129]
                            rk = [("VA", qb * 2 + kt), "VA1"]
                        e0 = ebase[h][c] + kt * 256 + qt * 128
                        last = (kt == nkt - 1)
                        S.emit("pe", I("matmul", out=PS[:, bk, col:col + 129], lhsT=ETALL[:, e0:e0 + 128], rhs=rhs,
                                       start=(kt == 0), stop=last),
                               reads=rk + ekey[h][c], writes=[("PS", bk)], sig=last)
            sm = ATT_S
            for bk in obanks:
                ks = [k for k in range(2 * U) if oblock(k)[0] == bk]
                n = len(ks)
                S.emit("dve", I("reciprocal", out=sm[:, ks[0]:ks[0] + n],
                                in_=PS[:, bk, 0:n * 129].rearrange("p (k e) -> p k e", e=129)[:, :, 128]),
                       reads=[("PS", bk)], writes=[("ATTS", "r", bk)])
            rkeys = [("ATTS", "r", bk) for bk in obanks]
            S.emit("dve", I("tensor_scalar", out=sm[:, 8:8 + U], in0=sm[:, 1:2 * U:2], scalar1=LAM[:, d, 0:1],
                            scalar2=None, op0=ALU.mult),
                   reads=rkeys + [("LAM", d, 0)], writes=[("ATTS", "n")])
            for u in range(U):
                bk, col = oblock(u * 2)
                S.emit("dve", I("tensor_scalar", out=ATT_A[:, u, :], in0=PS[:, bk, col:col + 128],
                                scalar1=sm[:, 2 * u:2 * u + 1], scalar2=None, op0=ALU.mult),
                       reads=[("PS", bk)] + rkeys, writes=[("ATTA", u)])
            for u in range(U):
                bk, col = oblock(u * 2 + 1)
                S.emit("dve", I("scalar_tensor_tensor", out=ATT_A[:, u, :], in0=PS[:, bk, col:col + 128],
                                scalar=sm[:, 8 + u:9 + u], in1=ATT_A[:, u, :], op0=ALU.mult, op1=ALU.add),
                       reads=[("PS", bk), ("ATTS", "n"), ("ATTA", u)], writes=[("ATTA", u)])
            akeys = [("ATTA", u) for u in range(U)]
            jn = rot("ntmp", 3)
            ATT_J = NTMP[jn]
            S.emit("dve", I("tensor_tensor", out=ATT_J[:, 0:U * 128], in0=ATT_A[:, 0:U, :].rearrange("p u e -> p (u e)"),
                            in1=ATT_A[:, 0:U, :].rearrange("p u e -> p (u e)"), op=ALU.mult),
                   reads=akeys, writes=[("NTMP", jn)])
            S.emit("dve", I("tensor_reduce", out=sm[:, 16:16 + U], in_=ATT_J[:, 0:U * 128].rearrange("p (u e) -> p u e", u=U),
                            axis=AX.X, op=ALU.add),
                   reads=[("NTMP", jn)], writes=[("ATTS", "q")])
            S.emit("act", I("activation", out=sm[:, 20:20 + U], in_=sm[:, 16:16 + U], func=AF.Ln, bias=EPSB[:, 0:1],
                            scale=1.0 / 128.0),
                   reads=[("ATTS", "q"), "EPSB"], writes=[("ATTS", "l")])
            S.emit("act", I("activation", out=sm[:, 24:24 + U], in_=sm[:, 20:20 + U], func=AF.Exp, scale=-0.5),
                   reads=[("ATTS", "l")], writes=[("ATTS", "s")])
            for u in range(U):
                S.emit("dve", I("scalar_tensor_tensor", out=ATT_O[:, u, :], in0=ATT_A[:, u, :], scalar=sm[:, 24 + u:25 + u],
                                in1=GSUB[:], op0=ALU.mult, op1=ALU.mult),
                       reads=[("ATTA", u), ("ATTS", "s"), "GSUB"], writes=[("ATTO", oi, u)])
            def fin():
                TPV = PS[:, 4, 0:256].bitcast(BF16)
                for u in range(U):
                    S.emit("pe", I("transpose", out=TPV[:, u * 128:(u + 1) * 128], in_=ATT_O[:, u, :], identity=IDB[:]),
                           reads=[("ATTO", oi, u), "IDB"], writes=[("PS", 4)], sig=(u == U - 1))
                if sample:
                    hh = units[0][0]
                    S.emit("act", I("activation", out=CAT[:, hh, q0:q0 + 256], in_=TPV[:, 0:256], func=AF.Copy),
                           reads=[("PS", 4)], writes=[("HT", hh, qb)])
                else:
                    qt0 = units[0][1]
                    t0 = q0 + qt0 * 128
                    S.emit("act", I("activation", out=CAT[:, 0:4, t0:t0 + 128],
                                    in_=TPV[:, 0:512].rearrange("p (h q) -> p h q", h=4), func=AF.Copy),
                           reads=[("PS", 4)], writes=[("HT", hh, qb) for hh in range(4)])
            return fin

        def push(fin):
            att_pending.append(fin)
            while len(att_pending) > 1:
                att_pending.pop(0)()

        if sample:
            for h in range(4):
                scores(h)
                push(batch([(h, 0), (h, 1)]))
                if filler is not None:
                    filler()
        else:
            for h in range(4):
                scores(h)
            for qt in range(2):
                push(batch([(h, qt) for h in range(4)]))
                if filler is not None:
                    filler()

    att_pending = []

    def att_flush():
        while att_pending:
            att_pending.pop(0)()

    def do_conv_prompt(d):
        for j in range(2):
            Z3 = ZU[:, j, 0:NPT].rearrange("p (s t) -> p s t", s=4)
            C3 = CAT[:, 4 + j, 0:NPT].rearrange("p (s t) -> p s t", s=4)
            zr = [("ZU", j, b) for b in range(4)]
            ck = [("HT", 4 + j, b) for b in range(4)]
            S.emit("dve", I("tensor_scalar", out=CONVY[:, 0:NPT], in0=ZU[:, j, 0:NPT], scalar1=CW[:, d, j, 1:2],
                            scalar2=CW[:, d, j, 3:4], op0=ALU.mult, op1=ALU.add),
                   reads=zr + ["CW"], writes=["CONVY", ("RSTD", 0), ("RSTD", 1)])
            Y3 = CONVY[:, 0:NPT].rearrange("p (s t) -> p s t", s=4)
            S.emit("dve", I("scalar_tensor_tensor", out=Y3[:, :, 1:256], in0=Z3[:, :, 0:255], scalar=CW[:, d, j, 0:1],
                            in1=Y3[:, :, 1:256], op0=ALU.mult, op1=ALU.add),
                   reads=zr + ["CW", "CONVY"], writes=["CONVY"])
            S.emit("dve", I("scalar_tensor_tensor", out=Y3[:, :, 0:255], in0=Z3[:, :, 1:256], scalar=CW[:, d, j, 2:3],
                            in1=Y3[:, :, 0:255], op0=ALU.mult, op1=ALU.add),
                   reads=zr + ["CW", "CONVY"], writes=["CONVY"])
            S.emit("dve", I("tensor_tensor", out=CAT[:, 4 + j, 0:NPT], in0=CONVY[:, 0:NPT], in1=G[:, j, 0:NPT], op=ALU.mult),
                   reads=["CONVY"] + [("G", j, b) for b in range(4)], writes=ck)

    def do_conv(d, blk):
        t0 = blk * 256
        sample = blk == 4
        for j in range(2):
            ci = rot("ntmp", 3)
            Y = NTMP[ci][:, 0:256]
            zr = [("ZU", j, blk)]
            S.emit("dve", I("tensor_scalar", out=Y[:], in0=ZU[:, j, t0:t0 + 256], scalar1=CW[:, d, j, 1:2],
                            scalar2=CW[:, d, j, 3:4], op0=ALU.mult, op1=ALU.add),
                   reads=zr + ["CW"], writes=[("NTMP", ci)])
            S.emit("dve", I("scalar_tensor_tensor", out=Y[:, 1:256], in0=ZU[:, j, t0:t0 + 255], scalar=CW[:, d, j, 0:1],
                            in1=Y[:, 1:256], op0=ALU.mult, op1=ALU.add),
                   reads=zr + ["CW", ("NTMP", ci)], writes=[("NTMP", ci)])
            S.emit("dve", I("scalar_tensor_tensor", out=Y[:, 0:255], in0=ZU[:, j, t0 + 1:t0 + 256], scalar=CW[:, d, j, 2:3],
                            in1=Y[:, 0:255], op0=ALU.mult, op1=ALU.add),
                   reads=zr + ["CW", ("NTMP", ci)], writes=[("NTMP", ci)])
            if sample:
                S.emit("dve", I("scalar_tensor_tensor", out=Y[:, 0:1], in0=HALO[:, 0, j:j + 1], scalar=CW[:, d, j, 0:1],
                                in1=Y[:, 0:1], op0=ALU.mult, op1=ALU.add),
                       reads=["HALO", "CW", ("NTMP", ci)], writes=[("NTMP", ci)])
                S.emit("dve", I("scalar_tensor_tensor", out=Y[:, 255:256], in0=HALO[:, 1, j:j + 1], scalar=CW[:, d, j, 2:3],
                                in1=Y[:, 255:256], op0=ALU.mult, op1=ALU.add),
                       reads=["HALO", "CW", ("NTMP", ci)], writes=[("NTMP", ci)])
            S.emit("dve", I("tensor_tensor", out=CAT[:, 4 + j, t0:t0 + 256], in0=Y[:], in1=G[:, j, t0:t0 + 256], op=ALU.mult),
                   reads=[("NTMP", ci), ("G", j, blk)], writes=[("HT", 4 + j, blk)])

    def do_cmlp(d, blk):
        for half in range(2):
            tk = blk * 2 + half
            t0 = tk * 128
            for pair in range(2):
                bank = mmbank()
                S.emit("pe", I("matmul", out=PS[:, bank, 0:256], lhsT=VS[:, tk, pair * 128:(pair + 1) * 128],
                               rhs=WST[:, d, pair * 2:pair * 2 + 2, :].rearrange("q g p -> q (g p)"),
                               start=True, stop=True),
                       reads=[("VS", tk), "WST"], writes=[("PS", bank)], sig=True)
                ci = rot("ntmp", 3)
                for gi in range(2):
                    S.emit("dve", I("tensor_tensor", out=NTMP[ci][gi * 64:gi * 64 + 64, 0:128],
                                    in0=PS[gi * 64:gi * 64 + 64, bank, gi * 128:(gi + 1) * 128],
                                    in1=BSB[gi * 64:gi * 64 + 64, d, pair, :], op=ALU.add),
                           reads=[("PS", bank), "BSB"], writes=[("NTMP", ci, gi)])
                S.emit("dve", I("tensor_tensor", out=CAT[:, 6 + pair, t0:t0 + 128], in0=NTMP[ci][:, 0:128],
                                in1=ZU[:, 2 + pair, t0:t0 + 128], op=ALU.mult),
                       reads=[("NTMP", ci, 0), ("NTMP", ci, 1), ("ZU", 2 + pair, blk)], writes=[("HT", 6 + pair, blk)])

    def resid_evac(bank, oc, ti, d, which, rng=None):
        t0, n = TT[ti] if rng is None else rng
        v = 0 if t0 < NPT else 1
        bl = blks(t0, n)
        S.emit("dve", I("scalar_tensor_tensor", out=X[:, oc, t0:t0 + n], in0=PS[:, bank, 0:n],
                        scalar=mod_ap(d, which, oc, v), in1=X[:, oc, t0:t0 + n], op0=ALU.mult, op1=ALU.add),
               reads=[("PS", bank)] + mod_key(d, which) + [("X", oc, b) for b in bl], writes=[("X", oc, b) for b in bl])

    def out_groups(d, pis, rng):
        slots = [wslot(pis[0], la=NSLOT - 1), wslot(pis[1], la=NSLOT - 2)]
        res = []
        for j in range(2):
            wv = wview(slots[j], 8, 512)
            for fc in range(4):
                def g(j=j, fc=fc, wv=wv):
                    oc = j * 4 + fc
                    t0, n = rng
                    bank = mmbank()
                    for kc in range(8):
                        S.emit("pe", I("matmul", out=PS[:, bank, 0:n], lhsT=wv[:, kc, fc * 128:(fc + 1) * 128],
                                       rhs=CAT[:, kc, t0:t0 + n], start=(kc == 0), stop=(kc == 7)),
                               reads=wkeys(slots[j]) + [("HT", kc, b) for b in blks(t0, n)], writes=[("PS", bank)],
                               sig=(kc == 7))
                    resid_evac(bank, oc, 0, d, 2, rng=rng)
                res.append(g)
        return res

    def do_out_piece(d, j, pi, tis=(0, 1, 2)):
        slot = wslot(pi, la=NSLOT - 1 - j)
        wv = wview(slot, 8, 512)
        for ti in tis:
            for fc in range(4):
                oc = j * 4 + fc
                t0, n = TT[ti]
                bank = mmbank()
                for kc in range(8):
                    S.emit("pe", I("matmul", out=PS[:, bank, 0:n], lhsT=wv[:, kc, fc * 128:(fc + 1) * 128],
                                   rhs=CAT[:, kc, t0:t0 + n], start=(kc == 0), stop=(kc == 7)),
                           reads=wkeys(slot) + [("HT", kc, b) for b in blks(t0, n)], writes=[("PS", bank)], sig=(kc == 7))
                resid_evac(bank, oc, ti, d, 2)

    def do_up_piece(d, j, pi):
        slot = wslot(pi)
        wv = wview(slot, 8, 512)
        for ti in range(3):
            for fc in range(4):
                hc = j * 4 + fc
                t0, n = TT[ti]
                bank = mmbank()
                ws_matmul_fm(slot, wv, fc, ti, bank)
                nt = rot("ntmp", 3)
                S.emit("act", I("activation", out=NTMP[nt][:, 0:n], in_=PS[:, bank, 0:n], func=AF.Relu),
                       reads=[("PS", bank)], writes=[("NTMP", nt)])
                S.emit("dve", I("tensor_tensor", out=HID[:, hc, t0:t0 + n], in0=NTMP[nt][:, 0:n], in1=NTMP[nt][:, 0:n],
                                op=ALU.mult),
                       reads=[("NTMP", nt)], writes=[("HID", hc, b) for b in blks(t0, n)])

    stat_pending = []

    def do_down_piece(d, j, pi):
        slot = wslot(pi)
        wv = wview(slot, 32, 128)
        for ti in range(3):
            t0, n = TT[ti]
            bank = mmbank()
            for kc in range(32):
                S.emit("pe", I("matmul", out=PS[:, bank, 0:n], lhsT=wv[:, kc, :], rhs=HID[:, kc, t0:t0 + n],
                               start=(kc == 0), stop=(kc == 31)),
                       reads=wkeys(slot) + [("HID", kc, b) for b in blks(t0, n)], writes=[("PS", bank)], sig=(kc == 31))
            resid_evac(bank, j, ti, d, 5)
            stat_pending.append((j, ti))
            while len(stat_pending) > (0 if (j == 7 and ti == 2) else 2):
                stat_accum(*stat_pending.pop(0))

    REGION_MIX = ([("QT", c, b) for c in range(4) for b in range(5)] + [("KT", c, b) for c in range(4) for b in range(5)]
                  + [("VA", k) for k in range(10)] + ["VA1"] + [("G", c, b) for c in range(4) for b in range(5)]
                  + [("ZU", c, b) for c in range(4) for b in range(5)] + [("VS", k) for k in range(10)]
                  + [("ETS", i) for i in range(6)] + KALLK + VALLK)
    REGION_HID = [("HID", c, b) for c in range(32) for b in range(5)]
    GZ_KEYS = [("G", c, b) for c in range(4) for b in range(5)] + [("ZU", c, b) for c in range(4) for b in range(5)]

    def fence(keys):
        S.emit("dve", I("memset", ap=FEN[:, 0:1], constant=0.0), reads=[], writes=list(keys))

    for (kind, d, j, pi) in plan:
        if kind == "mod":
            do_mod_piece(d, j, pi)
            continue
        if kind == "in_g":
            load(GSUB[:], gsub_d[:, d, :], "GSUB")
            S.emit("dve", I("tensor_scalar", out=GSUB[:], in0=GSUB[:], scalar1=(1.0 - lam_inits[d]),
                            scalar2=None, op0=ALU.mult),
                   reads=["GSUB"], writes=["GSUB"])
            do_scale_prep(d, 1, GMIX, "GMIX", S1, "S1")
            fence(REGION_MIX + REGION_HID)
            do_norm(d, S1, "S1", 0, have_stats=(d > 0))
            S.emit("dve", I("memset", ap=VA[:, :, :, 128:129], constant=1.0), reads=[], writes=["VA1"])
        if kind.startswith("in_"):
            mm_nb[0] = 8
            do_in_piece(kind[3:], d, pi)
            mm_nb[0] = 4
            if kind == "in_v":
                do_exchange(d)
            continue
        if kind == "mix":
            continue
        if kind == "out":
            if j == 0:
                out_pis = [pi]
                continue
            out_pis.append(pi)
            gb = [out_groups(d, out_pis, (b * 256, 256)) for b in range(5)]

            def fill_from(lst, k):
                def f():
                    for _ in range(k):
                        if lst:
                            lst.pop(0)()
                return f

            sched = [None, (0, 4), (0, 4), (1, 4), (1, 4), (2, 4), (2, 4), (3, 3), (3, 3), (3, 2)]
            fidx = [0]

            def filler():
                e = sched[fidx[0]] if fidx[0] < len(sched) else None
                fidx[0] += 1
                if e is not None:
                    fill_from(gb[e[0]], e[1])()

            do_attention(d, 0, False)
            do_attention(d, 1, False, filler=filler)
            do_halo(d)
            do_conv(d, 4)
            fence(GZ_KEYS + KALLK + VALLK)
            do_load_sample_keys(d)
            do_attention(d, 2, False, filler=filler)
            do_scale_prep(d, 4, GMLP, "GMLP", S2, "S2")
            do_attention(d, 3, False, filler=filler)
            att_flush()
            fill_from(gb[0], 8)()
            fill_from(gb[1], 8)()
            do_norm(d, S2, "S2", 3, tis=(0,))
            do_attention(d, 4, True, filler=filler)
            for b_ in range(4):
                fill_from(gb[b_], 8)()
            att_flush()
            fill_from(gb[4], 8)()
            do_norm(d, S2, "S2", 3, tis=(1,))
            do_norm(d, S2, "S2", 3, tis=(2,))
            fence(REGION_MIX + REGION_HID)
            continue
        if kind == "up":
            mm_nb[0] = 8
            do_up_piece(d, j, pi)
            mm_nb[0] = 4
            continue
        if kind == "down":
            mm_nb[0] = 5
            do_down_piece(d, j, pi)
            mm_nb[0] = 4
            continue

    for ti, (t0, n) in enumerate(TT):
        bank = 5 + ti
        bl = blks(t0, n)
        rs = rot("rstd", 2)
        S.emit("act", I("activation", out=RSTD[rs][:, 0:n], in_=PS[:, bank, 0:n], func=AF.Ln, bias=EPSB[:, 0:1], scale=1.0),
               reads=[("PS", bank), "EPSB"], writes=[("RSTD", rs)])
        S.emit("act", I("activation", out=RSTD[rs][:, 0:n], in_=RSTD[rs][:, 0:n], func=AF.Exp, scale=-0.5),
               reads=[("RSTD", rs)], writes=[("RSTD", rs)])
        for fc in range(8):
            S.emit("dve", I("scalar_tensor_tensor", out=X[:, fc, t0:t0 + n], in0=X[:, fc, t0:t0 + n],
                            scalar=GFIN[:, fc:fc + 1], in1=RSTD[rs][:, 0:n], op0=ALU.mult, op1=ALU.mult),
                   reads=[("X", fc, b) for b in bl] + ["GFIN", ("RSTD", rs)], writes=[("X", fc, b) for b in bl])
    for fc in range(8):
        S.emit("sp", I("dma_start", out=yT_d[:, fc, :], in_=X[:, fc, :]),
               reads=[("X", fc, b) for b in range(5)], writes=[], kind="d", semkey=("yo", fc))

    S.run(nc)
    st.close()
    return nc


def _rope_tables_np(pos0, n):
    t = np.arange(pos0, pos0 + n)
    row = (t // 64).astype(np.float64)
    col = (t % 64).astype(np.float64)
    inv = 10000.0 ** (-np.arange(0, 32, 2, dtype=np.float64) / 32.0)
    ar = row[:, None] * inv[None, :]
    ac = col[:, None] * inv[None, :]
    cosT = np.zeros((128, n), np.float32)
    sinS = np.zeros((128, n), np.float32)
    for p in range(128):
        e = p % 64
        ax = e // 32
        i = e % 32
        first = i < 16
        ang = (ar if ax == 0 else ac)[:, i % 16]
        cosT[p] = np.cos(ang)
        sinS[p] = (-np.sin(ang)) if first else np.sin(ang)
    return cosT, sinS


def _perm_np():
    P = np.zeros((128, 128), np.float32)
    for m in range(128):
        i = m % 32
        partner = m + 16 if i < 16 else m - 16
        P[partner, m] = 1.0
    return P


_NC_CACHE = {}


def _prepare(x_prompt, x_sample, cache_k, cache_v, c, c_ctx, w_mod, b_mod, norm_mix, norm_mlp,
             w_in, lam_q1, lam_k1, lam_q2, lam_k2, subln, conv_w, conv_b, w_s, b_s,
             w_out, w_up, w_down, norm_final):
    f = lambda a: np.ascontiguousarray(np.asarray(a, dtype=np.float32))
    x_prompt, x_sample, cache_k, cache_v, c, c_ctx = map(f, (x_prompt, x_sample, cache_k, cache_v, c, c_ctx))
    w_mod, b_mod, norm_mix, norm_mlp, w_in = map(f, (w_mod, b_mod, norm_mix, norm_mlp, w_in))
    lam_q1, lam_k1, lam_q2, lam_k2, subln = map(f, (lam_q1, lam_k1, lam_q2, lam_k2, subln))
    conv_w, conv_b, w_s, b_s, w_out, w_up, w_down, norm_final = map(
        f, (conv_w, conv_b, w_s, b_s, w_out, w_up, w_down, norm_final))
    depth = w_in.shape[0]
    bmod = np.ascontiguousarray(b_mod.reshape(depth, 48, 128).transpose(2, 0, 1))
    gmix = np.ascontiguousarray(norm_mix.reshape(depth, 8, 128).transpose(2, 0, 1))
    gmlp = np.ascontiguousarray(norm_mlp.reshape(depth, 8, 128).transpose(2, 0, 1))
    gfin = np.ascontiguousarray(norm_final.reshape(8, 128).T)
    lamv = np.ascontiguousarray(np.broadcast_to(np.stack([lam_q1, lam_k1, lam_q2, lam_k2], axis=1)[None], (128, depth, 4, 64)))
    gsub = np.ascontiguousarray(np.broadcast_to(subln[None], (128, depth, 128)))
    cw = np.zeros((128, depth, 2, 4), np.float32)
    cw[:, :, :, 0:3] = conv_w.reshape(depth, 3, 2, 128).transpose(3, 0, 2, 1)
    cw[:, :, :, 3] = conv_b.reshape(depth, 2, 128).transpose(2, 0, 1)
    wsT = np.ascontiguousarray(w_s.transpose(3, 0, 1, 2))
    bsB = np.zeros((128, depth, 2, 128), np.float32)
    for pair in range(2):
        for gi in range(2):
            bsB[gi * 64:(gi + 1) * 64, :, pair, :] = b_s[:, pair * 2 + gi, :][None]
    perm = _perm_np()
    ident = np.eye(128, dtype=np.float32)
    in_maps = []
    for core in range(8):
        s = core // 4
        r = core % 4
        xtok = np.concatenate([x_prompt[4 * core:4 * core + 4].reshape(1024, D),
                               x_sample[s, r * 256:(r + 1) * 256]], axis=0)
        xT = np.ascontiguousarray(xtok.reshape(T, 8, 128).transpose(2, 1, 0))
        ckT = np.ascontiguousarray(cache_k[s, :depth].transpose(0, 3, 2, 1))
        cv = np.ascontiguousarray(cache_v[s, :depth].reshape(depth, PAST, 512))
        cvec = np.ascontiguousarray(np.stack([c_ctx, c[s]], axis=-1).reshape(8, 128, 2).transpose(1, 0, 2))
        cosT, sinS = _rope_tables_np(r * 256, 256)
        sel = np.zeros((128, 2, 4), np.float32)
        if r > 0:
            sel[:, 0, r - 1] = 1.0
        if r < 3:
            sel[:, 1, r + 1] = 1.0
        in_maps.append({
            "xT": xT, "ckT": ckT, "cv": cv, "cvec": cvec,
            "w_mod": w_mod, "w_in": w_in, "w_out": w_out, "w_up": w_up, "w_down": w_down,
            "bmod": bmod, "gmix": gmix, "gmlp": gmlp, "gfin": gfin, "lamv": lamv, "gsub": gsub,
            "cw": cw, "wsT": wsT, "bsB": bsB, "cosT": cosT, "sinS": sinS, "perm": perm, "ident": ident,
            "sel": sel,
        })
    return depth, in_maps


def _assemble(outs, depth):
    y_prompt = np.zeros((32, 256, D), np.float32)
    y_sample = np.zeros((2, 1024, D), np.float32)
    new_k = np.zeros((32, depth, 256, 4, 128), np.float32)
    new_v = np.zeros((32, depth, 256, 4, 128), np.float32)
    for core in range(8):
        s = core // 4
        r = core % 4
        ytok = np.asarray(outs[core]["yT"]).transpose(2, 1, 0).reshape(T, D)
        y_prompt[4 * core:4 * core + 4] = ytok[:1024].reshape(4, 256, D)
        y_sample[s, r * 256:(r + 1) * 256] = ytok[1024:]
        new_k[4 * core:4 * core + 4] = np.asarray(outs[core]["nk"]).reshape(4, depth, 256, 4, 128)
        new_v[4 * core:4 * core + 4] = np.asarray(outs[core]["nv"]).reshape(4, depth, 256, 4, 128)
    return (y_prompt, y_sample, new_k, new_v)


def kernel(**inputs):
    depth, in_maps = _prepare(**inputs)
    if depth not in _NC_CACHE:
        _NC_CACHE[depth] = build_nc(depth)
    nc = _NC_CACHE[depth]
    res = run_bass_kernel_spmd(nc, in_maps, core_ids=list(range(8)))
    return _assemble(res.results, depth)
```

```python
import math
import numpy as np
import concourse.bass as bass
import concourse.mybir as mybir
from concourse.bass_utils import run_bass_kernel_spmd

F32 = mybir.dt.float32
BF16 = mybir.dt.bfloat16
AF = mybir.ActivationFunctionType
ALU = mybir.AluOpType
AX = mybir.AxisListType

D = 1024
DEPTH = 4
T = 1280
NPT = 1024
NST = 256
PAST = 512
NKS = PAST + 1024
EPS = 1e-6
TT = [(0, 512), (512, 512), (1024, 256)]
NSLOT = 4
SLOT_ELEMS = 4096
BROWS = 513


def blks(a, n):
    return list(range(a // 256, (a + n + 255) // 256))


class Op:
    __slots__ = ("q", "fn", "deps", "kind", "sig", "sem", "val", "inc", "idx")


class Sched:
    QS = ("pe", "act", "dve", "pool", "sp")

    def __init__(self):
        self.ops = {q: [] for q in self.QS}
        self.res = {}
        self.dma_keys = {}
        self.cc_count = 0

    def emit(self, q, fn, reads=(), writes=(), kind="c", sig=True, semkey=None):
        op = Op()
        op.q = q
        op.fn = fn
        op.kind = kind
        op.sig = sig
        op.sem = None
        op.val = None
        op.inc = 1
        deps = {}
        for r in reads:
            e = self.res.get(r)
            if e is not None and e[0] is not None:
                deps[id(e[0])] = (e[0], True)
        for w in writes:
            e = self.res.get(w)
            if e is not None:
                if e[0] is not None and id(e[0]) not in deps:
                    deps[id(e[0])] = (e[0], False)
                for rd in e[1]:
                    if id(rd) not in deps:
                        deps[id(rd)] = (rd, False)
        dl = []
        for dep, raw in deps.values():
            if dep is op:
                continue
            if dep.q == q and dep.kind == "c" and kind == "c":
                if q == "pe":
                    continue
            dl.append(dep)
        if kind == "d":
            ent = self.dma_keys.setdefault(semkey, [0, None])
            if ent[1] is not None:
                dl.append(ent[1])
            ent[0] += 1
            ent[1] = op
            op.sem = ("dma", semkey)
            op.val = ent[0] * 16
        elif kind == "cc":
            self.cc_count += 1
            op.sem = ("cc", 0)
            op.val = self.cc_count
        op.deps = dl
        for r in reads:
            e = self.res.setdefault(r, [None, []])
            e[1].append(op)
        for w in writes:
            self.res[w] = [op, []]
        op.idx = len(self.ops[q])
        self.ops[q].append(op)
        return op

    def finalize(self):
        for q in self.QS:
            ops = [o for o in self.ops[q] if o.kind == "c"]
            if ops:
                ops[-1].sig = True
            cnt = 0
            for o in ops:
                if o.sig:
                    cnt += 1
                    o.val = cnt
                o.sem = ("q", q)
            nxt = None
            for o in reversed(ops):
                if o.sig:
                    nxt = o.val
                else:
                    o.val = nxt

    def run(self, nc):
        self.finalize()
        names = [("q", q) for q in self.QS] + [("dma", k) for k in self.dma_keys] + [("cc", 0)]
        from contextlib import ExitStack
        with ExitStack() as st:
            sems = {}
            for i, n in enumerate(names):
                sems[n] = st.enter_context(nc.semaphore("s%d" % i))
            block = st.enter_context(nc.Block())
            handles = {"pe": block.tensor, "act": block.scalar, "dve": block.vector,
                       "pool": block.gpsimd, "sp": block.sync}
            for q in self.QS:
                ops = self.ops[q]

                def body(eng, ops=ops, q=q):
                    waited = {}
                    for o in ops:
                        for dpn in o.deps:
                            if waited.get(dpn.sem, 0) < dpn.val:
                                eng.wait_ge(sems[dpn.sem], dpn.val)
                                waited[dpn.sem] = dpn.val
                        ins = o.fn(eng)
                        if o.kind == "c":
                            if o.sig:
                                ins.then_inc(sems[o.sem], 1)
                        elif o.kind == "d":
                            ins.then_inc(sems[o.sem], 16)
                        else:
                            ins.then_inc(sems[o.sem], 1)
                    if q == "sp":
                        for k, ent in self.dma_keys.items():
                            if waited.get(("dma", k), 0) < ent[0] * 16:
                                eng.wait_ge(sems[("dma", k)], ent[0] * 16)
                        for qq in self.QS:
                            cops = [o for o in self.ops[qq] if o.kind == "c"]
                            if cops:
                                eng.wait_ge(sems[("q", qq)], cops[-1].val)

                handles[q](body)


def I(method, **kw):
    return lambda e: getattr(e, method)(**kw)


def build_nc(depth=DEPTH):
    nc = bass.Bass("TRN2", target_bir_lowering=False)
    S = Sched()

    def din(name, shape, dt=F32):
        return nc.dram_tensor(name, list(shape), dt, kind="ExternalInput").ap()

    xT_d = din("xT", [128, 8, T])
    ckT_d = din("ckT", [depth, 128, 4, PAST])
    cv_d = din("cv", [depth, PAST, 512])
    cvec_d = din("cvec", [128, 8, 2])
    w_mod_d = din("w_mod", [depth, D, 6 * D])
    w_in_d = din("w_in", [depth, D, 2816])
    w_out_d = din("w_out", [depth, D, D])
    w_up_d = din("w_up", [depth, D, 4 * D])
    w_down_d = din("w_down", [depth, 4 * D, D])
    bmod_d = din("bmod", [128, depth, 48])
    gmix_d = din("gmix", [128, depth, 8])
    gmlp_d = din("gmlp", [128, depth, 8])
    gfin_d = din("gfin", [128, 8])
    lamv_d = din("lamv", [128, depth, 4, 64])
    gsub_d = din("gsub", [128, depth, 128])
    cw_d = din("cw", [128, depth, 2, 4])
    wsT_d = din("wsT", [128, depth, 4, 128])
    bsB_d = din("bsB", [128, depth, 2, 128])
    cosT_d = din("cosT", [128, NST])
    sinS_d = din("sinS", [128, NST])
    perm_d = din("perm", [128, 128])
    ident_d = din("ident", [128, 128])
    sel_d = din("sel", [128, 2, 4])

    yT_d = nc.dram_tensor("yT", [128, 8, T], F32, kind="ExternalOutput").ap()
    nk_d = nc.dram_tensor("nk", [4, depth, 256, 512], F32, kind="ExternalOutput").ap()
    nv_d = nc.dram_tensor("nv", [4, depth, 256, 512], F32, kind="ExternalOutput").ap()
    bounce_t = [nc.dram_tensor("bounce%d" % d, [BROWS, 512], BF16, kind="Internal") for d in range(depth)]
    gath_t = [nc.dram_tensor("gath%d" % d, [4 * BROWS, 512], BF16, kind="Internal") for d in range(depth)]

    from contextlib import ExitStack
    st = ExitStack()

    def sb(name, shape, dt):
        return st.enter_context(nc.sbuf_tensor(name, list(shape), dt))

    X = sb("X", [128, 8, T], F32)
    HT = sb("HT", [128, 8, T], BF16)
    WS = sb("WS", [128, NSLOT, SLOT_ELEMS], BF16)
    R = sb("R", [128, 40960], BF16)
    PS = st.enter_context(nc.psum_tensor("PS", [128, 8, 512], F32))

    def rview(off, n, pat=None, dt=None, **kw):
        v = R[:, off:off + n]
        if dt is not None:
            v = v.bitcast(dt)
        if pat is not None:
            v = v.rearrange(pat, **kw)
        return v

    HID = rview(0, 40960, "p (c t) -> p c t", c=32)
    QT = rview(0, 5120, "p (c t) -> p c t", c=4)
    KT = rview(5120, 5120, "p (c t) -> p c t", c=4)
    VA = rview(10240, 5200, "p (k h e) -> p k h e", k=10, h=4)
    G = rview(15440, 5120, "p (c t) -> p c t", c=4)
    ZU = rview(20560, 10240, "p (c t) -> p c t", dt=F32, c=4)
    VS = rview(30800, 2560, "p (k e) -> p k e", k=10)
    KALL = rview(15440, 6144, "p (h k) -> p h k", h=4)
    VALL = rview(21584, 6240, "p (k h e) -> p k h e", k=12, h=4)
    CAT = HT
    LAMV = rview(33360, depth * 512, "p (d a e) -> p d a e", dt=F32, d=depth, a=4)
    LTMP = rview(36432, 512, "p (a e) -> p a e", dt=F32, a=4)

    MOD = [sb("MOD%d" % i, [128, 48, 2], F32) for i in range(2)]
    S1 = sb("S1", [128, 8, 2], F32)
    S2 = sb("S2", [128, 8, 2], F32)
    SC = sb("SC", [128, 8, 2], BF16)
    CV = sb("CVEC", [128, 8, 2], F32)
    BMOD = sb("BMOD", [128, depth, 48], F32)
    GMIX = sb("GMIX", [128, depth, 8], F32)
    GMLP = sb("GMLP", [128, depth, 8], F32)
    GFIN = sb("GFIN", [128, 8], F32)
    GSUB = sb("GSUB", [128, 128], F32)
    CW = sb("CW", [128, depth, 2, 4], F32)
    WST = sb("WST", [128, depth, 4, 128], BF16)
    BSB = sb("BSB", [128, depth, 2, 128], F32)
    COST = sb("COST", [128, NST], F32)
    SINS = sb("SINS", [128, NST], F32)
    PERM = sb("PERM", [128, 128], F32)
    IDB = sb("IDB", [128, 128], BF16)
    ONES = sb("ONES", [128, 128], BF16)
    SEL = sb("SEL", [128, 2, 4], F32)
    LAM = sb("LAM", [128, depth, 4], F32)
    SQ = [sb("SQ%d" % i, [128, 512], BF16) for i in range(3)]
    RSTDT = sb("RSTD", [128, 2, 512], F32)
    RSTD = [RSTDT[:, 0, :], RSTDT[:, 1, :]]
    CONVY = RSTDT[:].rearrange("p a n -> p (a n)")
    NTMP = [sb("NTMP%d" % i, [128, 512], F32) for i in range(3)]
    STGT = sb("STG", [128, 2, 512], F32)
    STG = [STGT[:, 0, :], STGT[:, 1, :]]
    CONVT = STGT[:].rearrange("p a n -> p (a n)")
    ZB = sb("ZB", [128, 2, 2], BF16)
    HB = sb("HB", [128, 4, 2, 2], BF16)
    HBF = sb("HBF", [128, 4, 2, 2], F32)
    HTMP = sb("HTMP", [128, 2, 2, 4], F32)
    HALO = sb("HALO", [128, 2, 2], F32)
    ATT_A = sb("ATTA", [128, 4, 128], F32)
    ATT_OB = [sb("ATTO%d" % i, [128, 4, 128], BF16) for i in range(2)]
    ATT_S = sb("ATTS", [128, 32], F32)
    FEN = sb("FEN", [128, 2], F32)
    EPSB = sb("EPSB", [128, 1], F32)

    rr = {}

    def rot(name, n):
        v = rr.get(name, 0)
        rr[name] = v + 1
        return v % n

    mm_nb = [4]

    def mmbank():
        v = rr.get("mm", 0)
        rr["mm"] = v + 1
        return v % mm_nb[0]

    pieces = []
    issued = [0]

    def add_piece(src_list):
        pieces.append(src_list)
        return len(pieces) - 1

    def ensure_issued(upto):
        while issued[0] <= min(upto, len(pieces) - 1):
            j = issued[0]
            slot = j % NSLOT
            plist = pieces[j]
            for hi, (kc0, kcn, cols, src) in enumerate(plist):
                dst = WS[:, slot, kc0 * cols:(kc0 + kcn) * cols].rearrange("p (k n) -> p k n", k=kcn)
                wkeys = [("WS", slot, hi)] if len(plist) == 2 else [("WS", slot, 0), ("WS", slot, 1)]
                S.emit("pool", I("dma_start", out=dst, in_=src), reads=[], writes=wkeys,
                       kind="d", semkey=("ws", slot, hi))
            issued[0] += 1

    def wslot(pi, la=NSLOT - 1):
        ensure_issued(pi + la)
        return pi % NSLOT

    def wkeys(slot):
        return [("WS", slot, 0), ("WS", slot, 1)]

    def wview(slot, kcn, cols):
        return WS[:, slot, 0:kcn * cols].rearrange("p (k n) -> p k n", k=kcn)

    def wsrc(w, c0, cols):
        return w.rearrange("(k p) n -> p k n", p=128)[:, :, c0:c0 + cols]

    plan = []
    WIN_ORDER = [(1536, 512, "g"), (2048, 512, "xu"), (512, 512, "k"), (1024, 512, "v"),
                 (0, 512, "q"), (2560, 256, "vs")]

    def mod_piece(d, j):
        pi = add_piece([(0, 8, 512, wsrc(w_mod_d[d], j * 512, 512))])
        plan.append(("mod", d, j, pi))

    for j in range(4):
        mod_piece(0, j)
    pend0 = list(range(4, 12))
    for d in range(depth):
        for wi, (c0, cols, nm) in enumerate(WIN_ORDER):
            pi = add_piece([(0, 8, cols, wsrc(w_in_d[d], c0, cols))])
            plan.append(("in_" + nm, d, 0, pi))
            if d == 0:
                for _ in range(2 if wi < 2 else 1):
                    if pend0:
                        mod_piece(0, pend0.pop(0))
        plan.append(("mix", d, 0, -1))
        for j in range(2):
            pi = add_piece([(0, 8, 512, wsrc(w_out_d[d], j * 512, 512))])
            plan.append(("out", d, j, pi))
        for j in range(8):
            pi = add_piece([(0, 8, 512, wsrc(w_up_d[d], j * 512, 512))])
            plan.append(("up", d, j, pi))
            if d + 1 < depth:
                mod_piece(d + 1, j)
        for j in range(8):
            src = w_down_d[d].rearrange("(k p) n -> p k n", p=128)
            pi = add_piece([(0, 16, 128, src[:, 0:16, j * 128:(j + 1) * 128]),
                            (16, 16, 128, src[:, 16:32, j * 128:(j + 1) * 128])])
            plan.append(("down", d, j, pi))
            if d + 1 < depth and j < 4:
                mod_piece(d + 1, 8 + j)

    def load(dst, src, key, q="sp", extra_w=()):
        S.emit(q, I("dma_start", out=dst, in_=src), reads=[], writes=[key] + list(extra_w),
               kind="d", semkey=key)

    load(CV[:], cvec_d, "CV")
    load(BMOD[:], bmod_d, "BMOD")
    for fc in range(8):
        load(X[:, fc, :], xT_d[:, fc, :], ("Xld", fc), extra_w=[("X", fc, b) for b in range(5)])
    load(GMIX[:], gmix_d, "GMIX")
    load(GMLP[:], gmlp_d, "GMLP")
    load(GFIN[:], gfin_d, "GFIN")
    load(LAMV[:], lamv_d, "LAMV")
    load(CW[:], cw_d, "CW")
    load(WST[:], wsT_d, "WST", q="pool")
    load(BSB[:], bsB_d, "BSB")
    load(COST[:], cosT_d, "COST")
    load(SINS[:], sinS_d, "SINS")
    load(PERM[:], perm_d, "PERM")
    load(IDB[:], ident_d, "IDB", q="pool")
    load(SEL[:], sel_d, "SEL")

    S.emit("dve", I("memset", ap=ONES[:], constant=1.0 / 1024.0), writes=["ONES"])
    S.emit("dve", I("memset", ap=EPSB[:], constant=EPS), writes=["EPSB"])
    S.emit("act", I("activation", out=SC[:], in_=CV[:], func=AF.Silu), reads=["CV"], writes=["SC"])
    lam_inits = [0.8 - 0.6 * math.exp(-0.3 * d) for d in range(depth)]
    for d in range(depth):
        for i in range(2):
            S.emit("dve", I("tensor_tensor", out=LTMP[:, i, :], in0=LAMV[:, d, 2 * i, :],
                            in1=LAMV[:, d, 2 * i + 1, :], op=ALU.mult),
                   reads=["LAMV"], writes=[("LTMP", i)])
        S.emit("dve", I("tensor_reduce", out=LAM[:, d, 1:3], in_=LTMP[:, 0:2, :], axis=AX.X, op=ALU.add),
               reads=[("LTMP", 0), ("LTMP", 1)], writes=[("LAM", d, 1)])
        S.emit("act", I("activation", out=LAM[:, d, 1:3], in_=LAM[:, d, 1:3], func=AF.Exp),
               reads=[("LAM", d, 1)], writes=[("LAM", d, 1)])
        S.emit("dve", I("scalar_tensor_tensor", out=LAM[:, d, 0:1], in0=LAM[:, d, 2:3],
                        scalar=-lam_inits[d], in1=LAM[:, d, 1:2], op0=ALU.add, op1=ALU.subtract),
               reads=[("LAM", d, 1)], writes=[("LAM", d, 0)])

    def do_mod_piece(d, j, pi):
        slot = wslot(pi)
        wv = wview(slot, 8, 512)
        bank = mmbank()
        M = MOD[d % 2]
        for fc in range(4):
            for kc in range(8):
                S.emit("pe", I("matmul", out=PS[:, bank, fc * 2:fc * 2 + 2],
                               lhsT=wv[:, kc, fc * 128:(fc + 1) * 128],
                               rhs=SC[:, kc, :], start=(kc == 0), stop=(kc == 7)),
                       reads=wkeys(slot) + ["SC"], writes=[("PS", bank)], sig=(kc == 7 and fc == 3))
        for v in range(2):
            S.emit("dve", I("tensor_tensor", out=M[:, j * 4:(j + 1) * 4, v], in0=PS[:, bank, v:8:2],
                            in1=BMOD[:, d, j * 4:(j + 1) * 4], op=ALU.add),
                   reads=[("PS", bank), "BMOD"], writes=[("MOD", d % 2, j, v)])

    def mod_ap(d, which, fc, v):
        return MOD[d % 2][:, which * 8 + fc, v:v + 1]

    def mod_key(d, which):
        return [("MOD", d % 2, which * 2 + i, v) for i in range(2) for v in range(2)]

    def do_scale_prep(d, which, gain, gkey, dst, name):
        M = MOD[d % 2]
        for v in range(2):
            S.emit("dve", I("scalar_tensor_tensor", out=dst[:, :, v], in0=M[:, which * 8:which * 8 + 8, v],
                            scalar=1.0, in1=gain[:, d, :], op0=ALU.add, op1=ALU.mult),
                   reads=mod_key(d, which) + [gkey], writes=[(name, v)])

    def stat_accum(oc, ti):
        t0, n = TT[ti]
        bl = blks(t0, n)
        sq = rot("sq", 3)
        S.emit("act", I("activation", out=SQ[sq][:, 0:n], in_=X[:, oc, t0:t0 + n], func=AF.Square),
               reads=[("X", oc, b) for b in bl], writes=[("SQ", sq)])
        S.emit("pe", I("matmul", out=PS[:, 5 + ti, 0:n], lhsT=ONES[:], rhs=SQ[sq][:, 0:n],
                       start=(oc == 0), stop=(oc == 7)),
               reads=["ONES", ("SQ", sq)], writes=[("PS", 5 + ti)], sig=True)

    def do_norm(d, scale_t, scale_name, shift_which, tis=(0, 1, 2), have_stats=False):
        for ti in tis:
            t0, n = TT[ti]
            v = 0 if ti < 2 else 1
            bl = blks(t0, n)
            if have_stats:
                bank = 5 + ti
            else:
                bank = mmbank()
            for fc in range(8):
                if have_stats:
                    break
                sq = rot("sq", 3)
                S.emit("act", I("activation", out=SQ[sq][:, 0:n], in_=X[:, fc, t0:t0 + n], func=AF.Square),
                       reads=[("X", fc, b) for b in bl], writes=[("SQ", sq)])
                S.emit("pe", I("matmul", out=PS[:, bank, 0:n], lhsT=ONES[:], rhs=SQ[sq][:, 0:n],
                               start=(fc == 0), stop=(fc == 7)),
                       reads=["ONES", ("SQ", sq)], writes=[("PS", bank)], sig=True)
            rs = rot("rstd", 2)
            S.emit("act", I("activation", out=RSTD[rs][:, 0:n], in_=PS[:, bank, 0:n], func=AF.Ln, bias=EPSB[:, 0:1], scale=1.0),
                   reads=[("PS", bank), "EPSB"], writes=[("RSTD", rs)])
            S.emit("act", I("activation", out=RSTD[rs][:, 0:n], in_=RSTD[rs][:, 0:n], func=AF.Exp, scale=-0.5),
                   reads=[("RSTD", rs)], writes=[("RSTD", rs)])
            for fc in range(8):
                nt = rot("ntmp", 3)
                S.emit("dve", I("scalar_tensor_tensor", out=NTMP[nt][:, 0:n], in0=X[:, fc, t0:t0 + n],
                                scalar=scale_t[:, fc, v:v + 1], in1=RSTD[rs][:, 0:n], op0=ALU.mult, op1=ALU.mult),
                       reads=[("X", fc, b) for b in bl] + [(scale_name, v), ("RSTD", rs)], writes=[("NTMP", nt)])
                S.emit("act", I("activation", out=HT[:, fc, t0:t0 + n], in_=NTMP[nt][:, 0:n], func=AF.Identity,
                                bias=mod_ap(d, shift_which, fc, v), scale=1.0),
                       reads=[("NTMP", nt)] + mod_key(d, shift_which), writes=[("HT", fc, b) for b in bl])

    def ws_matmul_fm(slot, wv, fc, ti, bank):
        t0, n = TT[ti]
        bl = blks(t0, n)
        for kc in range(8):
            S.emit("pe", I("matmul", out=PS[:, bank, 0:n], lhsT=wv[:, kc, fc * 128:(fc + 1) * 128],
                           rhs=HT[:, kc, t0:t0 + n], start=(kc == 0), stop=(kc == 7)),
                   reads=wkeys(slot) + [("HT", kc, b) for b in bl], writes=[("PS", bank)], sig=(kc == 7))

    def ws_matmul_tm(slot, wv, tk, bank, cols):
        for kc in range(8):
            S.emit("pe", I("matmul", out=PS[:, bank, 0:cols], lhsT=HT[:, kc, tk * 128:(tk + 1) * 128],
                           rhs=wv[:, kc, 0:cols], start=(kc == 0), stop=(kc == 7)),
                   reads=wkeys(slot) + [("HT", kc, tk // 2)], writes=[("PS", bank)], sig=(kc == 7))

    def rope_evac(bank, dstT, fc, keyname):
        nt = rot("ntmp", 3)
        QSv = NTMP[nt][:, 0:NST]
        RTv = NTMP[nt][:, NST:2 * NST]
        S.emit("act", I("activation", out=QSv, in_=PS[:, bank, 0:NST], func=AF.Copy),
               reads=[("PS", bank)], writes=[("NTMP", nt)])
        b2 = mmbank()
        S.emit("pe", I("matmul", out=PS[:, b2, 0:NST], lhsT=PERM[:], rhs=QSv, start=True, stop=True),
               reads=["PERM", ("NTMP", nt)], writes=[("PS", b2)], sig=True)
        S.emit("dve", I("tensor_tensor", out=RTv, in0=PS[:, b2, 0:NST], in1=SINS[:], op=ALU.mult),
               reads=[("PS", b2), "SINS"], writes=[("NTMP", nt)])
        S.emit("dve", I("tensor_tensor", out=QSv, in0=QSv, in1=COST[:], op=ALU.mult),
               reads=[("NTMP", nt), "COST"], writes=[("NTMP", nt)])
        S.emit("dve", I("tensor_tensor", out=dstT[:, fc, NPT:T], in0=QSv, in1=RTv, op=ALU.add),
               reads=[("NTMP", nt)], writes=[(keyname, fc, 4)])

    ktr_pending = []

    def do_in_piece(kind, d, pi):
        cols = 256 if kind == "vs" else 512
        slot = wslot(pi)
        wv = wview(slot, 8, cols)
        if kind in ("q", "k"):
            dstT = QT if kind == "q" else KT
            kn = "QT" if kind == "q" else "KT"
            for fc in range(4):
                for ti in range(3):
                    if kind == "k" and ti < 2:
                        continue
                    t0, n = TT[ti]
                    bank = mmbank()
                    ws_matmul_fm(slot, wv, fc, ti, bank)
                    if ti < 2:
                        S.emit("act", I("activation", out=dstT[:, fc, t0:t0 + n], in_=PS[:, bank, 0:n], func=AF.Copy),
                               reads=[("PS", bank)], writes=[(kn, fc, b) for b in blks(t0, n)])
                    else:
                        rope_evac(bank, dstT, fc, kn)
            if kind == "k":
                for tk in range(8):
                    bank = mmbank()
                    ws_matmul_tm(slot, wv, tk, bank, 512)
                    sg = rot("stg", 2)
                    S.emit("dve", I("tensor_copy", out=STG[sg][:], in_=PS[:, bank, :]),
                           reads=[("PS", bank)], writes=[("STG", sg)])
                    S.emit("sp", I("dma_start", out=nk_d[tk // 2, d, (tk % 2) * 128:(tk % 2) * 128 + 128, :], in_=STG[sg][:]),
                           reads=[("STG", sg)], writes=[], kind="d", semkey=("stgo", sg))
                    oi = rot("atto", 2)
                    KB = ATT_OB[oi]
                    S.emit("act", I("activation", out=KB[:].rearrange("p h e -> p (h e)"), in_=STG[sg][:], func=AF.Copy),
                           reads=[("STG", sg)], writes=[("ATTO", oi, u) for u in range(4)])

                    def ktr(tk=tk, oi=oi, KB=KB):
                        tb = mmbank()
                        TPV = PS[:, tb, 0:256].bitcast(BF16)
                        for h in range(4):
                            S.emit("pe", I("transpose", out=TPV[:, h * 128:(h + 1) * 128], in_=KB[:, h, :], identity=IDB[:]),
                                   reads=[("ATTO", oi, h), "IDB"], writes=[("PS", tb)], sig=(h == 3))
                        S.emit("act", I("activation", out=KT[:, :, tk * 128:(tk + 1) * 128],
                                        in_=TPV[:, 0:512].rearrange("p (h q) -> p h q", h=4), func=AF.Copy),
                               reads=[("PS", tb)], writes=[("KT", fc, tk // 2) for fc in range(4)])
                    ktr_pending.append(ktr)
                    while len(ktr_pending) > 1:
                        ktr_pending.pop(0)()
                while ktr_pending:
                    ktr_pending.pop(0)()
        elif kind == "v":
            for tk in range(10):
                bank = mmbank()
                ws_matmul_tm(slot, wv, tk, bank, 512)
                sg = rot("stg", 2)
                S.emit("dve", I("tensor_copy", out=STG[sg][:], in_=PS[:, bank, :]),
                       reads=[("PS", bank)], writes=[("STG", sg)])
                S.emit("act", I("activation", out=VA[:, tk, :, 0:128],
                                in_=STG[sg][:].rearrange("p (h e) -> p h e", h=4), func=AF.Copy),
                       reads=[("STG", sg)], writes=[("VA", tk)])
                if tk < 8:
                    S.emit("sp", I("dma_start", out=nv_d[tk // 2, d, (tk % 2) * 128:(tk % 2) * 128 + 128, :], in_=STG[sg][:]),
                           reads=[("STG", sg)], writes=[], kind="d", semkey=("stgo", sg))
        elif kind == "g":
            for ti in range(3):
                for fc in range(4):
                    t0, n = TT[ti]
                    bank = mmbank()
                    ws_matmul_fm(slot, wv, fc, ti, bank)
                    S.emit("act", I("activation", out=G[:, fc, t0:t0 + n], in_=PS[:, bank, 0:n], func=AF.Copy),
                           reads=[("PS", bank)], writes=[("G", fc, b) for b in blks(t0, n)])
        elif kind == "xu":
            for fc in range(4):
                for ti in range(3):
                    t0, n = TT[ti]
                    bank = mmbank()
                    ws_matmul_fm(slot, wv, fc, ti, bank)
                    if fc < 2:
                        S.emit("dve", I("tensor_tensor", out=ZU[:, fc, t0:t0 + n], in0=PS[:, bank, 0:n],
                                        in1=G[:, 2 + fc, t0:t0 + n], op=ALU.mult),
                               reads=[("PS", bank)] + [("G", 2 + fc, b) for b in blks(t0, n)],
                               writes=[("ZU", fc, b) for b in blks(t0, n)])
                    else:
                        S.emit("act", I("activation", out=ZU[:, fc, t0:t0 + n], in_=PS[:, bank, 0:n], func=AF.Copy),
                               reads=[("PS", bank)], writes=[("ZU", fc, b) for b in blks(t0, n)])
        elif kind == "vs":
            for tk in range(10):
                bank = mmbank()
                ws_matmul_tm(slot, wv, tk, bank, 256)
                S.emit("act", I("activation", out=VS[:, tk, :], in_=PS[:, bank, 0:256], func=AF.Copy),
                       reads=[("PS", bank)], writes=[("VS", tk)])
                if tk % 2 == 1 and tk >= 3:
                    blk = tk // 2 - 1
                    do_cmlp(d, blk)
            do_cmlp(d, 4)
            do_conv_prompt(d)

    def flat_ap(t, off, dims):
        return bass.AP(t, off, [list(x) for x in dims])

    def do_exchange(d):
        bt = bounce_t[d]
        gt = gath_t[d]
        S.emit("dve", I("tensor_copy", out=ZB[:, :, 0], in_=ZU[:, 0:2, NPT]),
               reads=[("ZU", 0, 4), ("ZU", 1, 4)], writes=[("ZB", 0)])
        S.emit("dve", I("tensor_copy", out=ZB[:, :, 1], in_=ZU[:, 0:2, T - 1]),
               reads=[("ZU", 0, 4), ("ZU", 1, 4)], writes=[("ZB", 1)])
        S.emit("sp", I("dma_start", out=flat_ap(bt, 0, [[256, 128], [128 * 256, 4], [1, 256]]), in_=KT[:, :, NPT:T]),
               reads=[("KT", fc, 4) for fc in range(4)], writes=[("BOUNCE", d, "k")], kind="d", semkey="bk")
        for jj in range(2):
            S.emit("sp", I("dma_start", out=flat_ap(bt, 131072 + jj * 65536, [[512, 128], [128, 4], [1, 128]]),
                           in_=VA[:, 8 + jj, :, 0:128]),
                   reads=[("VA", 8 + jj)], writes=[("BOUNCE", d, "v", jj)], kind="d", semkey=("bv", jj))
        S.emit("sp", I("dma_start", out=flat_ap(bt, 262144, [[4, 128], [1, 4]]), in_=ZB[:].rearrange("p j e -> p (j e)")),
               reads=[("ZB", 0), ("ZB", 1)], writes=[("BOUNCE", d, "h")], kind="d", semkey="bh")
        S.emit("pool", I("collective_compute", kind="AllGather", op=ALU.bypass,
                         replica_groups=[[0, 1, 2, 3], [4, 5, 6, 7]], ins=[bt.ap()], outs=[gt.ap()]),
               reads=[("BOUNCE", d, "k"), ("BOUNCE", d, "v", 0), ("BOUNCE", d, "v", 1), ("BOUNCE", d, "h")], writes=[("GATH", d)], kind="cc")
        S.emit("sp", I("dma_start", out=HB[:].rearrange("p r j e -> p r (j e)"),
                       in_=flat_ap(gt, 262144, [[4, 128], [BROWS * 512, 4], [1, 4]])),
               reads=[("GATH", d)], writes=["HB"], kind="d", semkey="hb")

    def do_halo(d):
        S.emit("dve", I("tensor_copy", out=HBF[:], in_=HB[:]), reads=["HB"], writes=["HBF"])
        for w, esrc in ((0, 1), (1, 0)):
            for j in range(2):
                S.emit("dve", I("tensor_tensor", out=HTMP[:, w, j, :], in0=HBF[:, :, j, esrc], in1=SEL[:, w, :], op=ALU.mult),
                       reads=["HBF", "SEL"], writes=[("HTMP", w, j)])
        S.emit("dve", I("tensor_reduce", out=HALO[:], in_=HTMP[:], axis=AX.X, op=ALU.add),
               reads=[("HTMP", w, j) for w in range(2) for j in range(2)], writes=["HALO"])

    KALLK = ["KALLc"] + [("KALLg", r) for r in range(4)]
    VALLK = [("VALLc", kk) for kk in range(4)] + ["VALL1"] + [("VALLg", r, jj) for r in range(4) for jj in range(2)]

    def do_load_sample_keys(d):
        gt = gath_t[d]
        S.emit("pool", I("dma_start", out=KALL[:, :, 0:PAST], in_=ckT_d[d]),
               reads=[], writes=["KALLc"], kind="d", semkey="ck")
        for kk in range(4):
            S.emit("pool", I("dma_start", out=VALL[:, kk, :, 0:128],
                             in_=cv_d[d, kk * 128:(kk + 1) * 128, :].rearrange("p (h e) -> p h e", h=4)),
                   reads=[], writes=[("VALLc", kk)], kind="d", semkey=("cvv", kk))
        S.emit("dve", I("memset", ap=VALL[:, :, :, 128:129], constant=1.0), reads=[], writes=["VALL1"])
        for r in range(4):
            S.emit("sp", I("dma_start", out=KALL[:, :, PAST + r * 256:PAST + (r + 1) * 256],
                           in_=flat_ap(gt, r * BROWS * 512, [[256, 128], [128 * 256, 4], [1, 256]])),
                   reads=[("GATH", d)], writes=[("KALLg", r)], kind="d", semkey=("gk", r))
            for jj in range(2):
                S.emit("sp", I("dma_start", out=VALL[:, 4 + 2 * r + jj, :, 0:128],
                               in_=flat_ap(gt, r * BROWS * 512 + 131072 + jj * 65536, [[512, 128], [128, 4], [1, 128]])),
                       reads=[("GATH", d)], writes=[("VALLg", r, jj)], kind="d", semkey=("gv", r, jj))

    ETALL = rview(33360, 6144)

    def oblock(k):
        return 5 + k // 3, (k % 3) * 129

    def do_attention(d, qb, sample, filler=None):
        nkt = 12 if sample else 2
        q0 = qb * 256
        ebase = {}
        ekey = {}
        for h in range(4):
            if sample:
                ebase[h] = (0, 3072)
                ekey[h] = ([("ETS", i) for i in range(0, 3)], [("ETS", i) for i in range(3, 6)])
            else:
                sl = rot("ets", 6)
                ebase[h] = (sl * 1024, sl * 1024 + 512)
                ekey[h] = ([("ETS", sl)], [("ETS", sl)])

        def scores(h):
            for j in range(nkt // 2):
                spair = ((0, 1), (2, 3))[rot("sb", 2)]
                for kk in range(2):
                    kt = 2 * j + kk
                    for c in range(2):
                        if sample:
                            lhsT = KALL[c * 64:(c + 1) * 64, h, kt * 128:(kt + 1) * 128]
                            rk = list(KALLK)
                        else:
                            lhsT = KT[c * 64:(c + 1) * 64, h, q0 + kt * 128:q0 + (kt + 1) * 128]
                            rk = [("KT", h, qb)]
                        S.emit("pe", I("matmul", out=PS[:, spair[c], kk * 256:(kk + 1) * 256], lhsT=lhsT,
                                       rhs=QT[c * 64:(c + 1) * 64, h, q0:q0 + 256], start=True, stop=True),
                               reads=rk + [("QT", h, qb)], writes=[("PS", spair[c])], sig=(kk == 1))
                for c in range(2):
                    o0 = ebase[h][c] + j * 512
                    S.emit("act", I("activation", out=ETALL[:, o0:o0 + 512], in_=PS[:, spair[c], :],
                                    func=AF.Exp, scale=0.125),
                           reads=[("PS", spair[c])], writes=ekey[h][c])

        def batch(units):
            U = len(units)
            oi = rot("atto", 2)
            ATT_O = ATT_OB[oi]
            obanks = sorted(set(oblock(k)[0] for k in range(2 * U)))
            okeys = [("PS", bk) for bk in obanks]
            order = [(u, c) for c in range(2) for u in range(U)] if sample else [(u, c) for u in range(U) for c in range(2)]
            for (u, c) in order:
                h, qt = units[u]
                if True:
                    bk, col = oblock(u * 2 + c)
                    for kt in range(nkt):
                        if sample:
                            rhs = VALL[:, kt, h, 0:129]
                            rk = list(VALLK)
                        else:
                            rhs = VA[:, qb * 2 + kt, h, 0:129]
                            rk = [("VA", qb * 2 + kt), "VA1"]
                        e0 = ebase[h][c] + kt * 256 + qt * 128
                        last = (kt == nkt - 1)
                        S.emit("pe", I("matmul", out=PS[:, bk, col:col + 129], lhsT=ETALL[:, e0:e0 + 128], rhs=rhs,
                                       start=(kt == 0), stop=last),
                               reads=rk + ekey[h][c], writes=[("PS", bk)], sig=last)
            sm = ATT_S
            for bk in obanks:
                ks = [k for k in range(2 * U) if oblock(k)[0] == bk]
                n = len(ks)
                S.emit("dve", I("reciprocal", out=sm[:, ks[0]:ks[0] + n],
                                in_=PS[:, bk, 0:n * 129].rearrange("p (k e) -> p k e", e=129)[:, :, 128]),
                       reads=[("PS", bk)], writes=[("ATTS", "r", bk)])
            rkeys = [("ATTS", "r", bk) for bk in obanks]
            S.emit("dve", I("tensor_scalar", out=sm[:, 8:8 + U], in0=sm[:, 1:2 * U:2], scalar1=LAM[:, d, 0:1],
                            scalar2=None, op0=ALU.mult),
                   reads=rkeys + [("LAM", d, 0)], writes=[("ATTS", "n")])
            for u in range(U):
                bk, col = oblock(u * 2)
                S.emit("dve", I("tensor_scalar", out=ATT_A[:, u, :], in0=PS[:, bk, col:col + 128],
                                scalar1=sm[:, 2 * u:2 * u + 1], scalar2=None, op0=ALU.mult),
                       reads=[("PS", bk)] + rkeys, writes=[("ATTA", u)])
            for u in range(U):
                bk, col = oblock(u * 2 + 1)
                S.emit("dve", I("scalar_tensor_tensor", out=ATT_A[:, u, :], in0=PS[:, bk, col:col + 128],
                                scalar=sm[:, 8 + u:9 + u], in1=ATT_A[:, u, :], op0=ALU.mult, op1=ALU.add),
                       reads=[("PS", bk), ("ATTS", "n"), ("ATTA", u)], writes=[("ATTA", u)])
            akeys = [("ATTA", u) for u in range(U)]
            jn = rot("ntmp", 3)
            ATT_J = NTMP[jn]
            S.emit("dve", I("tensor_tensor", out=ATT_J[:, 0:U * 128], in0=ATT_A[:, 0:U, :].rearrange("p u e -> p (u e)"),
                            in1=ATT_A[:, 0:U, :].rearrange("p u e -> p (u e)"), op=ALU.mult),
                   reads=akeys, writes=[("NTMP", jn)])
            S.emit("dve", I("tensor_reduce", out=sm[:, 16:16 + U], in_=ATT_J[:, 0:U * 128].rearrange("p (u e) -> p u e", u=U),
                            axis=AX.X, op=ALU.add),
                   reads=[("NTMP", jn)], writes=[("ATTS", "q")])
            S.emit("act", I("activation", out=sm[:, 20:20 + U], in_=sm[:, 16:16 + U], func=AF.Ln, bias=EPSB[:, 0:1],
                            scale=1.0 / 128.0),
                   reads=[("ATTS", "q"), "EPSB"], writes=[("ATTS", "l")])
            S.emit("act", I("activation", out=sm[:, 24:24 + U], in_=sm[:, 20:20 + U], func=AF.Exp, scale=-0.5),
                   reads=[("ATTS", "l")], writes=[("ATTS", "s")])
            for u in range(U):
                S.emit("dve", I("scalar_tensor_tensor", out=ATT_O[:, u, :], in0=ATT_A[:, u, :], scalar=sm[:, 24 + u:25 + u],
                                in1=GSUB[:], op0=ALU.mult, op1=ALU.mult),
                       reads=[("ATTA", u), ("ATTS", "s"), "GSUB"], writes=[("ATTO", oi, u)])
            def fin():
                TPV = PS[:, 4, 0:256].bitcast(BF16)
                for u in range(U):
                    S.emit("pe", I("transpose", out=TPV[:, u * 128:(u + 1) * 128], in_=ATT_O[:, u, :], identity=IDB[:]),
                           reads=[("ATTO", oi, u), "IDB"], writes=[("PS", 4)], sig=(u == U - 1))
                if sample:
                    hh = units[0][0]
                    S.emit("act", I("activation", out=CAT[:, hh, q0:q0 + 256], in_=TPV[:, 0:256], func=AF.Copy),
                           reads=[("PS", 4)], writes=[("HT", hh, qb)])
                else:
                    qt0 = units[0][1]
                    t0 = q0 + qt0 * 128
                    S.emit("act", I("activation", out=CAT[:, 0:4, t0:t0 + 128],
                                    in_=TPV[:, 0:512].rearrange("p (h q) -> p h q", h=4), func=AF.Copy),
                           reads=[("PS", 4)], writes=[("HT", hh, qb) for hh in range(4)])
            return fin

        def push(fin):
            att_pending.append(fin)
            while len(att_pending) > 1:
                att_pending.pop(0)()

        if sample:
            for h in range(4):
                scores(h)
                push(batch([(h, 0), (h, 1)]))
                if filler is not None:
                    filler()
        else:
            for h in range(4):
                scores(h)
            for qt in range(2):
                push(batch([(h, qt) for h in range(4)]))
                if filler is not None:
                    filler()

    att_pending = []

    def att_flush():
        while att_pending:
            att_pending.pop(0)()

    def do_conv_prompt(d):
        for j in range(2):
            Z3 = ZU[:, j, 0:NPT].rearrange("p (s t) -> p s t", s=4)
            C3 = CAT[:, 4 + j, 0:NPT].rearrange("p (s t) -> p s t", s=4)
            zr = [("ZU", j, b) for b in range(4)]
            ck = [("HT", 4 + j, b) for b in range(4)]
            S.emit("dve", I("tensor_scalar", out=CONVY[:, 0:NPT], in0=ZU[:, j, 0:NPT], scalar1=CW[:, d, j, 1:2],
                            scalar2=CW[:, d, j, 3:4], op0=ALU.mult, op1=ALU.add),
                   reads=zr + ["CW"], writes=["CONVY", ("RSTD", 0), ("RSTD", 1)])
            Y3 = CONVY[:, 0:NPT].rearrange("p (s t) -> p s t", s=4)
            S.emit("dve", I("scalar_tensor_tensor", out=Y3[:, :, 1:256], in0=Z3[:, :, 0:255], scalar=CW[:, d, j, 0:1],
                            in1=Y3[:, :, 1:256], op0=ALU.mult, op1=ALU.add),
                   reads=zr + ["CW", "CONVY"], writes=["CONVY"])
            S.emit("dve", I("scalar_tensor_tensor", out=Y3[:, :, 0:255], in0=Z3[:, :, 1:256], scalar=CW[:, d, j, 2:3],
                            in1=Y3[:, :, 0:255], op0=ALU.mult, op1=ALU.add),
                   reads=zr + ["CW", "CONVY"], writes=["CONVY"])
            S.emit("dve", I("tensor_tensor", out=CAT[:, 4 + j, 0:NPT], in0=CONVY[:, 0:NPT], in1=G[:, j, 0:NPT], op=ALU.mult),
                   reads=["CONVY"] + [("G", j, b) for b in range(4)], writes=ck)

    def do_conv(d, blk):
        t0 = blk * 256
        sample = blk == 4
        for j in range(2):
            ci = rot("ntmp", 3)
            Y = NTMP[ci][:, 0:256]
            zr = [("ZU", j, blk)]
            S.emit("dve", I("tensor_scalar", out=Y[:], in0=ZU[:, j, t0:t0 + 256], scalar1=CW[:, d, j, 1:2],
                            scalar2=CW[:, d, j, 3:4], op0=ALU.mult, op1=ALU.add),
                   reads=zr + ["CW"], writes=[("NTMP", ci)])
            S.emit("dve", I("scalar_tensor_tensor", out=Y[:, 1:256], in0=ZU[:, j, t0:t0 + 255], scalar=CW[:, d, j, 0:1],
                            in1=Y[:, 1:256], op0=ALU.mult, op1=ALU.add),
                   reads=zr + ["CW", ("NTMP", ci)], writes=[("NTMP", ci)])
            S.emit("dve", I("scalar_tensor_tensor", out=Y[:, 0:255], in0=ZU[:, j, t0 + 1:t0 + 256], scalar=CW[:, d, j, 2:3],
                            in1=Y[:, 0:255], op0=ALU.mult, op1=ALU.add),
                   reads=zr + ["CW", ("NTMP", ci)], writes=[("NTMP", ci)])
            if sample:
                S.emit("dve", I("scalar_tensor_tensor", out=Y[:, 0:1], in0=HALO[:, 0, j:j + 1], scalar=CW[:, d, j, 0:1],
                                in1=Y[:, 0:1], op0=ALU.mult, op1=ALU.add),
                       reads=["HALO", "CW", ("NTMP", ci)], writes=[("NTMP", ci)])
                S.emit("dve", I("scalar_tensor_tensor", out=Y[:, 255:256], in0=HALO[:, 1, j:j + 1], scalar=CW[:, d, j, 2:3],
                                in1=Y[:, 255:256], op0=ALU.mult, op1=ALU.add),
                       reads=["HALO", "CW", ("NTMP", ci)], writes=[("NTMP", ci)])
            S.emit("dve", I("tensor_tensor", out=CAT[:, 4 + j, t0:t0 + 256], in0=Y[:], in1=G[:, j, t0:t0 + 256], op=ALU.mult),
                   reads=[("NTMP", ci), ("G", j, blk)], writes=[("HT", 4 + j, blk)])

    def do_cmlp(d, blk):
        for half in range(2):
            tk = blk * 2 + half
            t0 = tk * 128
            for pair in range(2):
                bank = mmbank()
                S.emit("pe", I("matmul", out=PS[:, bank, 0:256], lhsT=VS[:, tk, pair * 128:(pair + 1) * 128],
                               rhs=WST[:, d, pair * 2:pair * 2 + 2, :].rearrange("q g p -> q (g p)"),
                               start=True, stop=True),
                       reads=[("VS", tk), "WST"], writes=[("PS", bank)], sig=True)
                ci = rot("ntmp", 3)
                for gi in range(2):
                    S.emit("dve", I("tensor_tensor", out=NTMP[ci][gi * 64:gi * 64 + 64, 0:128],
                                    in0=PS[gi * 64:gi * 64 + 64, bank, gi * 128:(gi + 1) * 128],
                                    in1=BSB[gi * 64:gi * 64 + 64, d, pair, :], op=ALU.add),
                           reads=[("PS", bank), "BSB"], writes=[("NTMP", ci, gi)])
                S.emit("dve", I("tensor_tensor", out=CAT[:, 6 + pair, t0:t0 + 128], in0=NTMP[ci][:, 0:128],
                                in1=ZU[:, 2 + pair, t0:t0 + 128], op=ALU.mult),
                       reads=[("NTMP", ci, 0), ("NTMP", ci, 1), ("ZU", 2 + pair, blk)], writes=[("HT", 6 + pair, blk)])

    def resid_evac(bank, oc, ti, d, which, rng=None):
        t0, n = TT[ti] if rng is None else rng
        v = 0 if t0 < NPT else 1
        bl = blks(t0, n)
        S.emit("dve", I("scalar_tensor_tensor", out=X[:, oc, t0:t0 + n], in0=PS[:, bank, 0:n],
                        scalar=mod_ap(d, which, oc, v), in1=X[:, oc, t0:t0 + n], op0=ALU.mult, op1=ALU.add),
               reads=[("PS", bank)] + mod_key(d, which) + [("X", oc, b) for b in bl], writes=[("X", oc, b) for b in bl])

    def out_groups(d, pis, rng):
        slots = [wslot(pis[0], la=NSLOT - 1), wslot(pis[1], la=NSLOT - 2)]
        res = []
        for j in range(2):
            wv = wview(slots[j], 8, 512)
            for fc in range(4):
                def g(j=j, fc=fc, wv=wv):
                    oc = j * 4 + fc
                    t0, n = rng
                    bank = mmbank()
                    for kc in range(8):
                        S.emit("pe", I("matmul", out=PS[:, bank, 0:n], lhsT=wv[:, kc, fc * 128:(fc + 1) * 128],
                                       rhs=CAT[:, kc, t0:t0 + n], start=(kc == 0), stop=(kc == 7)),
                               reads=wkeys(slots[j]) + [("HT", kc, b) for b in blks(t0, n)], writes=[("PS", bank)],
                               sig=(kc == 7))
                    resid_evac(bank, oc, 0, d, 2, rng=rng)
                res.append(g)
        return res

    def do_out_piece(d, j, pi, tis=(0, 1, 2)):
        slot = wslot(pi, la=NSLOT - 1 - j)
        wv = wview(slot, 8, 512)
        for ti in tis:
            for fc in range(4):
                oc = j * 4 + fc
                t0, n = TT[ti]
                bank = mmbank()
                for kc in range(8):
                    S.emit("pe", I("matmul", out=PS[:, bank, 0:n], lhsT=wv[:, kc, fc * 128:(fc + 1) * 128],
                                   rhs=CAT[:, kc, t0:t0 + n], start=(kc == 0), stop=(kc == 7)),
                           reads=wkeys(slot) + [("HT", kc, b) for b in blks(t0, n)], writes=[("PS", bank)], sig=(kc == 7))
                resid_evac(bank, oc, ti, d, 2)

    def do_up_piece(d, j, pi):
        slot = wslot(pi)
        wv = wview(slot, 8, 512)
        for ti in range(3):
            for fc in range(4):
                hc = j * 4 + fc
                t0, n = TT[ti]
                bank = mmbank()
                ws_matmul_fm(slot, wv, fc, ti, bank)
                nt = rot("ntmp", 3)
                S.emit("act", I("activation", out=NTMP[nt][:, 0:n], in_=PS[:, bank, 0:n], func=AF.Relu),
                       reads=[("PS", bank)], writes=[("NTMP", nt)])
                S.emit("dve", I("tensor_tensor", out=HID[:, hc, t0:t0 + n], in0=NTMP[nt][:, 0:n], in1=NTMP[nt][:, 0:n],
                                op=ALU.mult),
                       reads=[("NTMP", nt)], writes=[("HID", hc, b) for b in blks(t0, n)])

    stat_pending = []

    def do_down_piece(d, j, pi):
        slot = wslot(pi)
        wv = wview(slot, 32, 128)
        for ti in range(3):
            t0, n = TT[ti]
            bank = mmbank()
            for kc in range(32):
                S.emit("pe", I("matmul", out=PS[:, bank, 0:n], lhsT=wv[:, kc, :], rhs=HID[:, kc, t0:t0 + n],
                               start=(kc == 0), stop=(kc == 31)),
                       reads=wkeys(slot) + [("HID", kc, b) for b in blks(t0, n)], writes=[("PS", bank)], sig=(kc == 31))
            resid_evac(bank, j, ti, d, 5)
            stat_pending.append((j, ti))
            while len(stat_pending) > (0 if (j == 7 and ti == 2) else 2):
                stat_accum(*stat_pending.pop(0))

    REGION_MIX = ([("QT", c, b) for c in range(4) for b in range(5)] + [("KT", c, b) for c in range(4) for b in range(5)]
                  + [("VA", k) for k in range(10)] + ["VA1"] + [("G", c, b) for c in range(4) for b in range(5)]
                  + [("ZU", c, b) for c in range(4) for b in range(5)] + [("VS", k) for k in range(10)]
                  + [("ETS", i) for i in range(6)] + KALLK + VALLK)
    REGION_HID = [("HID", c, b) for c in range(32) for b in range(5)]
    GZ_KEYS = [("G", c, b) for c in range(4) for b in range(5)] + [("ZU", c, b) for c in range(4) for b in range(5)]

    def fence(keys):
        S.emit("dve", I("memset", ap=FEN[:, 0:1], constant=0.0), reads=[], writes=list(keys))

    for (kind, d, j, pi) in plan:
        if kind == "mod":
            do_mod_piece(d, j, pi)
            continue
        if kind == "in_g":
            load(GSUB[:], gsub_d[:, d, :], "GSUB")
            S.emit("dve", I("tensor_scalar", out=GSUB[:], in0=GSUB[:], scalar1=(1.0 - lam_inits[d]),
                            scalar2=None, op0=ALU.mult),
                   reads=["GSUB"], writes=["GSUB"])
            do_scale_prep(d, 1, GMIX, "GMIX", S1, "S1")
            fence(REGION_MIX + REGION_HID)
            do_norm(d, S1, "S1", 0, have_stats=(d > 0))
            S.emit("dve", I("memset", ap=VA[:, :, :, 128:129], constant=1.0), reads=[], writes=["VA1"])
        if kind.startswith("in_"):
            mm_nb[0] = 8
            do_in_piece(kind[3:], d, pi)
            mm_nb[0] = 4
            if kind == "in_v":
                do_exchange(d)
            continue
        if kind == "mix":
            continue
        if kind == "out":
            if j == 0:
                out_pis = [pi]
                continue
            out_pis.append(pi)
            gb = [out_groups(d, out_pis, (b * 256, 256)) for b in range(5)]

            def fill_from(lst, k):
                def f():
                    for _ in range(k):
                        if lst:
                            lst.pop(0)()
                return f

            sched = [None, (0, 4), (0, 4), (1, 4), (1, 4), (2, 4), (2, 4), (3, 3), (3, 3), (3, 2)]
            fidx = [0]

            def filler():
                e = sched[fidx[0]] if fidx[0] < len(sched) else None
                fidx[0] += 1
                if e is not None:
                    fill_from(gb[e[0]], e[1])()

            do_attention(d, 0, False)
            do_attention(d, 1, False, filler=filler)
            do_halo(d)
            do_conv(d, 4)
            fence(GZ_KEYS + KALLK + VALLK)
            do_load_sample_keys(d)
            do_attention(d, 2, False, filler=filler)
            do_scale_prep(d, 4, GMLP, "GMLP", S2, "S2")
            do_attention(d, 3, False, filler=filler)
            att_flush()
            fill_from(gb[0], 8)()
            fill_from(gb[1], 8)()
            do_norm(d, S2, "S2", 3, tis=(0,))
            do_attention(d, 4, True, filler=filler)
            for b_ in range(4):
                fill_from(gb[b_], 8)()
            att_flush()
            fill_from(gb[4], 8)()
            do_norm(d, S2, "S2", 3, tis=(1,))
            do_norm(d, S2, "S2", 3, tis=(2,))
            fence(REGION_MIX + REGION_HID)
            continue
        if kind == "up":
            mm_nb[0] = 8
            do_up_piece(d, j, pi)
            mm_nb[0] = 4
            continue
        if kind == "down":
            mm_nb[0] = 5
            do_down_piece(d, j, pi)
            mm_nb[0] = 4
            continue

    for ti, (t0, n) in enumerate(TT):
        bank = 5 + ti
        bl = blks(t0, n)
        rs = rot("rstd", 2)
        S.emit("act", I("activation", out=RSTD[rs][:, 0:n], in_=PS[:, bank, 0:n], func=AF.Ln, bias=EPSB[:, 0:1], scale=1.0),
               reads=[("PS", bank), "EPSB"], writes=[("RSTD", rs)])
        S.emit("act", I("activation", out=RSTD[rs][:, 0:n], in_=RSTD[rs][:, 0:n], func=AF.Exp, scale=-0.5),
               reads=[("RSTD", rs)], writes=[("RSTD", rs)])
        for fc in range(8):
            S.emit("dve", I("scalar_tensor_tensor", out=X[:, fc, t0:t0 + n], in0=X[:, fc, t0:t0 + n],
                            scalar=GFIN[:, fc:fc + 1], in1=RSTD[rs][:, 0:n], op0=ALU.mult, op1=ALU.mult),
                   reads=[("X", fc, b) for b in bl] + ["GFIN", ("RSTD", rs)], writes=[("X", fc, b) for b in bl])
    for fc in range(8):
        S.emit("sp", I("dma_start", out=yT_d[:, fc, :], in_=X[:, fc, :]),
               reads=[("X", fc, b) for b in range(5)], writes=[], kind="d", semkey=("yo", fc))

    S.run(nc)
    st.close()
    return nc


def _rope_tables_np(pos0, n):
    t = np.arange(pos0, pos0 + n)
    row = (t // 64).astype(np.float64)
    col = (t % 64).astype(np.float64)
    inv = 10000.0 ** (-np.arange(0, 32, 2, dtype=np.float64) / 32.0)
    ar = row[:, None] * inv[None, :]
    ac = col[:, None] * inv[None, :]
    cosT = np.zeros((128, n), np.float32)
    sinS = np.zeros((128, n), np.float32)
    for p in range(128):
        e = p % 64
        ax = e // 32
        i = e % 32
        first = i < 16
        ang = (ar if ax == 0 else ac)[:, i % 16]
        cosT[p] = np.cos(ang)
        sinS[p] = (-np.sin(ang)) if first else np.sin(ang)
    return cosT, sinS


def _perm_np():
    P = np.zeros((128, 128), np.float32)
    for m in range(128):
        i = m % 32
        partner = m + 16 if i < 16 else m - 16
        P[partner, m] = 1.0
    return P


_NC_CACHE = {}


def _prepare(x_prompt, x_sample, cache_k, cache_v, c, c_ctx, w_mod, b_mod, norm_mix, norm_mlp,
             w_in, lam_q1, lam_k1, lam_q2, lam_k2, subln, conv_w, conv_b, w_s, b_s,
             w_out, w_up, w_down, norm_final):
    f = lambda a: np.ascontiguousarray(np.asarray(a, dtype=np.float32))
    x_prompt, x_sample, cache_k, cache_v, c, c_ctx = map(f, (x_prompt, x_sample, cache_k, cache_v, c, c_ctx))
    w_mod, b_mod, norm_mix, norm_mlp, w_in = map(f, (w_mod, b_mod, norm_mix, norm_mlp, w_in))
    lam_q1, lam_k1, lam_q2, lam_k2, subln = map(f, (lam_q1, lam_k1, lam_q2, lam_k2, subln))
    conv_w, conv_b, w_s, b_s, w_out, w_up, w_down, norm_final = map(
        f, (conv_w, conv_b, w_s, b_s, w_out, w_up, w_down, norm_final))
    depth = w_in.shape[0]
    bmod = np.ascontiguousarray(b_mod.reshape(depth, 48, 128).transpose(2, 0, 1))
    gmix = np.ascontiguousarray(norm_mix.reshape(depth, 8, 128).transpose(2, 0, 1))
    gmlp = np.ascontiguousarray(norm_mlp.reshape(depth, 8, 128).transpose(2, 0, 1))
    gfin = np.ascontiguousarray(norm_final.reshape(8, 128).T)
    lamv = np.ascontiguousarray(np.broadcast_to(np.stack([lam_q1, lam_k1, lam_q2, lam_k2], axis=1)[None], (128, depth, 4, 64)))
    gsub = np.ascontiguousarray(np.broadcast_to(subln[None], (128, depth, 128)))
    cw = np.zeros((128, depth, 2, 4), np.float32)
    cw[:, :, :, 0:3] = conv_w.reshape(depth, 3, 2, 128).transpose(3, 0, 2, 1)
    cw[:, :, :, 3] = conv_b.reshape(depth, 2, 128).transpose(2, 0, 1)
    wsT = np.ascontiguousarray(w_s.transpose(3, 0, 1, 2))
    bsB = np.zeros((128, depth, 2, 128), np.float32)
    for pair in range(2):
        for gi in range(2):
            bsB[gi * 64:(gi + 1) * 64, :, pair, :] = b_s[:, pair * 2 + gi, :][None]
    perm = _perm_np()
    ident = np.eye(128, dtype=np.float32)
    in_maps = []
    for core in range(8):
        s = core // 4
        r = core % 4
        xtok = np.concatenate([x_prompt[4 * core:4 * core + 4].reshape(1024, D),
                               x_sample[s, r * 256:(r + 1) * 256]], axis=0)
        xT = np.ascontiguousarray(xtok.reshape(T, 8, 128).transpose(2, 1, 0))
        ckT = np.ascontiguousarray(cache_k[s, :depth].transpose(0, 3, 2, 1))
        cv = np.ascontiguousarray(cache_v[s, :depth].reshape(depth, PAST, 512))
        cvec = np.ascontiguousarray(np.stack([c_ctx, c[s]], axis=-1).reshape(8, 128, 2).transpose(1, 0, 2))
        cosT, sinS = _rope_tables_np(r * 256, 256)
        sel = np.zeros((128, 2, 4), np.float32)
        if r > 0:
            sel[:, 0, r - 1] = 1.0
        if r < 3:
            sel[:, 1, r + 1] = 1.0
        in_maps.append({
            "xT": xT, "ckT": ckT, "cv": cv, "cvec": cvec,
            "w_mod": w_mod, "w_in": w_in, "w_out": w_out, "w_up": w_up, "w_down": w_down,
            "bmod": bmod, "gmix": gmix, "gmlp": gmlp, "gfin": gfin, "lamv": lamv, "gsub": gsub,
            "cw": cw, "wsT": wsT, "bsB": bsB, "cosT": cosT, "sinS": sinS, "perm": perm, "ident": ident,
            "sel": sel,
        })
    return depth, in_maps


def _assemble(outs, depth):
    y_prompt = np.zeros((32, 256, D), np.float32)
    y_sample = np.zeros((2, 1024, D), np.float32)
    new_k = np.zeros((32, depth, 256, 4, 128), np.float32)
    new_v = np.zeros((32, depth, 256, 4, 128), np.float32)
    for core in range(8):
        s = core // 4
        r = core % 4
        ytok = np.asarray(outs[core]["yT"]).transpose(2, 1, 0).reshape(T, D)
        y_prompt[4 * core:4 * core + 4] = ytok[:1024].reshape(4, 256, D)
        y_sample[s, r * 256:(r + 1) * 256] = ytok[1024:]
        new_k[4 * core:4 * core + 4] = np.asarray(outs[core]["nk"]).reshape(4, depth, 256, 4, 128)
        new_v[4 * core:4 * core + 4] = np.asarray(outs[core]["nv"]).reshape(4, depth, 256, 4, 128)
    return (y_prompt, y_sample, new_k, new_v)


def kernel(**inputs):
    depth, in_maps = _prepare(**inputs)
    if depth not in _NC_CACHE:
        _NC_CACHE[depth] = build_nc(depth)
    nc = _NC_CACHE[depth]
    res = run_bass_kernel_spmd(nc, in_maps, core_ids=list(range(8)))
    return _assemble(res.results, depth)
```

```python
import math
import numpy as np
import concourse.bass as bass
import concourse.mybir as mybir
from concourse.bass_utils import run_bass_kernel_spmd

F32 = mybir.dt.float32
BF16 = mybir.dt.bfloat16
AF = mybir.ActivationFunctionType
ALU = mybir.AluOpType
AX = mybir.AxisListType

D = 1024
DEPTH = 4
T = 1280
NPT = 1024
NST = 256
PAST = 512
NKS = PAST + 1024
EPS = 1e-6
TT = [(0, 512), (512, 512), (1024, 256)]
NSLOT = 4
SLOT_ELEMS = 4096
BROWS = 513


def blks(a, n):
    return list(range(a // 256, (a + n + 255) // 256))


class Op:
    __slots__ = ("q", "fn", "deps", "kind", "sig", "sem", "val", "inc", "idx")


class Sched:
    QS = ("pe", "act", "dve", "pool", "sp")

    def __init__(self):
        self.ops = {q: [] for q in self.QS}
        self.res = {}
        self.dma_keys = {}
        self.cc_count = 0

    def emit(self, q, fn, reads=(), writes=(), kind="c", sig=True, semkey=None):
        op = Op()
        op.q = q
        op.fn = fn
        op.kind = kind
        op.sig = sig
        op.sem = None
        op.val = None
        op.inc = 1
        deps = {}
        for r in reads:
            e = self.res.get(r)
            if e is not None and e[0] is not None:
                deps[id(e[0])] = (e[0], True)
        for w in writes:
            e = self.res.get(w)
            if e is not None:
                if e[0] is not None and id(e[0]) not in deps:
                    deps[id(e[0])] = (e[0], False)
                for rd in e[1]:
                    if id(rd) not in deps:
                        deps[id(rd)] = (rd, False)
        dl = []
        for dep, raw in deps.values():
            if dep is op:
                continue
            if dep.q == q and dep.kind == "c" and kind == "c":
                if q == "pe":
                    continue
            dl.append(dep)
        if kind == "d":
            ent = self.dma_keys.setdefault(semkey, [0, None])
            if ent[1] is not None:
                dl.append(ent[1])
            ent[0] += 1
            ent[1] = op
            op.sem = ("dma", semkey)
            op.val = ent[0] * 16
        elif kind == "cc":
            self.cc_count += 1
            op.sem = ("cc", 0)
            op.val = self.cc_count
        op.deps = dl
        for r in reads:
            e = self.res.setdefault(r, [None, []])
            e[1].append(op)
        for w in writes:
            self.res[w] = [op, []]
        op.idx = len(self.ops[q])
        self.ops[q].append(op)
        return op

    def finalize(self):
        for q in self.QS:
            ops = [o for o in self.ops[q] if o.kind == "c"]
            if ops:
                ops[-1].sig = True
            cnt = 0
            for o in ops:
                if o.sig:
                    cnt += 1
                    o.val = cnt
                o.sem = ("q", q)
            nxt = None
            for o in reversed(ops):
                if o.sig:
                    nxt = o.val
                else:
                    o.val = nxt

    def run(self, nc):
        self.finalize()
        names = [("q", q) for q in self.QS] + [("dma", k) for k in self.dma_keys] + [("cc", 0)]
        from contextlib import ExitStack
        with ExitStack() as st:
            sems = {}
            for i, n in enumerate(names):
                sems[n] = st.enter_context(nc.semaphore("s%d" % i))
            block = st.enter_context(nc.Block())
            handles = {"pe": block.tensor, "act": block.scalar, "dve": block.vector,
                       "pool": block.gpsimd, "sp": block.sync}
            for q in self.QS:
                ops = self.ops[q]

                def body(eng, ops=ops, q=q):
                    waited = {}
                    for o in ops:
                        for dpn in o.deps:
                            if waited.get(dpn.sem, 0) < dpn.val:
                                eng.wait_ge(sems[dpn.sem], dpn.val)
                                waited[dpn.sem] = dpn.val
                        ins = o.fn(eng)
                        if o.kind == "c":
                            if o.sig:
                                ins.then_inc(sems[o.sem], 1)
                        elif o.kind == "d":
                            ins.then_inc(sems[o.sem], 16)
                        else:
                            ins.then_inc(sems[o.sem], 1)
                    if q == "sp":
                        for k, ent in self.dma_keys.items():
                            if waited.get(("dma", k), 0) < ent[0] * 16:
                                eng.wait_ge(sems[("dma", k)], ent[0] * 16)
                        for qq in self.QS:
                            cops = [o for o in self.ops[qq] if o.kind == "c"]
                            if cops:
                                eng.wait_ge(sems[("q", qq)], cops[-1].val)

                handles[q](body)


def I(method, **kw):
    return lambda e: getattr(e, method)(**kw)


def build_nc(depth=DEPTH):
    nc = bass.Bass("TRN2", target_bir_lowering=False)
    S = Sched()

    def din(name, shape, dt=F32):
        return nc.dram_tensor(name, list(shape), dt, kind="ExternalInput").ap()

    xT_d = din("xT", [128, 8, T])
    ckT_d = din("ckT", [depth, 128, 4, PAST])
    cv_d = din("cv", [depth, PAST, 512])
    cvec_d = din("cvec", [128, 8, 2])
    w_mod_d = din("w_mod", [depth, D, 6 * D])
    w_in_d = din("w_in", [depth, D, 2816])
    w_out_d = din("w_out", [depth, D, D])
    w_up_d = din("w_up", [depth, D, 4 * D])
    w_down_d = din("w_down", [depth, 4 * D, D])
    bmod_d = din("bmod", [128, depth, 48])
    gmix_d = din("gmix", [128, depth, 8])
    gmlp_d = din("gmlp", [128, depth, 8])
    gfin_d = din("gfin", [128, 8])
    lamv_d = din("lamv", [128, depth, 4, 64])
    gsub_d = din("gsub", [128, depth, 128])
    cw_d = din("cw", [128, depth, 2, 4])
    wsT_d = din("wsT", [128, depth, 4, 128])
    bsB_d = din("bsB", [128, depth, 2, 128])
    cosT_d = din("cosT", [128, NST])
    sinS_d = din("sinS", [128, NST])
    perm_d = din("perm", [128, 128])
    ident_d = din("ident", [128, 128])
    sel_d = din("sel", [128, 2, 4])

    yT_d = nc.dram_tensor("yT", [128, 8, T], F32, kind="ExternalOutput").ap()
    nk_d = nc.dram_tensor("nk", [4, depth, 256, 512], F32, kind="ExternalOutput").ap()
    nv_d = nc.dram_tensor("nv", [4, depth, 256, 512], F32, kind="ExternalOutput").ap()
    bounce_t = [nc.dram_tensor("bounce%d" % d, [BROWS, 512], BF16, kind="Internal") for d in range(depth)]
    gath_t = [nc.dram_tensor("gath%d" % d, [4 * BROWS, 512], BF16, kind="Internal") for d in range(depth)]

    from contextlib import ExitStack
    st = ExitStack()

    def sb(name, shape, dt):
        return st.enter_context(nc.sbuf_tensor(name, list(shape), dt))

    X = sb("X", [128, 8, T], F32)
    HT = sb("HT", [128, 8, T], BF16)
    WS = sb("WS", [128, NSLOT, SLOT_ELEMS], BF16)
    R = sb("R", [128, 40960], BF16)
    PS = st.enter_context(nc.psum_tensor("PS", [128, 8, 512], F32))

    def rview(off, n, pat=None, dt=None, **kw):
        v = R[:, off:off + n]
        if dt is not None:
            v = v.bitcast(dt)
        if pat is not None:
            v = v.rearrange(pat, **kw)
        return v

    HID = rview(0, 40960, "p (c t) -> p c t", c=32)
    QT = rview(0, 5120, "p (c t) -> p c t", c=4)
    KT = rview(5120, 5120, "p (c t) -> p c t", c=4)
    VA = rview(10240, 5200, "p (k h e) -> p k h e", k=10, h=4)
    G = rview(15440, 5120, "p (c t) -> p c t", c=4)
    ZU = rview(20560, 10240, "p (c t) -> p c t", dt=F32, c=4)
    VS = rview(30800, 2560, "p (k e) -> p k e", k=10)
    KALL = rview(15440, 6144, "p (h k) -> p h k", h=4)
    VALL = rview(21584, 6240, "p (k h e) -> p k h e", k=12, h=4)
    CAT = HT
    LAMV = rview(33360, depth * 512, "p (d a e) -> p d a e", dt=F32, d=depth, a=4)
    LTMP = rview(36432, 512, "p (a e) -> p a e", dt=F32, a=4)

    MOD = [sb("MOD%d" % i, [128, 48, 2], F32) for i in range(2)]
    S1 = sb("S1", [128, 8, 2], F32)
    S2 = sb("S2", [128, 8, 2], F32)
    SC = sb("SC", [128, 8, 2], BF16)
    CV = sb("CVEC", [128, 8, 2], F32)
    BMOD = sb("BMOD", [128, depth, 48], F32)
    GMIX = sb("GMIX", [128, depth, 8], F32)
    GMLP = sb("GMLP", [128, depth, 8], F32)
    GFIN = sb("GFIN", [128, 8], F32)
    GSUB = sb("GSUB", [128, 128], F32)
    CW = sb("CW", [128, depth, 2, 4], F32)
    WST = sb("WST", [128, depth, 4, 128], BF16)
    BSB = sb("BSB", [128, depth, 2, 128], F32)
    COST = sb("COST", [128, NST], F32)
    SINS = sb("SINS", [128, NST], F32)
    PERM = sb("PERM", [128, 128], F32)
    IDB = sb("IDB", [128, 128], BF16)
    ONES = sb("ONES", [128, 128], BF16)
    SEL = sb("SEL", [128, 2, 4], F32)
    LAM = sb("LAM", [128, depth, 4], F32)
    SQ = [sb("SQ%d" % i, [128, 512], BF16) for i in range(3)]
    RSTDT = sb("RSTD", [128, 2, 512], F32)
    RSTD = [RSTDT[:, 0, :], RSTDT[:, 1, :]]
    CONVY = RSTDT[:].rearrange("p a n -> p (a n)")
    NTMP = [sb("NTMP%d" % i, [128, 512], F32) for i in range(3)]
    STGT = sb("STG", [128, 2, 512], F32)
    STG = [STGT[:, 0, :], STGT[:, 1, :]]
    CONVT = STGT[:].rearrange("p a n -> p (a n)")
    ZB = sb("ZB", [128, 2, 2], BF16)
    HB = sb("HB", [128, 4, 2, 2], BF16)
    HBF = sb("HBF", [128, 4, 2, 2], F32)
    HTMP = sb("HTMP", [128, 2, 2, 4], F32)
    HALO = sb("HALO", [128, 2, 2], F32)
    ATT_A = sb("ATTA", [128, 4, 128], F32)
    ATT_OB = [sb("ATTO%d" % i, [128, 4, 128], BF16) for i in range(2)]
    ATT_S = sb("ATTS", [128, 32], F32)
    FEN = sb("FEN", [128, 2], F32)
    EPSB = sb("EPSB", [128, 1], F32)

    rr = {}

    def rot(name, n):
        v = rr.get(name, 0)
        rr[name] = v + 1
        return v % n

    mm_nb = [4]

    def mmbank():
        v = rr.get("mm", 0)
        rr["mm"] = v + 1
        return v % mm_nb[0]

    pieces = []
    issued = [0]

    def add_piece(src_list):
        pieces.append(src_list)
        return len(pieces) - 1

    def ensure_issued(upto):
        while issued[0] <= min(upto, len(pieces) - 1):
            j = issued[0]
            slot = j % NSLOT
            plist = pieces[j]
            for hi, (kc0, kcn, cols, src) in enumerate(plist):
                dst = WS[:, slot, kc0 * cols:(kc0 + kcn) * cols].rearrange("p (k n) -> p k n", k=kcn)
                wkeys = [("WS", slot, hi)] if len(plist) == 2 else [("WS", slot, 0), ("WS", slot, 1)]
                S.emit("pool", I("dma_start", out=dst, in_=src), reads=[], writes=wkeys,
                       kind="d", semkey=("ws", slot, hi))
            issued[0] += 1

    def wslot(pi, la=NSLOT - 1):
        ensure_issued(pi + la)
        return pi % NSLOT

    def wkeys(slot):
        return [("WS", slot, 0), ("WS", slot, 1)]

    def wview(slot, kcn, cols):
        return WS[:, slot, 0:kcn * cols].rearrange("p (k n) -> p k n", k=kcn)

    def wsrc(w, c0, cols):
        return w.rearrange("(k p) n -> p k n", p=128)[:, :, c0:c0 + cols]

    plan = []
    WIN_ORDER = [(1536, 512, "g"), (2048, 512, "xu"), (512, 512, "k"), (1024, 512, "v"),
                 (0, 512, "q"), (2560, 256, "vs")]

    def mod_piece(d, j):
        pi = add_piece([(0, 8, 512, wsrc(w_mod_d[d], j * 512, 512))])
        plan.append(("mod", d, j, pi))

    for j in range(4):
        mod_piece(0, j)
    pend0 = list(range(4, 12))
    for d in range(depth):
        for wi, (c0, cols, nm) in enumerate(WIN_ORDER):
            pi = add_piece([(0, 8, cols, wsrc(w_in_d[d], c0, cols))])
            plan.append(("in_" + nm, d, 0, pi))
            if d == 0:
                for _ in range(2 if wi < 2 else 1):
                    if pend0:
                        mod_piece(0, pend0.pop(0))
        plan.append(("mix", d, 0, -1))
        for j in range(2):
            pi = add_piece([(0, 8, 512, wsrc(w_out_d[d], j * 512, 512))])
            plan.append(("out", d, j, pi))
        for j in range(8):
            pi = add_piece([(0, 8, 512, wsrc(w_up_d[d], j * 512, 512))])
            plan.append(("up", d, j, pi))
            if d + 1 < depth:
                mod_piece(d + 1, j)
        for j in range(8):
            src = w_down_d[d].rearrange("(k p) n -> p k n", p=128)
            pi = add_piece([(0, 16, 128, src[:, 0:16, j * 128:(j + 1) * 128]),
                            (16, 16, 128, src[:, 16:32, j * 128:(j + 1) * 128])])
            plan.append(("down", d, j, pi))
            if d + 1 < depth and j < 4:
                mod_piece(d + 1, 8 + j)

    def load(dst, src, key, q="sp", extra_w=()):
        S.emit(q, I("dma_start", out=dst, in_=src), reads=[], writes=[key] + list(extra_w),
               kind="d", semkey=key)

    load(CV[:], cvec_d, "CV")
    load(BMOD[:], bmod_d, "BMOD")
    for fc in range(8):
        load(X[:, fc, :], xT_d[:, fc, :], ("Xld", fc), extra_w=[("X", fc, b) for b in range(5)])
    load(GMIX[:], gmix_d, "GMIX")
    load(GMLP[:], gmlp_d, "GMLP")
    load(GFIN[:], gfin_d, "GFIN")
    load(LAMV[:], lamv_d, "LAMV")
    load(CW[:], cw_d, "CW")
    load(WST[:], wsT_d, "WST", q="pool")
    load(BSB[:], bsB_d, "BSB")
    load(COST[:], cosT_d, "COST")
    load(SINS[:], sinS_d, "SINS")
    load(PERM[:], perm_d, "PERM")
    load(IDB[:], ident_d, "IDB", q="pool")
    load(SEL[:], sel_d, "SEL")

    S.emit("dve", I("memset", ap=ONES[:], constant=1.0 / 1024.0), writes=["ONES"])
    S.emit("dve", I("memset", ap=EPSB[:], constant=EPS), writes=["EPSB"])
    S.emit("act", I("activation", out=SC[:], in_=CV[:], func=AF.Silu), reads=["CV"], writes=["SC"])
    lam_inits = [0.8 - 0.6 * math.exp(-0.3 * d) for d in range(depth)]
    for d in range(depth):
        for i in range(2):
            S.emit("dve", I("tensor_tensor", out=LTMP[:, i, :], in0=LAMV[:, d, 2 * i, :],
                            in1=LAMV[:, d, 2 * i + 1, :], op=ALU.mult),
                   reads=["LAMV"], writes=[("LTMP", i)])
        S.emit("dve", I("tensor_reduce", out=LAM[:, d, 1:3], in_=LTMP[:, 0:2, :], axis=AX.X, op=ALU.add),
               reads=[("LTMP", 0), ("LTMP", 1)], writes=[("LAM", d, 1)])
        S.emit("act", I("activation", out=LAM[:, d, 1:3], in_=LAM[:, d, 1:3], func=AF.Exp),
               reads=[("LAM", d, 1)], writes=[("LAM", d, 1)])
        S.emit("dve", I("scalar_tensor_tensor", out=LAM[:, d, 0:1], in0=LAM[:, d, 2:3],
                        scalar=-lam_inits[d], in1=LAM[:, d, 1:2], op0=ALU.add, op1=ALU.subtract),
               reads=[("LAM", d, 1)], writes=[("LAM", d, 0)])

    def do_mod_piece(d, j, pi):
        slot = wslot(pi)
        wv = wview(slot, 8, 512)
        bank = mmbank()
        M = MOD[d % 2]
        for fc in range(4):
            for kc in range(8):
                S.emit("pe", I("matmul", out=PS[:, bank, fc * 2:fc * 2 + 2],
                               lhsT=wv[:, kc, fc * 128:(fc + 1) * 128],
                               rhs=SC[:, kc, :], start=(kc == 0), stop=(kc == 7)),
                       reads=wkeys(slot) + ["SC"], writes=[("PS", bank)], sig=(kc == 7 and fc == 3))
        for v in range(2):
            S.emit("dve", I("tensor_tensor", out=M[:, j * 4:(j + 1) * 4, v], in0=PS[:, bank, v:8:2],
                            in1=BMOD[:, d, j * 4:(j + 1) * 4], op=ALU.add),
                   reads=[("PS", bank), "BMOD"], writes=[("MOD", d % 2, j, v)])

    def mod_ap(d, which, fc, v):
        return MOD[d % 2][:, which * 8 + fc, v:v + 1]

    def mod_key(d, which):
        return [("MOD", d % 2, which * 2 + i, v) for i in range(2) for v in range(2)]

    def do_scale_prep(d, which, gain, gkey, dst, name):
        M = MOD[d % 2]
        for v in range(2):
            S.emit("dve", I("scalar_tensor_tensor", out=dst[:, :, v], in0=M[:, which * 8:which * 8 + 8, v],
                            scalar=1.0, in1=gain[:, d, :], op0=ALU.add, op1=ALU.mult),
                   reads=mod_key(d, which) + [gkey], writes=[(name, v)])

    def stat_accum(oc, ti):
        t0, n = TT[ti]
        bl = blks(t0, n)
        sq = rot("sq", 3)
        S.emit("act", I("activation", out=SQ[sq][:, 0:n], in_=X[:, oc, t0:t0 + n], func=AF.Square),
               reads=[("X", oc, b) for b in bl], writes=[("SQ", sq)])
        S.emit("pe", I("matmul", out=PS[:, 5 + ti, 0:n], lhsT=ONES[:], rhs=SQ[sq][:, 0:n],
                       start=(oc == 0), stop=(oc == 7)),
               reads=["ONES", ("SQ", sq)], writes=[("PS", 5 + ti)], sig=True)

    def do_norm(d, scale_t, scale_name, shift_which, tis=(0, 1, 2), have_stats=False):
        for ti in tis:
            t0, n = TT[ti]
            v = 0 if ti < 2 else 1
            bl = blks(t0, n)
            if have_stats:
                bank = 5 + ti
            else:
                bank = mmbank()
            for fc in range(8):
                if have_stats:
                    break
                sq = rot("sq", 3)
                S.emit("act", I("activation", out=SQ[sq][:, 0:n], in_=X[:, fc, t0:t0 + n], func=AF.Square),
                       reads=[("X", fc, b) for b in bl], writes=[("SQ", sq)])
                S.emit("pe", I("matmul", out=PS[:, bank, 0:n], lhsT=ONES[:], rhs=SQ[sq][:, 0:n],
                               start=(fc == 0), stop=(fc == 7)),
                       reads=["ONES", ("SQ", sq)], writes=[("PS", bank)], sig=True)
            rs = rot("rstd", 2)
            S.emit("act", I("activation", out=RSTD[rs][:, 0:n], in_=PS[:, bank, 0:n], func=AF.Ln, bias=EPSB[:, 0:1], scale=1.0),
                   reads=[("PS", bank), "EPSB"], writes=[("RSTD", rs)])
            S.emit("act", I("activation", out=RSTD[rs][:, 0:n], in_=RSTD[rs][:, 0:n], func=AF.Exp, scale=-0.5),
                   reads=[("RSTD", rs)], writes=[("RSTD", rs)])
            for fc in range(8):
                nt = rot("ntmp", 3)
                S.emit("dve", I("scalar_tensor_tensor", out=NTMP[nt][:, 0:n], in0=X[:, fc, t0:t0 + n],
                                scalar=scale_t[:, fc, v:v + 1], in1=RSTD[rs][:, 0:n], op0=ALU.mult, op1=ALU.mult),
                       reads=[("X", fc, b) for b in bl] + [(scale_name, v), ("RSTD", rs)], writes=[("NTMP", nt)])
                S.emit("act", I("activation", out=HT[:, fc, t0:t0 + n], in_=NTMP[nt][:, 0:n], func=AF.Identity,
                                bias=mod_ap(d, shift_which, fc, v), scale=1.0),
                       reads=[("NTMP", nt)] + mod_key(d, shift_which), writes=[("HT", fc, b) for b in bl])

    def ws_matmul_fm(slot, wv, fc, ti, bank):
        t0, n = TT[ti]
        bl = blks(t0, n)
        for kc in range(8):
            S.emit("pe", I("matmul", out=PS[:, bank, 0:n], lhsT=wv[:, kc, fc * 128:(fc + 1) * 128],
                           rhs=HT[:, kc, t0:t0 + n], start=(kc == 0), stop=(kc == 7)),
                   reads=wkeys(slot) + [("HT", kc, b) for b in bl], writes=[("PS", bank)], sig=(kc == 7))

    def ws_matmul_tm(slot, wv, tk, bank, cols):
        for kc in range(8):
            S.emit("pe", I("matmul", out=PS[:, bank, 0:cols], lhsT=HT[:, kc, tk * 128:(tk + 1) * 128],
                           rhs=wv[:, kc, 0:cols], start=(kc == 0), stop=(kc == 7)),
                   reads=wkeys(slot) + [("HT", kc, tk // 2)], writes=[("PS", bank)], sig=(kc == 7))

    def rope_evac(bank, dstT, fc, keyname):
        nt = rot("ntmp", 3)
        QSv = NTMP[nt][:, 0:NST]
        RTv = NTMP[nt][:, NST:2 * NST]
        S.emit("act", I("activation", out=QSv, in_=PS[:, bank, 0:NST], func=AF.Copy),
               reads=[("PS", bank)], writes=[("NTMP", nt)])
        b2 = mmbank()
        S.emit("pe", I("matmul", out=PS[:, b2, 0:NST], lhsT=PERM[:], rhs=QSv, start=True, stop=True),
               reads=["PERM", ("NTMP", nt)], writes=[("PS", b2)], sig=True)
        S.emit("dve", I("tensor_tensor", out=RTv, in0=PS[:, b2, 0:NST], in1=SINS[:], op=ALU.mult),
               reads=[("PS", b2), "SINS"], writes=[("NTMP", nt)])
        S.emit("dve", I("tensor_tensor", out=QSv, in0=QSv, in1=COST[:], op=ALU.mult),
               reads=[("NTMP", nt), "COST"], writes=[("NTMP", nt)])
        S.emit("dve", I("tensor_tensor", out=dstT[:, fc, NPT:T], in0=QSv, in1=RTv, op=ALU.add),
               reads=[("NTMP", nt)], writes=[(keyname, fc, 4)])

    ktr_pending = []

    def do_in_piece(kind, d, pi):
        cols = 256 if kind == "vs" else 512
        slot = wslot(pi)
        wv = wview(slot, 8, cols)
        if kind in ("q", "k"):
            dstT = QT if kind == "q" else KT
            kn = "QT" if kind == "q" else "KT"
            for fc in range(4):
                for ti in range(3):
                    if kind == "k" and ti < 2:
                        continue
                    t0, n = TT[ti]
                    bank = mmbank()
                    ws_matmul_fm(slot, wv, fc, ti, bank)
                    if ti < 2:
                        S.emit("act", I("activation", out=dstT[:, fc, t0:t0 + n], in_=PS[:, bank, 0:n], func=AF.Copy),
                               reads=[("PS", bank)], writes=[(kn, fc, b) for b in blks(t0, n)])
                    else:
                        rope_evac(bank, dstT, fc, kn)
            if kind == "k":
                for tk in range(8):
                    bank = mmbank()
                    ws_matmul_tm(slot, wv, tk, bank, 512)
                    sg = rot("stg", 2)
                    S.emit("dve", I("tensor_copy", out=STG[sg][:], in_=PS[:, bank, :]),
                           reads=[("PS", bank)], writes=[("STG", sg)])
                    S.emit("sp", I("dma_start", out=nk_d[tk // 2, d, (tk % 2) * 128:(tk % 2) * 128 + 128, :], in_=STG[sg][:]),
                           reads=[("STG", sg)], writes=[], kind="d", semkey=("stgo", sg))
                    oi = rot("atto", 2)
                    KB = ATT_OB[oi]
                    S.emit("act", I("activation", out=KB[:].rearrange("p h e -> p (h e)"), in_=STG[sg][:], func=AF.Copy),
                           reads=[("STG", sg)], writes=[("ATTO", oi, u) for u in range(4)])

                    def ktr(tk=tk, oi=oi, KB=KB):
                        tb = mmbank()
                        TPV = PS[:, tb, 0:256].bitcast(BF16)
                        for h in range(4):
                            S.emit("pe", I("transpose", out=TPV[:, h * 128:(h + 1) * 128], in_=KB[:, h, :], identity=IDB[:]),
                                   reads=[("ATTO", oi, h), "IDB"], writes=[("PS", tb)], sig=(h == 3))
                        S.emit("act", I("activation", out=KT[:, :, tk * 128:(tk + 1) * 128],
                                        in_=TPV[:, 0:512].rearrange("p (h q) -> p h q", h=4), func=AF.Copy),
                               reads=[("PS", tb)], writes=[("KT", fc, tk // 2) for fc in range(4)])
                    ktr_pending.append(ktr)
                    while len(ktr_pending) > 1:
                        ktr_pending.pop(0)()
                while ktr_pending:
                    ktr_pending.pop(0)()
        elif kind == "v":
            for tk in range(10):
                bank = mmbank()
                ws_matmul_tm(slot, wv, tk, bank, 512)
                sg = rot("stg", 2)
                S.emit("dve", I("tensor_copy", out=STG[sg][:], in_=PS[:, bank, :]),
                       reads=[("PS", bank)], writes=[("STG", sg)])
                S.emit("act", I("activation", out=VA[:, tk, :, 0:128],
                                in_=STG[sg][:].rearrange("p (h e) -> p h e", h=4), func=AF.Copy),
                       reads=[("STG", sg)], writes=[("VA", tk)])
                if tk < 8:
                    S.emit("sp", I("dma_start", out=nv_d[tk // 2, d, (tk % 2) * 128:(tk % 2) * 128 + 128, :], in_=STG[sg][:]),
                           reads=[("STG", sg)], writes=[], kind="d", semkey=("stgo", sg))
        elif kind == "g":
            for ti in range(3):
                for fc in range(4):
                    t0, n = TT[ti]
                    bank = mmbank()
                    ws_matmul_fm(slot, wv, fc, ti, bank)
                    S.emit("act", I("activation", out=G[:, fc, t0:t0 + n], in_=PS[:, bank, 0:n], func=AF.Copy),
                           reads=[("PS", bank)], writes=[("G", fc, b) for b in blks(t0, n)])
        elif kind == "xu":
            for fc in range(4):
                for ti in range(3):
                    t0, n = TT[ti]
                    bank = mmbank()
                    ws_matmul_fm(slot, wv, fc, ti, bank)
                    if fc < 2:
                        S.emit("dve", I("tensor_tensor", out=ZU[:, fc, t0:t0 + n], in0=PS[:, bank, 0:n],
                                        in1=G[:, 2 + fc, t0:t0 + n], op=ALU.mult),
                               reads=[("PS", bank)] + [("G", 2 + fc, b) for b in blks(t0, n)],
                               writes=[("ZU", fc, b) for b in blks(t0, n)])
                    else:
                        S.emit("act", I("activation", out=ZU[:, fc, t0:t0 + n], in_=PS[:, bank, 0:n], func=AF.Copy),
                               reads=[("PS", bank)], writes=[("ZU", fc, b) for b in blks(t0, n)])
        elif kind == "vs":
            for tk in range(10):
                bank = mmbank()
                ws_matmul_tm(slot, wv, tk, bank, 256)
                S.emit("act", I("activation", out=VS[:, tk, :], in_=PS[:, bank, 0:256], func=AF.Copy),
                       reads=[("PS", bank)], writes=[("VS", tk)])
                if tk % 2 == 1 and tk >= 3:
                    blk = tk // 2 - 1
                    do_cmlp(d, blk)
            do_cmlp(d, 4)
            do_conv_prompt(d)

    def flat_ap(t, off, dims):
        return bass.AP(t, off, [list(x) for x in dims])

    def do_exchange(d):
        bt = bounce_t[d]
        gt = gath_t[d]
        S.emit("dve", I("tensor_copy", out=ZB[:, :, 0], in_=ZU[:, 0:2, NPT]),
               reads=[("ZU", 0, 4), ("ZU", 1, 4)], writes=[("ZB", 0)])
        S.emit("dve", I("tensor_copy", out=ZB[:, :, 1], in_=ZU[:, 0:2, T - 1]),
               reads=[("ZU", 0, 4), ("ZU", 1, 4)], writes=[("ZB", 1)])
        S.emit("sp", I("dma_start", out=flat_ap(bt, 0, [[256, 128], [128 * 256, 4], [1, 256]]), in_=KT[:, :, NPT:T]),
               reads=[("KT", fc, 4) for fc in range(4)], writes=[("BOUNCE", d, "k")], kind="d", semkey="bk")
        for jj in range(2):
            S.emit("sp", I("dma_start", out=flat_ap(bt, 131072 + jj * 65536, [[512, 128], [128, 4], [1, 128]]),
                           in_=VA[:, 8 + jj, :, 0:128]),
                   reads=[("VA", 8 + jj)], writes=[("BOUNCE", d, "v", jj)], kind="d", semkey=("bv", jj))
        S.emit("sp", I("dma_start", out=flat_ap(bt, 262144, [[4, 128], [1, 4]]), in_=ZB[:].rearrange("p j e -> p (j e)")),
               reads=[("ZB", 0), ("ZB", 1)], writes=[("BOUNCE", d, "h")], kind="d", semkey="bh")
        S.emit("pool", I("collective_compute", kind="AllGather", op=ALU.bypass,
                         replica_groups=[[0, 1, 2, 3], [4, 5, 6, 7]], ins=[bt.ap()], outs=[gt.ap()]),
               reads=[("BOUNCE", d, "k"), ("BOUNCE", d, "v", 0), ("BOUNCE", d, "v", 1), ("BOUNCE", d, "h")], writes=[("GATH", d)], kind="cc")
        S.emit("sp", I("dma_start", out=HB[:].rearrange("p r j e -> p r (j e)"),
                       in_=flat_ap(gt, 262144, [[4, 128], [BROWS * 512, 4], [1, 4]])),
               reads=[("GATH", d)], writes=["HB"], kind="d", semkey="hb")

    def do_halo(d):
        S.emit("dve", I("tensor_copy", out=HBF[:], in_=HB[:]), reads=["HB"], writes=["HBF"])
        for w, esrc in ((0, 1), (1, 0)):
            for j in range(2):
                S.emit("dve", I("tensor_tensor", out=HTMP[:, w, j, :], in0=HBF[:, :, j, esrc], in1=SEL[:, w, :], op=ALU.mult),
                       reads=["HBF", "SEL"], writes=[("HTMP", w, j)])
        S.emit("dve", I("tensor_reduce", out=HALO[:], in_=HTMP[:], axis=AX.X, op=ALU.add),
               reads=[("HTMP", w, j) for w in range(2) for j in range(2)], writes=["HALO"])

    KALLK = ["KALLc"] + [("KALLg", r) for r in range(4)]
    VALLK = [("VALLc", kk) for kk in range(4)] + ["VALL1"] + [("VALLg", r, jj) for r in range(4) for jj in range(2)]

    def do_load_sample_keys(d):
        gt = gath_t[d]
        S.emit("pool", I("dma_start", out=KALL[:, :, 0:PAST], in_=ckT_d[d]),
               reads=[], writes=["KALLc"], kind="d", semkey="ck")
        for kk in range(4):
            S.emit("pool", I("dma_start", out=VALL[:, kk, :, 0:128],
                             in_=cv_d[d, kk * 128:(kk + 1) * 128, :].rearrange("p (h e) -> p h e", h=4)),
                   reads=[], writes=[("VALLc", kk)], kind="d", semkey=("cvv", kk))
        S.emit("dve", I("memset", ap=VALL[:, :, :, 128:129], constant=1.0), reads=[], writes=["VALL1"])
        for r in range(4):
            S.emit("sp", I("dma_start", out=KALL[:, :, PAST + r * 256:PAST + (r + 1) * 256],
                           in_=flat_ap(gt, r * BROWS * 512, [[256, 128], [128 * 256, 4], [1, 256]])),
                   reads=[("GATH", d)], writes=[("KALLg", r)], kind="d", semkey=("gk", r))
            for jj in range(2):
                S.emit("sp", I("dma_start", out=VALL[:, 4 + 2 * r + jj, :, 0:128],
                               in_=flat_ap(gt, r * BROWS * 512 + 131072 + jj * 65536, [[512, 128], [128, 4], [1, 128]])),
                       reads=[("GATH", d)], writes=[("VALLg", r, jj)], kind="d", semkey=("gv", r, jj))

    ETALL = rview(33360, 6144)

    def oblock(k):
        return 5 + k // 3, (k % 3) * 129

    def do_attention(d, qb, sample, filler=None):
        nkt = 12 if sample else 2
        q0 = qb * 256
        ebase = {}
        ekey = {}
        for h in range(4):
            if sample:
                ebase[h] = (0, 3072)
                ekey[h] = ([("ETS", i) for i in range(0, 3)], [("ETS", i) for i in range(3, 6)])
            else:
                sl = rot("ets", 6)
                ebase[h] = (sl * 1024, sl * 1024 + 512)
                ekey[h] = ([("ETS", sl)], [("ETS", sl)])

        def scores(h):
            for j in range(nkt // 2):
                spair = ((0, 1), (2, 3))[rot("sb", 2)]
                for kk in range(2):
                    kt = 2 * j + kk
                    for c in range(2):
                        if sample:
                            lhsT = KALL[c * 64:(c + 1) * 64, h, kt * 128:(kt + 1) * 128]
                            rk = list(KALLK)
                        else:
                            lhsT = KT[c * 64:(c + 1) * 64, h, q0 + kt * 128:q0 + (kt + 1) * 128]
                            rk = [("KT", h, qb)]
                        S.emit("pe", I("matmul", out=PS[:, spair[c], kk * 256:(kk + 1) * 256], lhsT=lhsT,
                                       rhs=QT[c * 64:(c + 1) * 64, h, q0:q0 + 256], start=True, stop=True),
                               reads=rk + [("QT", h, qb)], writes=[("PS", spair[c])], sig=(kk == 1))
                for c in range(2):
                    o0 = ebase[h][c] + j * 512
                    S.emit("act", I("activation", out=ETALL[:, o0:o0 + 512], in_=PS[:, spair[c], :],
                                    func=AF.Exp, scale=0.125),
                           reads=[("PS", spair[c])], writes=ekey[h][c])

        def batch(units):
            U = len(units)
            oi = rot("atto", 2)
            ATT_O = ATT_OB[oi]
            obanks = sorted(set(oblock(k)[0] for k in range(2 * U)))
            okeys = [("PS", bk) for bk in obanks]
            order = [(u, c) for c in range(2) for u in range(U)] if sample else [(u, c) for u in range(U) for c in range(2)]
            for (u, c) in order:
                h, qt = units[u]
                if True:
                    bk, col = oblock(u * 2 + c)
                    for kt in range(nkt):
                        if sample:
                            rhs = VALL[:, kt, h, 0:129]
                            rk = list(VALLK)
                        else:
                            rhs = VA[:, qb * 2 + kt, h, 0:129]
                            rk = [("VA", qb * 2 + kt), "VA1"]
                        e0 = ebase[h][c] + kt * 256 + qt * 128
                        last = (kt == nkt - 1)
                        S.emit("pe", I("matmul", out=PS[:, bk, col:col + 129], lhsT=ETALL[:, e0:e0 + 128], rhs=rhs,
                                       start=(kt == 0), stop=last),
                               reads=rk + ekey[h][c], writes=[("PS", bk)], sig=last)
            sm = ATT_S
            for bk in obanks:
                ks = [k for k in range(2 * U) if oblock(k)[0] == bk]
                n = len(ks)
                S.emit("dve", I("reciprocal", out=sm[:, ks[0]:ks[0] + n],
                                in_=PS[:, bk, 0:n * 129].rearrange("p (k e) -> p k e", e=129)[:, :, 128]),
                       reads=[("PS", bk)], writes=[("ATTS", "r", bk)])
            rkeys = [("ATTS", "r", bk) for bk in obanks]
            S.emit("dve", I("tensor_scalar", out=sm[:, 8:8 + U], in0=sm[:, 1:2 * U:2], scalar1=LAM[:, d, 0:1],
                            scalar2=None, op0=ALU.mult),
                   reads=rkeys + [("LAM", d, 0)], writes=[("ATTS", "n")])
            for u in range(U):
                bk, col = oblock(u * 2)
                S.emit("dve", I("tensor_scalar", out=ATT_A[:, u, :], in0=PS[:, bk, col:col + 128],
                                scalar1=sm[:, 2 * u:2 * u + 1], scalar2=None, op0=ALU.mult),
                       reads=[("PS", bk)] + rkeys, writes=[("ATTA", u)])
            for u in range(U):
                bk, col = oblock(u * 2 + 1)
                S.emit("dve", I("scalar_tensor_tensor", out=ATT_A[:, u, :], in0=PS[:, bk, col:col + 128],
                                scalar=sm[:, 8 + u:9 + u], in1=ATT_A[:, u, :], op0=ALU.mult, op1=ALU.add),
                       reads=[("PS", bk), ("ATTS", "n"), ("ATTA", u)], writes=[("ATTA", u)])
            akeys = [("ATTA", u) for u in range(U)]
            jn = rot("ntmp", 3)
            ATT_J = NTMP[jn]
            S.emit("dve", I("tensor_tensor", out=ATT_J[:, 0:U * 128], in0=ATT_A[:, 0:U, :].rearrange("p u e -> p (u e)"),
                            in1=ATT_A[:, 0:U, :].rearrange("p u e -> p (u e)"), op=ALU.mult),
                   reads=akeys, writes=[("NTMP", jn)])
            S.emit("dve", I("tensor_reduce", out=sm[:, 16:16 + U], in_=ATT_J[:, 0:U * 128].rearrange("p (u e) -> p u e", u=U),
                            axis=AX.X, op=ALU.add),
                   reads=[("NTMP", jn)], writes=[("ATTS", "q")])
            S.emit("act", I("activation", out=sm[:, 20:20 + U], in_=sm[:, 16:16 + U], func=AF.Ln, bias=EPSB[:, 0:1],
                            scale=1.0 / 128.0),
                   reads=[("ATTS", "q"), "EPSB"], writes=[("ATTS", "l")])
            S.emit("act", I("activation", out=sm[:, 24:24 + U], in_=sm[:, 20:20 + U], func=AF.Exp, scale=-0.5),
                   reads=[("ATTS", "l")], writes=[("ATTS", "s")])
            for u in range(U):
                S.emit("dve", I("scalar_tensor_tensor", out=ATT_O[:, u, :], in0=ATT_A[:, u, :], scalar=sm[:, 24 + u:25 + u],
                                in1=GSUB[:], op0=ALU.mult, op1=ALU.mult),
                       reads=[("ATTA", u), ("ATTS", "s"), "GSUB"], writes=[("ATTO", oi, u)])
            def fin():
                TPV = PS[:, 4, 0:256].bitcast(BF16)
                for u in range(U):
                    S.emit("pe", I("transpose", out=TPV[:, u * 128:(u + 1) * 128], in_=ATT_O[:, u, :], identity=IDB[:]),
                           reads=[("ATTO", oi, u), "IDB"], writes=[("PS", 4)], sig=(u == U - 1))
                if sample:
                    hh = units[0][0]
                    S.emit("act", I("activation", out=CAT[:, hh, q0:q0 + 256], in_=TPV[:, 0:256], func=AF.Copy),
                           reads=[("PS", 4)], writes=[("HT", hh, qb)])
                else:
                    qt0 = units[0][1]
                    t0 = q0 + qt0 * 128
                    S.emit("act", I("activation", out=CAT[:, 0:4, t0:t0 + 128],
                                    in_=TPV[:, 0:512].rearrange("p (h q) -> p h q", h=4), func=AF.Copy),
                           reads=[("PS", 4)], writes=[("HT", hh, qb) for hh in range(4)])
            return fin

        def push(fin):
            att_pending.append(fin)
            while len(att_pending) > 1:
                att_pending.pop(0)()

        if sample:
            for h in range(4):
                scores(h)
                push(batch([(h, 0), (h, 1)]))
                if filler is not None:
                    filler()
        else:
            for h in range(4):
                scores(h)
            for qt in range(2):
                push(batch([(h, qt) for h in range(4)]))
                if filler is not None:
                    filler()

    att_pending = []

    def att_flush():
        while att_pending:
            att_pending.pop(0)()

    def do_conv_prompt(d):
        for j in range(2):
            Z3 = ZU[:, j, 0:NPT].rearrange("p (s t) -> p s t", s=4)
            C3 = CAT[:, 4 + j, 0:NPT].rearrange("p (s t) -> p s t", s=4)
            zr = [("ZU", j, b) for b in range(4)]
            ck = [("HT", 4 + j, b) for b in range(4)]
            S.emit("dve", I("tensor_scalar", out=CONVY[:, 0:NPT], in0=ZU[:, j, 0:NPT], scalar1=CW[:, d, j, 1:2],
                            scalar2=CW[:, d, j, 3:4], op0=ALU.mult, op1=ALU.add),
                   reads=zr + ["CW"], writes=["CONVY", ("RSTD", 0), ("RSTD", 1)])
            Y3 = CONVY[:, 0:NPT].rearrange("p (s t) -> p s t", s=4)
            S.emit("dve", I("scalar_tensor_tensor", out=Y3[:, :, 1:256], in0=Z3[:, :, 0:255], scalar=CW[:, d, j, 0:1],
                            in1=Y3[:, :, 1:256], op0=ALU.mult, op1=ALU.add),
                   reads=zr + ["CW", "CONVY"], writes=["CONVY"])
            S.emit("dve", I("scalar_tensor_tensor", out=Y3[:, :, 0:255], in0=Z3[:, :, 1:256], scalar=CW[:, d, j, 2:3],
                            in1=Y3[:, :, 0:255], op0=ALU.mult, op1=ALU.add),
                   reads=zr + ["CW", "CONVY"], writes=["CONVY"])
            S.emit("dve", I("tensor_tensor", out=CAT[:, 4 + j, 0:NPT], in0=CONVY[:, 0:NPT], in1=G[:, j, 0:NPT], op=ALU.mult),
                   reads=["CONVY"] + [("G", j, b) for b in range(4)], writes=ck)

    def do_conv(d, blk):
        t0 = blk * 256
        sample = blk == 4
        for j in range(2):
            ci = rot("ntmp", 3)
            Y = NTMP[ci][:, 0:256]
            zr = [("ZU", j, blk)]
            S.emit("dve", I("tensor_scalar", out=Y[:], in0=ZU[:, j, t0:t0 + 256], scalar1=CW[:, d, j, 1:2],
                            scalar2=CW[:, d, j, 3:4], op0=ALU.mult, op1=ALU.add),
                   reads=zr + ["CW"], writes=[("NTMP", ci)])
            S.emit("dve", I("scalar_tensor_tensor", out=Y[:, 1:256], in0=ZU[:, j, t0:t0 + 255], scalar=CW[:, d, j, 0:1],
                            in1=Y[:, 1:256], op0=ALU.mult, op1=ALU.add),
                   reads=zr + ["CW", ("NTMP", ci)], writes=[("NTMP", ci)])
            S.emit("dve", I("scalar_tensor_tensor", out=Y[:, 0:255], in0=ZU[:, j, t0 + 1:t0 + 256], scalar=CW[:, d, j, 2:3],
                            in1=Y[:, 0:255], op0=ALU.mult, op1=ALU.add),
                   reads=zr + ["CW", ("NTMP", ci)], writes=[("NTMP", ci)])
            if sample:
                S.emit("dve", I("scalar_tensor_tensor", out=Y[:, 0:1], in0=HALO[:, 0, j:j + 1], scalar=CW[:, d, j, 0:1],
                                in1=Y[:, 0:1], op0=ALU.mult, op1=ALU.add),
                       reads=["HALO", "CW", ("NTMP", ci)], writes=[("NTMP", ci)])
                S.emit("dve", I("scalar_tensor_tensor", out=Y[:, 255:256], in0=HALO[:, 1, j:j + 1], scalar=CW[:, d, j, 2:3],
                                in1=Y[:, 255:256], op0=ALU.mult, op1=ALU.add),
                       reads=["HALO", "CW", ("NTMP", ci)], writes=[("NTMP", ci)])
            S.emit("dve", I("tensor_tensor", out=CAT[:, 4 + j, t0:t0 + 256], in0=Y[:], in1=G[:, j, t0:t0 + 256], op=ALU.mult),
                   reads=[("NTMP", ci), ("G", j, blk)], writes=[("HT", 4 + j, blk)])

    def do_cmlp(d, blk):
        for half in range(2):
            tk = blk * 2 + half
            t0 = tk * 128
            bank = mmbank()
            for pair in range(2):
                S.emit("pe", I("matmul", out=PS[:, bank, pair * 256:(pair + 1) * 256],
                               lhsT=VS[:, tk, pair * 128:(pair + 1) * 128],
                               rhs=WST[:, d, pair * 2:pair * 2 + 2, :].rearrange("q g p -> q (g p)"),
                               start=True, stop=True),
                       reads=[("VS", tk), "WST"], writes=[("PS", bank)], sig=(pair == 1))
            ci = rot("ntmp", 3)
            N3 = NTMP[ci][:, 0:256].rearrange("p (a e) -> p a e", a=2)
            P4 = PS[:, bank, :].rearrange("p (a g e) -> p a g e", a=2, g=2)
            for gi in range(2):
                S.emit("dve", I("tensor_tensor", out=N3[gi * 64:(gi + 1) * 64], in0=P4[gi * 64:(gi + 1) * 64, :, gi, :],
                                in1=BSB[gi * 64:(gi + 1) * 64, d, :, :], op=ALU.add),
                       reads=[("PS", bank), "BSB"], writes=[("NTMP", ci, gi)])
            S.emit("dve", I("tensor_tensor", out=CAT[:, 6:8, t0:t0 + 128], in0=N3, in1=ZU[:, 2:4, t0:t0 + 128], op=ALU.mult),
                   reads=[("NTMP", ci, 0), ("NTMP", ci, 1), ("ZU", 2, blk), ("ZU", 3, blk)],
                   writes=[("HT", 6, blk), ("HT", 7, blk)])

    def resid_evac(bank, oc, ti, d, which, rng=None):
        t0, n = TT[ti] if rng is None else rng
        v = 0 if t0 < NPT else 1
        bl = blks(t0, n)
        S.emit("dve", I("scalar_tensor_tensor", out=X[:, oc, t0:t0 + n], in0=PS[:, bank, 0:n],
                        scalar=mod_ap(d, which, oc, v), in1=X[:, oc, t0:t0 + n], op0=ALU.mult, op1=ALU.add),
               reads=[("PS", bank)] + mod_key(d, which) + [("X", oc, b) for b in bl], writes=[("X", oc, b) for b in bl])

    def out_groups(d, pis, rng):
        slots = [wslot(pis[0], la=NSLOT - 1), wslot(pis[1], la=NSLOT - 2)]
        res = []
        for j in range(2):
            wv = wview(slots[j], 8, 512)
            for fc in range(4):
                def g(j=j, fc=fc, wv=wv):
                    oc = j * 4 + fc
                    t0, n = rng
                    bank = mmbank()
                    for kc in range(8):
                        S.emit("pe", I("matmul", out=PS[:, bank, 0:n], lhsT=wv[:, kc, fc * 128:(fc + 1) * 128],
                                       rhs=CAT[:, kc, t0:t0 + n], start=(kc == 0), stop=(kc == 7)),
                               reads=wkeys(slots[j]) + [("HT", kc, b) for b in blks(t0, n)], writes=[("PS", bank)],
                               sig=(kc == 7))
                    resid_evac(bank, oc, 0, d, 2, rng=rng)
                res.append(g)
        return res

    def do_out_piece(d, j, pi, tis=(0, 1, 2)):
        slot = wslot(pi, la=NSLOT - 1 - j)
        wv = wview(slot, 8, 512)
        for ti in tis:
            for fc in range(4):
                oc = j * 4 + fc
                t0, n = TT[ti]
                bank = mmbank()
                for kc in range(8):
                    S.emit("pe", I("matmul", out=PS[:, bank, 0:n], lhsT=wv[:, kc, fc * 128:(fc + 1) * 128],
                                   rhs=CAT[:, kc, t0:t0 + n], start=(kc == 0), stop=(kc == 7)),
                           reads=wkeys(slot) + [("HT", kc, b) for b in blks(t0, n)], writes=[("PS", bank)], sig=(kc == 7))
                resid_evac(bank, oc, ti, d, 2)

    def do_up_piece(d, j, pi):
        slot = wslot(pi)
        wv = wview(slot, 8, 512)
        for ti in range(3):
            for fc in range(4):
                hc = j * 4 + fc
                t0, n = TT[ti]
                bank = mmbank()
                ws_matmul_fm(slot, wv, fc, ti, bank)
                nt = rot("ntmp", 3)
                S.emit("act", I("activation", out=NTMP[nt][:, 0:n], in_=PS[:, bank, 0:n], func=AF.Relu),
                       reads=[("PS", bank)], writes=[("NTMP", nt)])
                S.emit("dve", I("tensor_tensor", out=HID[:, hc, t0:t0 + n], in0=NTMP[nt][:, 0:n], in1=NTMP[nt][:, 0:n],
                                op=ALU.mult),
                       reads=[("NTMP", nt)], writes=[("HID", hc, b) for b in blks(t0, n)])

    stat_pending = []

    def do_down_piece(d, j, pi):
        slot = wslot(pi)
        wv = wview(slot, 32, 128)
        for ti in range(3):
            t0, n = TT[ti]
            bank = mmbank()
            for kc in range(32):
                S.emit("pe", I("matmul", out=PS[:, bank, 0:n], lhsT=wv[:, kc, :], rhs=HID[:, kc, t0:t0 + n],
                               start=(kc == 0), stop=(kc == 31)),
                       reads=wkeys(slot) + [("HID", kc, b) for b in blks(t0, n)], writes=[("PS", bank)], sig=(kc == 31))
            resid_evac(bank, j, ti, d, 5)
            stat_pending.append((j, ti))
            while len(stat_pending) > (0 if (j == 7 and ti == 2) else 2):
                stat_accum(*stat_pending.pop(0))

    REGION_MIX = ([("QT", c, b) for c in range(4) for b in range(5)] + [("KT", c, b) for c in range(4) for b in range(5)]
                  + [("VA", k) for k in range(10)] + ["VA1"] + [("G", c, b) for c in range(4) for b in range(5)]
                  + [("ZU", c, b) for c in range(4) for b in range(5)] + [("VS", k) for k in range(10)]
                  + [("ETS", i) for i in range(6)] + KALLK + VALLK)
    REGION_HID = [("HID", c, b) for c in range(32) for b in range(5)]
    GZ_KEYS = [("G", c, b) for c in range(4) for b in range(5)] + [("ZU", c, b) for c in range(4) for b in range(5)]

    def fence(keys):
        S.emit("dve", I("memset", ap=FEN[:, 0:1], constant=0.0), reads=[], writes=list(keys))

    for (kind, d, j, pi) in plan:
        if kind == "mod":
            do_mod_piece(d, j, pi)
            continue
        if kind == "in_g":
            load(GSUB[:], gsub_d[:, d, :], "GSUB")
            S.emit("dve", I("tensor_scalar", out=GSUB[:], in0=GSUB[:], scalar1=(1.0 - lam_inits[d]),
                            scalar2=None, op0=ALU.mult),
                   reads=["GSUB"], writes=["GSUB"])
            do_scale_prep(d, 1, GMIX, "GMIX", S1, "S1")
            fence(REGION_MIX + REGION_HID)
            do_norm(d, S1, "S1", 0, have_stats=(d > 0))
            S.emit("dve", I("memset", ap=VA[:, :, :, 128:129], constant=1.0), reads=[], writes=["VA1"])
        if kind.startswith("in_"):
            mm_nb[0] = 8
            do_in_piece(kind[3:], d, pi)
            mm_nb[0] = 4
            if kind == "in_v":
                do_exchange(d)
            continue
        if kind == "mix":
            continue
        if kind == "out":
            if j == 0:
                out_pis = [pi]
                continue
            out_pis.append(pi)
            gb = [out_groups(d, out_pis, (b * 256, 256)) for b in range(5)]

            def fill_from(lst, k):
                def f():
                    for _ in range(k):
                        if lst:
                            lst.pop(0)()
                return f

            sched = [None, (0, 4), (0, 4), (1, 4), (1, 4), (2, 4), (2, 4), (3, 3), (3, 3), (3, 2)]
            fidx = [0]

            def filler():
                e = sched[fidx[0]] if fidx[0] < len(sched) else None
                fidx[0] += 1
                if e is not None:
                    fill_from(gb[e[0]], e[1])()

            do_attention(d, 0, False)
            do_attention(d, 1, False, filler=filler)
            do_halo(d)
            do_conv(d, 4)
            fence(GZ_KEYS + KALLK + VALLK)
            do_load_sample_keys(d)
            do_attention(d, 2, False, filler=filler)
            do_scale_prep(d, 4, GMLP, "GMLP", S2, "S2")
            do_attention(d, 3, False, filler=filler)
            att_flush()
            fill_from(gb[0], 8)()
            fill_from(gb[1], 8)()
            do_norm(d, S2, "S2", 3, tis=(0,))
            do_attention(d, 4, True, filler=filler)
            for b_ in range(4):
                fill_from(gb[b_], 8)()
            att_flush()
            fill_from(gb[4], 8)()
            do_norm(d, S2, "S2", 3, tis=(1,))
            do_norm(d, S2, "S2", 3, tis=(2,))
            fence(REGION_MIX + REGION_HID)
            continue
        if kind == "up":
            mm_nb[0] = 8
            do_up_piece(d, j, pi)
            mm_nb[0] = 4
            continue
        if kind == "down":
            mm_nb[0] = 5
            do_down_piece(d, j, pi)
            mm_nb[0] = 4
            continue

    for ti, (t0, n) in enumerate(TT):
        bank = 5 + ti
        bl = blks(t0, n)
        rs = rot("rstd", 2)
        S.emit("act", I("activation", out=RSTD[rs][:, 0:n], in_=PS[:, bank, 0:n], func=AF.Ln, bias=EPSB[:, 0:1], scale=1.0),
               reads=[("PS", bank), "EPSB"], writes=[("RSTD", rs)])
        S.emit("act", I("activation", out=RSTD[rs][:, 0:n], in_=RSTD[rs][:, 0:n], func=AF.Exp, scale=-0.5),
               reads=[("RSTD", rs)], writes=[("RSTD", rs)])
        for fc in range(8):
            S.emit("dve", I("scalar_tensor_tensor", out=X[:, fc, t0:t0 + n], in0=X[:, fc, t0:t0 + n],
                            scalar=GFIN[:, fc:fc + 1], in1=RSTD[rs][:, 0:n], op0=ALU.mult, op1=ALU.mult),
                   reads=[("X", fc, b) for b in bl] + ["GFIN", ("RSTD", rs)], writes=[("X", fc, b) for b in bl])
    for fc in range(8):
        S.emit("sp", I("dma_start", out=yT_d[:, fc, :], in_=X[:, fc, :]),
               reads=[("X", fc, b) for b in range(5)], writes=[], kind="d", semkey=("yo", fc))

    S.run(nc)
    st.close()
    return nc


def _rope_tables_np(pos0, n):
    t = np.arange(pos0, pos0 + n)
    row = (t // 64).astype(np.float64)
    col = (t % 64).astype(np.float64)
    inv = 10000.0 ** (-np.arange(0, 32, 2, dtype=np.float64) / 32.0)
    ar = row[:, None] * inv[None, :]
    ac = col[:, None] * inv[None, :]
    cosT = np.zeros((128, n), np.float32)
    sinS = np.zeros((128, n), np.float32)
    for p in range(128):
        e = p % 64
        ax = e // 32
        i = e % 32
        first = i < 16
        ang = (ar if ax == 0 else ac)[:, i % 16]
        cosT[p] = np.cos(ang)
        sinS[p] = (-np.sin(ang)) if first else np.sin(ang)
    return cosT, sinS


def _perm_np():
    P = np.zeros((128, 128), np.float32)
    for m in range(128):
        i = m % 32
        partner = m + 16 if i < 16 else m - 16
        P[partner, m] = 1.0
    return P


_NC_CACHE = {}


def _prepare(x_prompt, x_sample, cache_k, cache_v, c, c_ctx, w_mod, b_mod, norm_mix, norm_mlp,
             w_in, lam_q1, lam_k1, lam_q2, lam_k2, subln, conv_w, conv_b, w_s, b_s,
             w_out, w_up, w_down, norm_final):
    f = lambda a: np.ascontiguousarray(np.asarray(a, dtype=np.float32))
    x_prompt, x_sample, cache_k, cache_v, c, c_ctx = map(f, (x_prompt, x_sample, cache_k, cache_v, c, c_ctx))
    w_mod, b_mod, norm_mix, norm_mlp, w_in = map(f, (w_mod, b_mod, norm_mix, norm_mlp, w_in))
    lam_q1, lam_k1, lam_q2, lam_k2, subln = map(f, (lam_q1, lam_k1, lam_q2, lam_k2, subln))
    conv_w, conv_b, w_s, b_s, w_out, w_up, w_down, norm_final = map(
        f, (conv_w, conv_b, w_s, b_s, w_out, w_up, w_down, norm_final))
    depth = w_in.shape[0]
    bmod = np.ascontiguousarray(b_mod.reshape(depth, 48, 128).transpose(2, 0, 1))
    gmix = np.ascontiguousarray(norm_mix.reshape(depth, 8, 128).transpose(2, 0, 1))
    gmlp = np.ascontiguousarray(norm_mlp.reshape(depth, 8, 128).transpose(2, 0, 1))
    gfin = np.ascontiguousarray(norm_final.reshape(8, 128).T)
    lamv = np.ascontiguousarray(np.broadcast_to(np.stack([lam_q1, lam_k1, lam_q2, lam_k2], axis=1)[None], (128, depth, 4, 64)))
    gsub = np.ascontiguousarray(np.broadcast_to(subln[None], (128, depth, 128)))
    cw = np.zeros((128, depth, 2, 4), np.float32)
    cw[:, :, :, 0:3] = conv_w.reshape(depth, 3, 2, 128).transpose(3, 0, 2, 1)
    cw[:, :, :, 3] = conv_b.reshape(depth, 2, 128).transpose(2, 0, 1)
    wsT = np.ascontiguousarray(w_s.transpose(3, 0, 1, 2))
    bsB = np.zeros((128, depth, 2, 128), np.float32)
    for pair in range(2):
        for gi in range(2):
            bsB[gi * 64:(gi + 1) * 64, :, pair, :] = b_s[:, pair * 2 + gi, :][None]
    perm = _perm_np()
    ident = np.eye(128, dtype=np.float32)
    in_maps = []
    for core in range(8):
        s = core // 4
        r = core % 4
        xtok = np.concatenate([x_prompt[4 * core:4 * core + 4].reshape(1024, D),
                               x_sample[s, r * 256:(r + 1) * 256]], axis=0)
        xT = np.ascontiguousarray(xtok.reshape(T, 8, 128).transpose(2, 1, 0))
        ckT = np.ascontiguousarray(cache_k[s, :depth].transpose(0, 3, 2, 1))
        cv = np.ascontiguousarray(cache_v[s, :depth].reshape(depth, PAST, 512))
        cvec = np.ascontiguousarray(np.stack([c_ctx, c[s]], axis=-1).reshape(8, 128, 2).transpose(1, 0, 2))
        cosT, sinS = _rope_tables_np(r * 256, 256)
        sel = np.zeros((128, 2, 4), np.float32)
        if r > 0:
            sel[:, 0, r - 1] = 1.0
        if r < 3:
            sel[:, 1, r + 1] = 1.0
        in_maps.append({
            "xT": xT, "ckT": ckT, "cv": cv, "cvec": cvec,
            "w_mod": w_mod, "w_in": w_in, "w_out": w_out, "w_up": w_up, "w_down": w_down,
            "bmod": bmod, "gmix": gmix, "gmlp": gmlp, "gfin": gfin, "lamv": lamv, "gsub": gsub,
            "cw": cw, "wsT": wsT, "bsB": bsB, "cosT": cosT, "sinS": sinS, "perm": perm, "ident": ident,
            "sel": sel,
        })
    return depth, in_maps


def _assemble(outs, depth):
    y_prompt = np.zeros((32, 256, D), np.float32)
    y_sample = np.zeros((2, 1024, D), np.float32)
    new_k = np.zeros((32, depth, 256, 4, 128), np.float32)
    new_v = np.zeros((32, depth, 256, 4, 128), np.float32)
    for core in range(8):
        s = core // 4
        r = core % 4
        ytok = np.asarray(outs[core]["yT"]).transpose(2, 1, 0).reshape(T, D)
        y_prompt[4 * core:4 * core + 4] = ytok[:1024].reshape(4, 256, D)
        y_sample[s, r * 256:(r + 1) * 256] = ytok[1024:]
        new_k[4 * core:4 * core + 4] = np.asarray(outs[core]["nk"]).reshape(4, depth, 256, 4, 128)
        new_v[4 * core:4 * core + 4] = np.asarray(outs[core]["nv"]).reshape(4, depth, 256, 4, 128)
    return (y_prompt, y_sample, new_k, new_v)


def kernel(**inputs):
    depth, in_maps = _prepare(**inputs)
    if depth not in _NC_CACHE:
        _NC_CACHE[depth] = build_nc(depth)
    nc = _NC_CACHE[depth]
    res = run_bass_kernel_spmd(nc, in_maps, core_ids=list(range(8)))
    return _assemble(res.results, depth)
```
